# Optimizing a Trainium2 kernel written in Bass

```python
import jax
import jax.numpy as jnp
from jax import lax
import numpy as np

D_MODEL = 1024
BATCH = 16
SEQ = 4096
DEPTH = 2
DEC_BATCH = 4
DEC_SEQ = 4096
PAST_LEN = 128

GRID_W = 64
HEAD_DIM = 64
NA_HEADS = 8
NA_WIN_R = 8
NA_WIN_C = 16
DIL_HEADS = 8
DIL_BRANCHES = ((128, 1), (512, 4), (2048, 16))
DIL_QBLOCK = 128
RET_HEADS = 4
RET_QK_DIM = 256
RET_V_DIM = 512
RET_CHUNK = 128
FFN_DIM = 2816
CONV_WIDTH = 3
ROPE_THETA = 10000.0
NORM_EPS = 1e-6
NEG_INF = -1e30
EVEN_MIX = (NA_HEADS + DIL_HEADS) * HEAD_DIM
EVEN_IN = 3 * EVEN_MIX
RET_IN = 2 * RET_HEADS * RET_QK_DIM + 2 * RET_HEADS * RET_V_DIM
RET_MIX = RET_HEADS * RET_V_DIM

kernel_name = 'hybrid_na_dilated_retention_encoder'


def rmsnorm(x, g):
    xf = x.astype(jnp.float32)
    y = xf * lax.rsqrt(jnp.mean(jnp.square(xf), axis=-1, keepdims=True) + NORM_EPS)
    return (y * g.astype(jnp.float32)).astype(x.dtype)


def rotary(x):
    t, dh = x.shape[1], x.shape[-1]
    inv = 1.0 / (ROPE_THETA ** (jnp.arange(0, dh, 2, dtype=jnp.float32) / dh))
    ang = jnp.arange(t, dtype=jnp.float32)[:, None] * inv[None, :]
    cos = jnp.cos(ang)[None, :, None, :]
    sin = jnp.sin(ang)[None, :, None, :]
    xf = x.astype(jnp.float32)
    x1, x2 = xf[..., :dh // 2], xf[..., dh // 2:]
    return jnp.concatenate([x1 * cos - x2 * sin, x1 * sin + x2 * cos], axis=-1).astype(x.dtype)


def neighbourhood_attention(q, k, v, rpb):
    b, t, h, dh = q.shape
    rows = t // GRID_W
    wr = min(NA_WIN_R, rows)
    wc = NA_WIN_C
    qg = q.reshape(b, rows, GRID_W, h, dh)
    kg = k.reshape(b, rows, GRID_W, h, dh)
    vg = v.reshape(b, rows, GRID_W, h, dh)
    col = jnp.arange(GRID_W)
    col_start = jnp.clip(col - wc // 2, 0, GRID_W - wc)
    col_idx = col_start[:, None] + jnp.arange(wc)[None, :]
    col_off = col_idx - col[:, None] + (NA_WIN_C - 1)
    scale = dh ** -0.5
    rpb32 = rpb.astype(jnp.float32)

    def one_row(r):
        row_start = jnp.clip(r - wr // 2, 0, rows - wr)
        qr = lax.dynamic_index_in_dim(qg, r, axis=1, keepdims=False)
        kr = lax.dynamic_slice_in_dim(kg, row_start, wr, axis=1)
        vr = lax.dynamic_slice_in_dim(vg, row_start, wr, axis=1)
        kn = jnp.take(kr, col_idx, axis=2)
        vn = jnp.take(vr, col_idx, axis=2)
        s = jnp.einsum('bqhd,biqjhd->bhqij', qr, kn).astype(jnp.float32) * scale
        row_off = row_start + jnp.arange(wr) - r + (NA_WIN_R - 1)
        bias = rpb32[:, row_off][:, :, col_off].transpose(0, 2, 1, 3)
        s = s + bias[None]
        p = jax.nn.softmax(s.reshape(b, h, GRID_W, wr * wc), axis=-1).reshape(b, h, GRID_W, wr, wc)
        return jnp.einsum('bhqij,biqjhd->bqhd', p.astype(v.dtype), vn)

    out = lax.map(one_row, jnp.arange(rows))
    return out.transpose(1, 0, 2, 3, 4).reshape(b, t, h, dh)


def dilated_branch(q, k, v, dilation, half):
    b, t, h, dh = q.shape
    L = t // dilation
    nblk = -(-L // DIL_QBLOCK)
    Lp = nblk * DIL_QBLOCK
    kw = DIL_QBLOCK + 2 * half

    def to_res(a):
        return a.reshape(b, L, dilation, h, dh).transpose(0, 2, 1, 3, 4).reshape(b * dilation, L, h, dh)

    qr = jnp.pad(to_res(q), ((0, 0), (0, Lp - L), (0, 0), (0, 0)))
    kr = jnp.pad(to_res(k), ((0, 0), (half, Lp - L + half), (0, 0), (0, 0)))
    vr = jnp.pad(to_res(v), ((0, 0), (half, Lp - L + half), (0, 0), (0, 0)))
    qi = jnp.arange(DIL_QBLOCK)
    kj = jnp.arange(kw)
    rel = kj[None, :] - half - qi[:, None]
    scale = dh ** -0.5

    def one_block(blk):
        s0 = blk * DIL_QBLOCK
        qb = lax.dynamic_slice_in_dim(qr, s0, DIL_QBLOCK, axis=1)
        kb = lax.dynamic_slice_in_dim(kr, s0, kw, axis=1)
        vb = lax.dynamic_slice_in_dim(vr, s0, kw, axis=1)
        kpos = s0 - half + kj
        valid = (jnp.abs(rel) <= half) & ((kpos >= 0) & (kpos < L))[None, :]
        s = jnp.einsum('nqhd,nkhd->nhqk', qb, kb).astype(jnp.float32) * scale
        s = jnp.where(valid[None, None], s, NEG_INF)
        m = jnp.max(s, axis=-1, keepdims=True)
        e = jnp.exp(s - m)
        z = jnp.sum(e, axis=-1)
        o = jnp.einsum('nhqk,nkhd->nqhd', (e / z[..., None]).astype(v.dtype), vb)
        lse = (m[..., 0] + jnp.log(z)).transpose(0, 2, 1)
        return o, lse

    o, lse = lax.map(one_block, jnp.arange(nblk))
    o = o.transpose(1, 0, 2, 3, 4).reshape(b * dilation, Lp, h, dh)[:, :L]
    lse = lse.transpose(1, 0, 2, 3).reshape(b * dilation, Lp, h)[:, :L]
    o = o.reshape(b, dilation, L, h, dh).transpose(0, 2, 1, 3, 4).reshape(b, t, h, dh)
    lse = lse.reshape(b, dilation, L, h).transpose(0, 2, 1, 3).reshape(b, t, h)
    return o, lse


def dilated_attention(q, k, v):
    outs, lses = [], []
    for window, dilation in DIL_BRANCHES:
        o, lse = dilated_branch(q, k, v, dilation, (window // 2) // dilation)
        outs.append(o)
        lses.append(lse)
    wts = jax.nn.softmax(jnp.stack(lses, axis=0), axis=0)
    return jnp.einsum('nbth,nbthd->bthd', wts.astype(q.dtype), jnp.stack(outs, axis=0))


def even_mixer(h, w_in, rpb, w_out):
    b, t, _ = h.shape
    proj = h @ w_in
    n_na = 3 * NA_HEADS * HEAD_DIM
    na = proj[..., :n_na].reshape(b, t, 3, NA_HEADS, HEAD_DIM)
    dl = proj[..., n_na:].reshape(b, t, 3, DIL_HEADS, HEAD_DIM)
    oa = neighbourhood_attention(na[:, :, 0], na[:, :, 1], na[:, :, 2], rpb)
    ob = dilated_attention(rotary(dl[:, :, 0]), rotary(dl[:, :, 1]), dl[:, :, 2])
    o = jnp.concatenate([oa.reshape(b, t, NA_HEADS * HEAD_DIM), ob.reshape(b, t, DIL_HEADS * HEAD_DIM)], axis=-1)
    return o @ w_out


def chunk_retention(q, k, v, log_gamma, include_diag):
    b, t, h, dk = q.shape
    dv = v.shape[-1]
    c = RET_CHUNK
    n = t // c

    def chunks(a):
        return a.reshape(b, n, c, h, a.shape[-1]).transpose(1, 0, 3, 2, 4)

    pos = jnp.arange(c, dtype=jnp.float32)
    diff = pos[:, None] - pos[None, :]
    mask = (diff >= 0) if include_diag else (diff > 0)
    lg = log_gamma.astype(jnp.float32)
    decay_in = jnp.where(mask, jnp.exp(lg[:, None, None] * jnp.where(mask, diff, 0.0)), 0.0)[None].astype(q.dtype)
    q_decay = jnp.exp(lg[:, None] * (pos + 1.0))[None, :, :, None].astype(q.dtype)
    k_decay = jnp.exp(lg[:, None] * (c - 1.0 - pos))[None, :, :, None].astype(q.dtype)
    s_decay = jnp.exp(lg * c)[None, :, None, None].astype(q.dtype)

    def step(state, inp):
        qc, kc, vc = inp
        inner = jnp.einsum('bhqd,bhkd->bhqk', qc, kc) * decay_in
        out = jnp.einsum('bhqk,bhkv->bhqv', inner, vc) + jnp.einsum('bhqd,bhdv->bhqv', qc * q_decay, state)
        state = state * s_decay + jnp.einsum('bhkd,bhkv->bhdv', kc * k_decay, vc)
        return state, out

    state0 = jnp.zeros((b, h, dk, dv), q.dtype)
    _, out = lax.scan(step, state0, (chunks(q), chunks(k), chunks(v)))
    return out.transpose(1, 0, 3, 2, 4).reshape(b, t, h, dv)


def retention_mixer(h, w_in, decay_fwd_raw, decay_bwd_raw, w_out):
    b, t, _ = h.shape
    nq = RET_HEADS * RET_QK_DIM
    nv = RET_HEADS * RET_V_DIM
    proj = h @ w_in
    q = proj[..., :nq].reshape(b, t, RET_HEADS, RET_QK_DIM)
    k = proj[..., nq:2 * nq].reshape(b, t, RET_HEADS, RET_QK_DIM)
    v = proj[..., 2 * nq:2 * nq + nv].reshape(b, t, RET_HEADS, RET_V_DIM)
    g = proj[..., 2 * nq + nv:]
    q = rotary(q) * (RET_QK_DIM ** -0.5)
    k = rotary(k)
    log_g_f = -jax.nn.softplus(decay_fwd_raw.astype(jnp.float32))
    log_g_b = -jax.nn.softplus(decay_bwd_raw.astype(jnp.float32))
    fwd = chunk_retention(q, k, v, log_g_f, True)
    bwd = jnp.flip(chunk_retention(jnp.flip(q, 1), jnp.flip(k, 1), jnp.flip(v, 1), log_g_b, False), 1)
    r = (fwd + bwd).astype(jnp.float32)
    mu = jnp.mean(r, axis=-1, keepdims=True)
    var = jnp.mean(jnp.square(r - mu), axis=-1, keepdims=True)
    r = ((r - mu) * lax.rsqrt(var + NORM_EPS)).astype(h.dtype).reshape(b, t, nv)
    return (jax.nn.silu(g) * r) @ w_out


def conv_ffn(h, w_up, conv_w, conv_b, w_down):
    t = h.shape[1]
    a = h @ w_up
    pad = CONV_WIDTH // 2
    ap = jnp.pad(a, ((0, 0), (pad, pad), (0, 0)))
    y = conv_b
    for j in range(CONV_WIDTH):
        y = y + ap[:, j:j + t] * conv_w[j]
    u, gate = y[..., :FFN_DIM], y[..., FFN_DIM:]
    return (u * jax.nn.gelu(gate)) @ w_down


def encoder_trunk(x, attn_norm, even_w_in, na_rpb, even_w_out, ret_w_in, ret_decay_fwd, ret_decay_bwd,
                  ret_w_out, ffn_norm, ffn_w_up, ffn_conv_w, ffn_conv_b, ffn_w_down, final_norm):
    for layer in range(DEPTH):
        h = rmsnorm(x, attn_norm[layer])
        if layer % 2 == 0:
            e = layer // 2
            x = x + even_mixer(h, even_w_in[e], na_rpb[e], even_w_out[e])
        else:
            o = layer // 2
            x = x + retention_mixer(h, ret_w_in[o], ret_decay_fwd[o], ret_decay_bwd[o], ret_w_out[o])
        h = rmsnorm(x, ffn_norm[layer])
        x = x + conv_ffn(h, ffn_w_up[layer], ffn_conv_w[layer], ffn_conv_b[layer], ffn_w_down[layer])
    return rmsnorm(x, final_norm)


def setup_inputs(seed: int = 0) -> dict:
    key = jax.random.key(seed)
    ks = jax.random.split(key, 16)
    f32 = jnp.float32
    n_even = (DEPTH + 1) // 2
    n_odd = DEPTH // 2

    def normal(k, shape, scale):
        return jax.random.normal(k, shape, f32) * scale

    neg_log_gamma = -jnp.log1p(-(2.0 ** (-5.0 - jnp.arange(RET_HEADS, dtype=f32))))
    decay_base = jnp.log(jnp.expm1(neg_log_gamma))
    return {
        'x_prompt': normal(ks[0], (BATCH, SEQ, D_MODEL), 1.0),
        'x_sample': normal(ks[1], (DEC_BATCH, DEC_SEQ, D_MODEL), 1.0),
        'attn_norm': 1.0 + normal(ks[2], (DEPTH, D_MODEL), 0.01),
        'even_w_in': normal(ks[3], (n_even, D_MODEL, EVEN_IN), D_MODEL ** -0.5),
        'na_rpb': normal(ks[4], (n_even, NA_HEADS, 2 * NA_WIN_R - 1, 2 * NA_WIN_C - 1), 0.1),
        'even_w_out': normal(ks[5], (n_even, EVEN_MIX, D_MODEL), EVEN_MIX ** -0.5),
        'ret_w_in': normal(ks[6], (n_odd, D_MODEL, RET_IN), D_MODEL ** -0.5),
        'ret_decay_fwd': decay_base[None, :] + normal(ks[7], (n_odd, RET_HEADS), 0.1),
        'ret_decay_bwd': decay_base[None, :] + normal(ks[8], (n_odd, RET_HEADS), 0.1),
        'ret_w_out': normal(ks[9], (n_odd, RET_MIX, D_MODEL), RET_MIX ** -0.5),
        'ffn_norm': 1.0 + normal(ks[10], (DEPTH, D_MODEL), 0.01),
        'ffn_w_up': normal(ks[11], (DEPTH, D_MODEL, 2 * FFN_DIM), D_MODEL ** -0.5),
        'ffn_conv_w': normal(ks[12], (DEPTH, CONV_WIDTH, 2 * FFN_DIM), CONV_WIDTH ** -0.5),
        'ffn_conv_b': normal(ks[13], (DEPTH, 2 * FFN_DIM), 0.01),
        'ffn_w_down': normal(ks[14], (DEPTH, FFN_DIM, D_MODEL), FFN_DIM ** -0.5),
        'final_norm': 1.0 + normal(ks[15], (D_MODEL,), 0.01),
    }


def reference(x_prompt, x_sample, attn_norm, even_w_in, na_rpb, even_w_out, ret_w_in, ret_decay_fwd,
              ret_decay_bwd, ret_w_out, ffn_norm, ffn_w_up, ffn_conv_w, ffn_conv_b, ffn_w_down, final_norm):
    y_prompt = encoder_trunk(x_prompt, attn_norm, even_w_in, na_rpb, even_w_out, ret_w_in, ret_decay_fwd,
                             ret_decay_bwd, ret_w_out, ffn_norm, ffn_w_up, ffn_conv_w, ffn_conv_b, ffn_w_down,
                             final_norm)
    y_sample = encoder_trunk(x_sample, attn_norm, even_w_in, na_rpb, even_w_out, ret_w_in, ret_decay_fwd,
                             ret_decay_bwd, ret_w_out, ffn_norm, ffn_w_up, ffn_conv_w, ffn_conv_b, ffn_w_down,
                             final_norm)
    return (y_prompt, y_sample)
```

```python
import math
from contextlib import ExitStack

import numpy as np
import ml_dtypes
import concourse.bass as bass
import concourse.mybir as mybir
from concourse.bass_utils import run_bass_kernel_spmd

F32 = mybir.dt.float32
BF16 = mybir.dt.bfloat16
AF = mybir.ActivationFunctionType
ALU = mybir.AluOpType

T = 4096
D = 1024
NT = 32
NB = 8
FF = 2816
NCORES = 8
EPS = 1e-6
BATCH, DEC_BATCH = 16, 4


class Buf:
    __slots__ = ("w", "r")

    def __init__(self):
        self.w = None
        self.r = {}


def bufs(n):
    return [Buf() for _ in range(n)]


class Sched:
    LIMIT = 30000

    def __init__(self, nc, n_dma_sems=40):
        self.nc = nc
        self.eng = dict(pe=nc.tensor, act=nc.scalar, dve=nc.vector, pool=nc.gpsimd, sp=nc.sync)
        self.csem, self.ccnt, self.nsem = {}, {}, 0
        for e in self.eng:
            self._newsem(e)
        self.known = {e: {} for e in self.eng}
        self.dsems, self.dcnt, self.dnext = {}, {}, {}
        for e, n in (("sp", n_dma_sems), ("pool", 12), ("act", 8)):
            self.dsems[e] = [nc.alloc_semaphore(f"dq_{e}{i}") for i in range(n)]
            self.dcnt[e] = [0] * n
            self.dnext[e] = 0
        self.ninstr = 0
        self.out_tks = []
        import os
        self.cap = int(os.environ.get('KCAP', '1000000000'))
        self.nreal = 0

    def _newsem(self, e):
        self.nsem += 1
        self.csem[e] = self.nc.alloc_semaphore(f"c_{e}_{self.nsem}")
        self.ccnt[e] = 0

    def _wait(self, e, tk):
        if tk is None:
            return
        sem, val = tk
        k = self.known[e]
        if k.get(sem.num, 0) >= val:
            return
        if e == "pe" and sem is self.csem["pe"]:
            return
        self.eng[e].wait_ge(sem, val)
        self.ninstr += 1
        k[sem.num] = val

    def _deps(self, e, reads, writes):
        for b in reads:
            self._wait(e, b.w)
        for b in writes:
            self._wait(e, b.w)
            for tk in list(b.r.values()):
                self._wait(e, tk)

    def _mark(self, tk, reads, writes):
        sem, val = tk
        for b in reads:
            b.r[sem.num] = tk
        for b in writes:
            b.w = tk
            b.r = {}

    def op(self, e, fn, reads=(), writes=()):
        self.nreal += 1
        if self.nreal > self.cap:
            return None
        self._deps(e, reads, writes)
        ins = fn(self.eng[e])
        if self.ccnt[e] >= self.LIMIT:
            self._newsem(e)
        self.ccnt[e] += 1
        sem = self.csem[e]
        ins.then_inc(sem, 1)
        self.ninstr += 1
        tk = (sem, self.ccnt[e])
        self._mark(tk, reads, writes)
        return tk

    def dma(self, e, out, in_, reads=(), writes=(), final=False, **kw):
        self.nreal += 1
        if self.nreal > self.cap:
            return None
        self._deps(e, reads, writes)
        i = self.dnext[e]
        self.dnext[e] = (i + 1) % len(self.dsems[e])
        sem = self.dsems[e][i]
        if self.dcnt[e][i] > 0:
            self._wait(e, (sem, self.dcnt[e][i]))
        self.dcnt[e][i] += 16
        self.eng[e].dma_start(out=out, in_=in_, **kw).then_inc(sem, 16)
        self.ninstr += 1
        tk = (sem, self.dcnt[e][i])
        self._mark(tk, reads, writes)
        return tk

    def barrier(self):
        for e in self.eng:
            for e2 in self.eng:
                if e2 != e and self.ccnt[e2] > 0:
                    self._wait(e, (self.csem[e2], self.ccnt[e2]))
            for q in self.dsems:
                for i, sem in enumerate(self.dsems[q]):
                    if self.dcnt[q][i] > 0:
                        self._wait(e, (sem, self.dcnt[q][i]))


class Scope:
    cnt = [0]

    def __init__(self, nc):
        self.nc = nc
        self.es = ExitStack()

    def sb(self, name, shape, dt):
        Scope.cnt[0] += 1
        return self.es.enter_context(self.nc.sbuf_tensor(f"{name}_{Scope.cnt[0]}", list(shape), dt))

    def psum(self, name, shape, dt):
        Scope.cnt[0] += 1
        return self.es.enter_context(self.nc.psum_tensor(f"{name}_{Scope.cnt[0]}", list(shape), dt))

    def close(self):
        self.es.close()


def alloc_psum(C, sc, nps, npt):
    C.pt = [sc.psum("pt", [128, 8, 128], BF16) for _ in range(npt)]
    C.bpt = bufs(npt)
    C.npt = npt
    C.pti = 0
    C.ps = [sc.psum("ps", [128, 512], F32) for _ in range(nps)]
    C.bps = bufs(nps)


def _rope_tables(dh, reps):
    half = dh // 2
    inv = (1.0 / (np.float32(10000.0) ** (np.arange(0, dh, 2, dtype=np.float32) / np.float32(dh)))).astype(np.float32)
    ang = np.arange(T, dtype=np.float32)[None, :] * inv[:, None]
    c = np.cos(ang).astype(np.float32)
    s = np.sin(ang).astype(np.float32)
    cos = np.concatenate([c, c], 0)
    sin = np.concatenate([-s, s], 0)
    return np.tile(cos, (reps, 1)).copy(), np.tile(sin, (reps, 1)).copy()


def _dil_masks():
    m = np.zeros((20, 128, 512), np.float32)
    k = np.arange(128)[:, None]
    q = np.arange(512)[None, :]
    for i in range(20):
        d = (i * 128 - 1024) + k - q
        ad = np.abs(d)
        m[i] = (ad <= 64).astype(np.float32) + ((d % 4 == 0) & (ad <= 256)) + ((d % 16 == 0) & (ad <= 1024))
    return m.astype(ml_dtypes.bfloat16)


def _na_onehot():
    L = np.zeros((31, 64, 128), np.float32)
    for cq in range(64):
        cs = min(max(cq - 8, 0), 48)
        for ck in range(cs, cs + 16):
            b = ck - cq + 15
            L[b, cq, ck] = 1.0
            L[b, cq, ck + 64] = 1.0
    return L.reshape(31, 64 * 128)


def _ret_consts():
    k = np.arange(128, dtype=np.float32)[:, None]
    q = np.arange(128, dtype=np.float32)[None, :]
    c = np.zeros((8, 128, 128), np.float32)
    c[0] = np.maximum(q - k, 0)
    c[1] = (q >= k) / 16.0
    c[2] = np.maximum(k - q, 0)
    c[3] = (k > q) / 16.0
    c[4] = np.broadcast_to(q + 1.0, (128, 128))
    c[5] = np.broadcast_to(128.0 - q, (128, 128))
    c[6, :, 0] = 127.0 - k[:, 0]
    c[6, :, 1] = k[:, 0]
    c[6, :, 2] = 128.0
    return c


_CONSTS = {}


def _consts():
    if not _CONSTS:
        cos0, sin0 = _rope_tables(64, 2)
        cos1, sin1 = _rope_tables(256, 1)
        _CONSTS.update(
            ident=np.eye(128, dtype=np.float32).astype(ml_dtypes.bfloat16),
            cos0=cos0, sin0=sin0,
            cos1=np.ascontiguousarray(cos1[:128]), sin1=np.ascontiguousarray(sin1[128:]),
            dmask=_dil_masks(), naL=_na_onehot(), retc=_ret_consts(),
        )
    return _CONSTS


class Ctx:
    pass


def build(nseq, upto=99, debug=False):
    nc = bass.Bass("TRN2", target_bir_lowering=False)
    S = Sched(nc)
    C = Ctx()
    C.nc, C.S, C.nseq = nc, S, nseq
    NTOK = nseq * T

    def din(name, shape, dt=F32):
        return nc.dram_tensor(name, list(shape), dt, kind="ExternalInput").ap()

    def dscr(name, shape, dt, out=False):
        return nc.dram_tensor(name, list(shape), dt, kind="ExternalOutput" if (out or (debug and name in debug)) else "Internal").ap()

    C.x = din("x", [NTOK, D])
    C.w_in0 = din("w_in0", [D, 4096])
    C.w_out0 = din("w_out0", [D, D])
    C.w_up = [din(f"w_up{l}", [D, 2 * FF]) for l in range(2)]
    C.w_dn = [din(f"w_dn{l}", [FF, D]) for l in range(2)]
    C.cwT = [din(f"cwT{l}", [128, 44 * 4]) for l in range(2)]
    C.w_in1 = din("w_in1", [D, 6144])
    C.w_out1 = din("w_out1", [2048, D])
    C.norms = din("norms", [5, D])
    C.rpbT = din("rpbT", [31, 8 * 15])
    C.decay = din("decay", [8])
    C.ident = din("ident", [128, 128], BF16)
    C.cos0 = din("cos0", [128, T])
    C.sin0 = din("sin0", [128, T])
    C.cos1 = din("cos1", [128, T])
    C.sin1 = din("sin1", [128, T])
    C.dmask = din("dmask", [20, 128, 512], BF16)
    C.naL = din("naL", [31, 64 * 128])
    C.retc = din("retc", [8, 128, 128])

    C.y = dscr("y", [NTOK, D], F32, out=True)
    C.QK0 = dscr("QK0", [nseq, 16, 128, T], BF16)
    C.V0 = dscr("V0", [NTOK, D], BF16)
    C.OT0 = dscr("OT0", [nseq, 8, 128, T], BF16)
    C.X1 = dscr("X1", [NTOK, D], F32)
    C.X2 = dscr("X2", [NTOK, D], F32)
    C.X3 = dscr("X3", [NTOK, D], F32)
    C.HTa = dscr("HTa", [nseq, 8, 128, T], BF16)
    C.HTb = dscr("HTb", [nseq, 8, 128, T], BF16)
    C.QTR = dscr("QTR", [nseq, 8, 128, T], BF16)
    C.KTR = dscr("KTR", [nseq, 8, 128, T], BF16)
    C.KTM = dscr("KTM", [NTOK, D], BF16)
    C.VR = dscr("VR", [NTOK, 2048], BF16)
    C.SG = dscr("SG", [NTOK, 2048], BF16)
    C.RF = dscr("RF", [NTOK, 2048], F32)
    C.TAZ = dscr("TAZ", [8, 2, 128, 32 * 64], F32)

    C.idt = nc.alloc_sbuf_tensor("idt", [128, 128], BF16)
    C.bidt = Buf()
    S.dma("sp", C.idt[:], C.ident[:, :], writes=[C.bidt])

    phases = [p1_inproj0, p2_attn, p3_outproj0,
              lambda c: p4_ffn(c, 0), p5_inproj1, p6_retention, lambda c: p4_ffn(c, 1)]
    for i, ph in enumerate(phases):
        if i >= upto:
            break
        ph(C)
        S.barrier()
    S.barrier()
    return nc, S


def load_w(C, sc, name, w_dram, kc, f, eng="pool"):
    t = sc.sb(name, [128, kc, f], BF16)
    b = Buf()
    src = w_dram.rearrange("(k p) f -> p k f", p=128)
    step = max(1, 2048 // f) if f < 2048 else 1
    for k in range(0, kc, step):
        k2 = min(kc, k + step)
        C.S.dma(eng, t[:, k:k2, :], src[:, k:k2, :], writes=[b])
    return t, b


def load_grow(C, sc, idx):
    g = sc.sb("grow", [128, D], F32)
    b = Buf()
    C.S.dma("sp", g[:], C.norms[idx, :].partition_broadcast(128), writes=[b])
    return g, b


class NormBufs:
    def __init__(self, sc, n=2):
        self.n = n
        self.junk = [sc.sb("nj", [128, D], BF16) for _ in range(n)]
        self.ss = [sc.sb("nss", [128, 1], F32) for _ in range(n)]
        self.b = [bufs(2) for _ in range(n)]
        self.i = 0


def rmsnorm(C, nb, xt, bx, grow, bg, out, bout):
    S = C.S
    i = nb.i % nb.n
    nb.i += 1
    junk, ss, (bj, bs) = nb.junk[i], nb.ss[i], nb.b[i]
    S.op("act", lambda e: e.activation(out=junk[:], in_=xt, func=AF.Square, accum_out=ss[:]), reads=[bx], writes=[bj, bs])
    S.op("dve", lambda e: e.tensor_scalar(out=ss[:], in0=ss[:], scalar1=1.0 / D, scalar2=EPS, op0=ALU.mult, op1=ALU.add), reads=[bs], writes=[bs])
    S.op("act", lambda e: e.activation(out=ss[:], in_=ss[:], func=AF.Sqrt), reads=[bs], writes=[bs])
    S.op("dve", lambda e: e.reciprocal(out=ss[:], in_=ss[:]), reads=[bs], writes=[bs])
    S.op("dve", lambda e: e.scalar_tensor_tensor(out=out, in0=xt, scalar=ss[:, 0:1], in1=grow[:], op0=ALU.mult, op1=ALU.mult),
         reads=[bx, bs, bg], writes=[bout])


def transpose_to(C, src, bsrc, nk, dst_fn, bdst, evac_eng):
    S = C.S
    for g in range(0, nk, 8):
        h = C.pti % C.npt
        C.pti += 1
        n = min(8, nk - g)
        for k in range(g, g + n):
            S.op("pe", lambda e: e.transpose(out=C.pt[h][:, k - g, :], in_=src[:, k * 128:(k + 1) * 128], identity=C.idt[:]),
                 reads=[bsrc, C.bidt], writes=[C.bpt[h]])
        if evac_eng == "act":
            S.op("act", lambda e: e.copy(out=dst_fn(g, g + n), in_=C.pt[h][:, 0:n, :]), reads=[C.bpt[h]], writes=[bdst])
        else:
            S.op(evac_eng, lambda e: e.tensor_copy(out=dst_fn(g, g + n), in_=C.pt[h][:, 0:n, :]), reads=[C.bpt[h]], writes=[bdst])


def p1_inproj0(C):
    nc, S = C.nc, C.S
    sc = Scope(nc)
    alloc_psum(C, sc, 6, 2)
    W, bW = load_w(C, sc, "w0", C.w_in0, 8, 4096)
    grow, bg = load_grow(C, sc, 0)
    nb = NormBufs(sc)
    xt = [sc.sb("xt", [128, D], F32) for _ in range(2)]
    bxt = bufs(2)
    hb = [sc.sb("hb", [128, D], BF16) for _ in range(2)]
    bhb = bufs(2)
    hT = [sc.sb("hT", [128, 8, 512], BF16) for _ in range(2)]
    bhT = bufs(2)
    cs = [sc.sb("cs", [128, 2, 512], F32) for _ in range(2)]
    bcs = bufs(2)
    t1 = [sc.sb("t1", [128, 512], F32) for _ in range(2)]
    t2 = [sc.sb("t2", [128, 512], F32) for _ in range(2)]
    bt1, bt2 = bufs(2), bufs(2)
    ob = [sc.sb("ob", [128, 512], BF16) for _ in range(4)]
    bob = bufs(4)
    oi = 0
    pi = 0
    for s in range(C.nseq):
        for tb in range(NB):
            hTc, bhTc = hT[tb % 2], bhT[tb % 2]
            csc, bcsc = cs[tb % 2], bcs[tb % 2]
            S.dma("sp", csc[:, 0, :], C.cos0[:, tb * 512:(tb + 1) * 512], writes=[bcsc])
            S.dma("sp", csc[:, 1, :], C.sin0[:, tb * 512:(tb + 1) * 512], writes=[bcsc])
            for j in range(4):
                tt = tb * 4 + j
                r0 = s * T + tt * 128
                S.dma("sp", xt[tt % 2][:], C.x[r0:r0 + 128, :], writes=[bxt[tt % 2]])
                rmsnorm(C, nb, xt[tt % 2][:], bxt[tt % 2], grow, bg, hb[tt % 2][:], bhb[tt % 2])
                transpose_to(C, hb[tt % 2], bhb[tt % 2], 8, lambda a, b: hTc[:, a:b, j * 128:(j + 1) * 128], bhTc, "act")
            for ft in list(range(8)) + list(range(12, 20)):
                p = pi % 4
                pi += 1
                for k in range(8):
                    S.op("pe", lambda e: e.matmul(C.ps[p][:], lhsT=W[:, k, ft * 128:(ft + 1) * 128], rhs=hTc[:, k, :], start=(k == 0), stop=(k == 7)),
                         reads=[bW, bhTc], writes=[C.bps[p]])
                o, bo = ob[oi % 4], bob[oi % 4]
                oi += 1
                if ft < 8:
                    S.op("act", lambda e: e.copy(out=o[:], in_=C.ps[p][:]), reads=[C.bps[p]], writes=[bo])
                    dst = ft
                else:
                    p2 = pi % 4
                    pi += 1
                    fs = ft + 12
                    for k in range(8):
                        S.op("pe", lambda e: e.matmul(C.ps[p2][:], lhsT=W[:, k, fs * 128:(fs + 1) * 128], rhs=hTc[:, k, :], start=(k == 0), stop=(k == 7)),
                             reads=[bW, bhTc], writes=[C.bps[p2]])
                    a, ba = t1[oi % 2], bt1[oi % 2]
                    b, bb = t2[oi % 2], bt2[oi % 2]
                    S.op("dve", lambda e: e.tensor_tensor(out=a[:], in0=C.ps[p][:], in1=csc[:, 0, :], op=ALU.mult), reads=[C.bps[p], bcsc], writes=[ba])
                    S.op("dve", lambda e: e.tensor_tensor(out=b[:], in0=C.ps[p2][:], in1=csc[:, 1, :], op=ALU.mult), reads=[C.bps[p2], bcsc], writes=[bb])
                    S.op("pool", lambda e: e.tensor_tensor(out=o[:], in0=a[:], in1=b[:], op=ALU.add), reads=[ba, bb], writes=[bo])
                    dst = ft - 4
                S.dma("sp", C.QK0[s, dst, :, tb * 512:(tb + 1) * 512], o[:], reads=[bo])
            for j in range(4):
                r0 = s * T + (tb * 4 + j) * 128
                for ci, c0 in enumerate((1024, 2560)):
                    p = pi % 4
                    pi += 1
                    for k in range(8):
                        S.op("pe", lambda e: e.matmul(C.ps[p][:], lhsT=hTc[:, k, j * 128:(j + 1) * 128], rhs=W[:, k, c0:c0 + 512], start=(k == 0), stop=(k == 7)),
                             reads=[bW, bhTc], writes=[C.bps[p]])
                    o, bo = ob[oi % 4], bob[oi % 4]
                    oi += 1
                    if ci == 0:
                        S.op("act", lambda e: e.copy(out=o[:], in_=C.ps[p][:]), reads=[C.bps[p]], writes=[bo])
                    else:
                        S.op("dve", lambda e: e.tensor_copy(out=o[:], in_=C.ps[p][:]), reads=[C.bps[p]], writes=[bo])
                    S.dma("sp", C.V0[r0:r0 + 128, ci * 512:(ci + 1) * 512], o[:], reads=[bo])
    S.barrier()
    sc.close()


def _na_valid(rq, rk):
    rs = min(max(rq - 4, 0), 56)
    return rs <= rk < rs + 8


def p2_attn(C):
    nc, S = C.nc, C.S
    sc = Scope(nc)
    alloc_psum(C, sc, 8, 0)
    sc2 = Scope(nc)
    L = sc2.sb("naL", [31, 64 * 128], F32)
    PT = sc2.sb("naPT", [31, 8 * 15], F32)
    Z = [sc2.sb("naZ", [128, 2, 32 * 64], F32) for _ in range(2)]
    bL, bPT, bZ = Buf(), Buf(), bufs(2)
    for c in range(0, 64 * 128, 2048):
        S.dma("sp", L[:, c:c + 2048], C.naL[:, c:c + 2048], writes=[bL])
    S.dma("sp", PT[:], C.rpbT[:, :], writes=[bPT])
    S.op("act", lambda e: e.activation(out=PT[:], in_=PT[:], func=AF.Exp), reads=[bPT], writes=[bPT])
    for z in range(2):
        S.op("pool", lambda e: e.memset(Z[z][:], 0.0), writes=[bZ[z]])
    for h in range(8):
        z = h % 2
        for half in range(2):
            p = (h * 2 + half) % 4
            pv = C.ps[p][:].rearrange("p (s c) -> p s c", c=64)
            ns = 8 if half == 0 else 7
            for cq in range(64):
                S.op("pe", lambda e: e.matmul(pv[:, 0:ns, cq], lhsT=L[:, cq * 128:(cq + 1) * 128], rhs=PT[:, h * 15 + half * 8:h * 15 + half * 8 + ns],
                                              start=True, stop=True), reads=[bL, bPT], writes=[C.bps[p]])
            a0 = (8 + half * 8) * 64
            S.op("dve", lambda e: e.tensor_copy(out=Z[z][0:64, 0, a0:a0 + ns * 64], in_=C.ps[p][0:64, 0:ns * 64]), reads=[C.bps[p]], writes=[bZ[z]])
            S.op("dve", lambda e: e.tensor_copy(out=Z[z][64:128, 0, a0 + 64:a0 + 64 + ns * 64], in_=C.ps[p][64:128, 0:ns * 64]), reads=[C.bps[p]], writes=[bZ[z]])
            i0, i1 = (4, 8) if half == 0 else (0, 4)
            S.op("dve", lambda e: e.tensor_copy(out=Z[z][0:64, 1, a0 + i0 * 64:a0 + i1 * 64], in_=C.ps[p][0:64, i0 * 64:i1 * 64]), reads=[C.bps[p]], writes=[bZ[z]])
            S.op("dve", lambda e: e.tensor_copy(out=Z[z][64:128, 1, a0 + 64 + i0 * 64:a0 + 64 + i1 * 64], in_=C.ps[p][64:128, i0 * 64:i1 * 64]), reads=[C.bps[p]], writes=[bZ[z]])
        S.dma("sp", C.TAZ[h, :, :, :].rearrange("v p c -> p v c"), Z[z][:], reads=[bZ[z]])
    S.barrier()
    sc2.close()
    TAz = sc.sb("TAz", [128, 2, 2, 32 * 64], F32)
    bTA = Buf()

    DM = sc.sb("dmask", [128, 20, 512], BF16)
    bDM = Buf()
    for i in range(0, 20, 4):
        S.dma("sp", DM[:, i:i + 4, :], C.dmask[i:i + 4].rearrange("i p q -> p i q"), writes=[bDM])
    LA = 8
    NST = 4
    Qz = [[sc.sb("Qz", [128, T], BF16) for _ in range(2)] for _ in range(2)]
    KT = [sc.sb("KT", [128, T], BF16) for _ in range(2)]
    VA = [sc.sb("VA", [128, NT, 256], BF16) for _ in range(2)]
    bQ, bK, bV = bufs(2), bufs(2), bufs(2)
    for b_ in range(2):
        S.op("pool", lambda e: e.memset(Qz[b_][0][64:128, :], 0.0), writes=[bQ[b_]])
        S.op("pool", lambda e: e.memset(Qz[b_][1][0:64, :], 0.0), writes=[bQ[b_]])
        S.op("pool", lambda e: e.memset(VA[b_][:, :, 64:192], 1.0), writes=[bV[b_]])
    NE, NEM = 8, LA + 4
    E = [sc.sb("E", [128, 512], F32) for _ in range(NE)]
    bE = bufs(NE)
    Eb = [sc.sb("Eb", [128, 512], BF16) for _ in range(NE)]
    bEb = bufs(NE)
    Em = [sc.sb("Em", [128, 512], BF16) for _ in range(NEM)]
    bEm = bufs(NEM)
    rc = [sc.sb("rc", [128, 512], F32) for _ in range(2)]
    rs = [sc.sb("rs", [128, 512], F32) for _ in range(2)]
    brc, brs = bufs(2), bufs(2)
    oT = [sc.sb("oT", [128, 512], BF16) for _ in range(2)]
    boT = bufs(2)

    stA, stB = [], []
    pending = []

    def mk_load(s, g, gi):
        def f():
            b_ = gi % 2
            qt, kt = (g, 4 + g) if g < 4 else (8 + (g - 4), 12 + (g - 4))
            for c in range(0, T, 2048):
                S.dma("sp", Qz[b_][0][0:64, c:c + 2048], C.QK0[s, qt, 0:64, c:c + 2048], writes=[bQ[b_]])
                S.dma("sp", Qz[b_][1][64:128, c:c + 2048], C.QK0[s, qt, 64:128, c:c + 2048], writes=[bQ[b_]])
                S.dma("sp", KT[b_][:, c:c + 2048], C.QK0[s, kt, :, c:c + 2048], writes=[bK[b_]])
            if g < 4:
                for hh in range(2):
                    S.dma("sp", TAz[:, hh, :, :], C.TAZ[g * 2 + hh, :, :, :].rearrange("v p c -> p v c"), writes=[bTA])
            vsrc = C.V0[s * T:(s + 1) * T, g * 128:(g + 1) * 128].rearrange("(t p) c -> p t c", p=128)
            for c in range(0, NT, 8):
                S.dma("sp", VA[b_][:, c:c + 8, 0:64], vsrc[:, c:c + 8, 0:64], writes=[bV[b_]])
                S.dma("sp", VA[b_][:, c:c + 8, 192:256], vsrc[:, c:c + 8, 64:128], writes=[bV[b_]])
        return f

    def mk_tile(s, g, gi, qb, qi, hp, ki, kb, nk, ti, fin):
        isna = g < 4
        b_ = gi % 2
        Qg, Kg, Vg = Qz[b_][hp], KT[b_], VA[b_]
        bQg, bKg, bVg = bQ[b_], bK[b_], bV[b_]
        p = ti % NST
        e_, be_ = (E[ti % NE], bE[ti % NE]) if isna else (Eb[ti % NE], bEb[ti % NE])
        em, bem = Em[ti % NEM], bEm[ti % NEM]
        pA = NST + (qi % 2) * 2
        pB = NST + 1 + (qi % 2) * 2
        R = qb * 8

        def a():
            S.op("pe", lambda e: e.matmul(C.ps[p][:], lhsT=Kg[:, kb * 128:(kb + 1) * 128], rhs=Qg[:, qb * 512:(qb + 1) * 512], start=True, stop=True),
                 reads=[bKg, bQg], writes=[C.bps[p]])
            S.op("act", lambda e: e.activation(out=e_[:], in_=C.ps[p][:], func=AF.Exp, scale=0.125), reads=[C.bps[p]], writes=[be_])
            if isna:
                rk0 = kb * 2
                s0 = 7 - rk0 + R
                fast = all((_na_valid(R + f, rk0 + ph) == (4 <= s0 - ph + f <= 11)) for f in range(8) for ph in range(2))
                if fast:
                    S.op("dve", lambda e: e.tensor_tensor(out=em[:], in0=e_[:], in1=TAz[:, hp, 1, (s0 + 8) * 64:(s0 + 16) * 64], op=ALU.mult), reads=[be_, bTA], writes=[bem])
                else:
                    for ph in range(2):
                        rk = rk0 + ph
                        fs = [f for f in range(8) if _na_valid(R + f, rk)]
                        pp = slice(ph * 64, (ph + 1) * 64)
                        if fs:
                            f1, f2 = fs[0], fs[-1]
                            assert fs == list(range(f1, f2 + 1))
                            c1 = (s0 + 8 + f1) * 64
                            n = f2 - f1 + 1
                            S.op("dve", lambda e: e.tensor_tensor(out=em[pp, f1 * 64:(f2 + 1) * 64], in0=e_[pp, f1 * 64:(f2 + 1) * 64],
                                                                in1=TAz[pp, hp, 0, c1:c1 + n * 64], op=ALU.mult), reads=[be_, bTA], writes=[bem])
                            if f1 > 0:
                                S.op("pool", lambda e: e.memset(em[pp, 0:f1 * 64], 0.0), writes=[bem])
                            if f2 < 7:
                                S.op("pool", lambda e: e.memset(em[pp, (f2 + 1) * 64:512], 0.0), writes=[bem])
                        else:
                            S.op("pool", lambda e: e.memset(em[pp, :], 0.0), writes=[bem])
            else:
                mi = (kb * 128 - qb * 512 + 1024) // 128
                eng = "dve" if (ti % 4) != 3 else "pool"
                S.op(eng, lambda e: e.tensor_tensor(out=em[:], in0=e_[:], in1=DM[:, mi, :], op=ALU.mult), reads=[be_, bDM], writes=[bem])

        def b():
            first, last = ki == 0, ki == nk - 1
            pX = pA if hp == 0 else pB
            S.op("pe", lambda e: e.matmul(C.ps[pX][:], lhsT=Vg[:, kb, hp * 128:(hp + 1) * 128], rhs=em[:], start=first, stop=last),
                 reads=[bVg, bem], writes=[C.bps[pX]])
            if fin:
                j = qi % 2

                def f1():
                    S.op("act", lambda e: e.copy(out=rc[j][64:128, :], in_=C.ps[pA][64:128, :]), reads=[C.bps[pA]], writes=[brc[j]])
                    S.op("act", lambda e: e.copy(out=rc[j][0:64, :], in_=C.ps[pB][0:64, :]), reads=[C.bps[pB]], writes=[brc[j]])
                    S.dma("sp", rs[j][0:64, :], rc[j][64:128, :], reads=[brc[j]], writes=[brs[j]])
                    S.dma("sp", rs[j][64:128, :], rc[j][0:64, :], reads=[brc[j]], writes=[brs[j]])

                def f2():
                    S.op("dve", lambda e: e.reciprocal(out=rs[j][:], in_=rs[j][:]), reads=[brs[j]], writes=[brs[j]])
                    S.op("dve", lambda e: e.tensor_tensor(out=oT[j][0:64, :], in0=C.ps[pA][0:64, :], in1=rs[j][0:64, :], op=ALU.mult), reads=[C.bps[pA], brs[j]], writes=[boT[j]])
                    S.op("dve", lambda e: e.tensor_tensor(out=oT[j][64:128, :], in0=C.ps[pB][64:128, :], in1=rs[j][64:128, :], op=ALU.mult), reads=[C.bps[pB], brs[j]], writes=[boT[j]])
                    S.dma("sp", C.OT0[s, g, :, qb * 512:(qb + 1) * 512], oT[j][:], reads=[boT[j]])
                pending.append([2, f1])
                pending.append([8, f2])
        return a, b

    ti = 0
    qi = 0
    gi = 0
    for s in range(C.nseq):
        for g in range(8):
            ld = mk_load(s, g, gi)
            firstpair = True
            for qb in range(NB):
                if g < 4:
                    R = qb * 8
                    lo = min(max(R - 4, 0), 56)
                    hi = min(max(R + 7 - 4, 0), 56) + 8
                    kbs = list(range(lo // 2, (hi + 1) // 2))
                else:
                    kbs = list(range(max(0, qb * 4 - 8), min(NT, qb * 4 + 4 + 8)))
                nk = len(kbs)
                for ki, kb in enumerate(kbs):
                    for hp in range(2):
                        a, b = mk_tile(s, g, gi, qb, qi, hp, ki, kb, nk, ti, fin=(ki == nk - 1 and hp == 1))
                        if firstpair:
                            a0 = a
                            a = (lambda ld=ld, a0=a0: (ld(), a0()))
                            firstpair = False
                        stA.append(a)
                        stB.append(b)
                        ti += 1
                qi += 1
            gi += 1
    n = len(stA)
    for i in range(n + LA + 10):
        if i < n:
            stA[i]()
        if LA <= i < n + LA:
            stB[i - LA]()
        for pe_ in list(pending):
            pe_[0] -= 1
            if pe_[0] <= 0:
                pe_[1]()
                pending.remove(pe_)
    assert not pending
    S.barrier()
    sc.close()


def p3_outproj0(C):
    nc, S = C.nc, C.S
    sc = Scope(nc)
    alloc_psum(C, sc, 6, 2)
    W, bW = load_w(C, sc, "wo0", C.w_out0, 8, D)
    grow, bg = load_grow(C, sc, 1)
    nb = NormBufs(sc)
    oT = [sc.sb("oTb", [128, 8, 512], BF16) for _ in range(2)]
    boT = bufs(2)
    xt = [sc.sb("xt", [128, D], F32) for _ in range(2)]
    bxt = bufs(2)
    x1 = [sc.sb("x1", [128, D], F32) for _ in range(2)]
    bx1 = bufs(2)
    hb = [sc.sb("hb", [128, D], BF16) for _ in range(2)]
    bhb = bufs(2)
    hT = [sc.sb("hT", [128, 8, 128], BF16) for _ in range(2)]
    bhT = bufs(2)
    pi = 0
    for s in range(C.nseq):
        for tb in range(NB):
            o, bo = oT[tb % 2], boT[tb % 2]
            for k in range(0, 8, 2):
                S.dma("sp", o[:, k:k + 2, :], C.OT0[s, k:k + 2, :, tb * 512:(tb + 1) * 512].rearrange("k p t -> p k t"), writes=[bo])
            for j in range(4):
                tt = tb * 4 + j
                r0 = s * T + tt * 128
                i2 = tt % 2
                S.dma("sp", xt[i2][:], C.x[r0:r0 + 128, :], writes=[bxt[i2]])
                for c in range(2):
                    p = pi % 3
                    pi += 1
                    for k in range(8):
                        S.op("pe", lambda e: e.matmul(C.ps[p][:], lhsT=o[:, k, j * 128:(j + 1) * 128], rhs=W[:, k, c * 512:(c + 1) * 512], start=(k == 0), stop=(k == 7)),
                             reads=[bW, bo], writes=[C.bps[p]])
                    S.op("dve", lambda e: e.tensor_tensor(out=x1[i2][:, c * 512:(c + 1) * 512], in0=C.ps[p][:], in1=xt[i2][:, c * 512:(c + 1) * 512], op=ALU.add),
                         reads=[C.bps[p], bxt[i2]], writes=[bx1[i2]])
                S.dma("sp", C.X1[r0:r0 + 128, :], x1[i2][:], reads=[bx1[i2]])
                rmsnorm(C, nb, x1[i2][:], bx1[i2], grow, bg, hb[i2][:], bhb[i2])
                transpose_to(C, hb[i2], bhb[i2], 8, lambda a, b: hT[i2][:, a:b, :], bhT[i2], "act")
                S.dma("sp", C.HTa[s, :, :, tt * 128:(tt + 1) * 128].rearrange("k p t -> p k t"), hT[i2][:], reads=[bhT[i2]])
    S.barrier()
    sc.close()


def p4_ffn(C, layer):
    nc, S = C.nc, C.S
    sc = Scope(nc)
    alloc_psum(C, sc, 7, 1)
    HTin = C.HTa if layer == 0 else C.HTb
    Xin = C.X1 if layer == 0 else C.X3
    Wu, bWu = load_w(C, sc, "wu", C.w_up[layer], 8, 2 * FF)
    Wd, bWd = load_w(C, sc, "wd", C.w_dn[layer], 22, D)
    cw = sc.sb("cw", [128, 44, 4], F32)
    bcw = Buf()
    S.dma("sp", cw[:], C.cwT[layer].rearrange("p (t c) -> p t c", c=4), writes=[bcw])
    grow, bg = load_grow(C, sc, 2 if layer == 0 else 4)
    nb = NormBufs(sc, 1)
    hT = sc.sb("h2T", [128, 8, 514], BF16)
    bh = Buf()
    gT = sc.sb("gT", [128, 22, 512], BF16)
    bgT = bufs(22)
    ah = sc.sb("ahalo", [128, 2, 44], F32)
    bah = Buf()
    yus = [sc.sb("yu", [128, 512], F32) for _ in range(2)]
    ygs = [sc.sb("yg", [128, 512], F32) for _ in range(2)]
    ggs = [sc.sb("gg", [128, 512], F32) for _ in range(2)]
    byus, bygs, bggs = bufs(2), bufs(2), bufs(2)
    xt = sc.sb("xt", [128, D], F32)
    bxt = Buf()
    x2 = sc.sb("x2", [128, D], F32)
    bx2 = Buf()
    if layer == 0:
        hbs = [sc.sb("hb", [128, D], BF16) for _ in range(2)]
        h3T = sc.sb("h3T", [128, 8, 128], BF16)
        bhbs, bh3 = bufs(2), Buf()
    else:
        yo = sc.sb("yo", [128, D], F32)
        byo = Buf()
    S.op("pool", lambda e: e.memset(hT[:], 0.0), writes=[bh])
    pi = 0
    deferred = []
    prev_s2 = [None]
    for s in range(C.nseq):
        for tb in range(NB):
            t0 = tb * 512
            lo = max(t0 - 1, 0)
            hi = min(t0 + 513, T)
            c0 = lo - (t0 - 1)
            if tb == 0:
                S.op("pool", lambda e: e.memset(hT[:, :, 0:1], 0.0), writes=[bh])
            if tb == NB - 1:
                S.op("pool", lambda e: e.memset(hT[:, :, 513:514], 0.0), writes=[bh])
            for k in range(0, 8, 2):
                S.dma("sp", hT[:, k:k + 2, c0:c0 + (hi - lo)], HTin[s, k:k + 2, :, lo:hi].rearrange("k p t -> p k t"), writes=[bh])
            hal = hT[:, :, 0:514:513]
            for f0 in range(0, 44, 11):
                p = 6
                for f in range(f0, f0 + 11):
                    for k in range(8):
                        S.op("pe", lambda e: e.matmul(C.ps[p][:, (f - f0) * 2:(f - f0) * 2 + 2], lhsT=Wu[:, k, f * 128:(f + 1) * 128], rhs=hal[:, k, :],
                                                      start=(k == 0), stop=(k == 7)), reads=[bWu, bh], writes=[C.bps[p]])
                pv = C.ps[p][:, 0:22].rearrange("p (f c) -> p c f", c=2)
                S.op("dve", lambda e: e.tensor_tensor(out=ah[:, 0, f0:f0 + 11], in0=pv[:, 0, :], in1=cw[:, f0:f0 + 11, 0], op=ALU.mult), reads=[C.bps[p], bcw], writes=[bah])
                S.op("dve", lambda e: e.tensor_tensor(out=ah[:, 1, f0:f0 + 11], in0=pv[:, 1, :], in1=cw[:, f0:f0 + 11, 2], op=ALU.mult), reads=[C.bps[p], bcw], writes=[bah])
            for f in range(22):
                yu, yg, gg = yus[f % 2], ygs[f % 2], ggs[f % 2]
                byu, byg, bgg = byus[f % 2], bygs[f % 2], bggs[f % 2]
                for (ft, yt, byt) in ((f, yu, byu), (22 + f, yg, byg)):
                    p = pi % 4
                    pi += 1
                    for k in range(8):
                        S.op("pe", lambda e: e.matmul(C.ps[p][:], lhsT=Wu[:, k, ft * 128:(ft + 1) * 128], rhs=hT[:, k, 1:513], start=(k == 0), stop=(k == 7)),
                             reads=[bWu, bh], writes=[C.bps[p]])
                    A = C.ps[p]
                    S.op("act", lambda e: e.activation(out=yt[:], in_=A[:], func=AF.Identity, scale=cw[:, ft, 1:2], bias=cw[:, ft, 3:4]), reads=[C.bps[p], bcw], writes=[byt])
                    S.op("dve", lambda e: e.scalar_tensor_tensor(out=yt[:, 1:512], in0=A[:, 0:511], scalar=cw[:, ft, 0:1], in1=yt[:, 1:512], op0=ALU.mult, op1=ALU.add),
                         reads=[C.bps[p], bcw, byt], writes=[byt])
                    S.op("dve", lambda e: e.scalar_tensor_tensor(out=yt[:, 0:511], in0=A[:, 1:512], scalar=cw[:, ft, 2:3], in1=yt[:, 0:511], op0=ALU.mult, op1=ALU.add),
                         reads=[C.bps[p], bcw, byt], writes=[byt])
                    S.op("pool", lambda e: e.tensor_tensor(out=yt[:, 0:1], in0=yt[:, 0:1], in1=ah[:, 0, ft:ft + 1], op=ALU.add), reads=[byt, bah], writes=[byt])
                    S.op("pool", lambda e: e.tensor_tensor(out=yt[:, 511:512], in0=yt[:, 511:512], in1=ah[:, 1, ft:ft + 1], op=ALU.add), reads=[byt, bah], writes=[byt])
                def s2(f=f, yu=yu, yg=yg, gg=gg, byu=byu, byg=byg, bgg=bgg):
                    S.op("act", lambda e: e.activation(out=gg[:], in_=yg[:], func=AF.Gelu_apprx_tanh), reads=[byg], writes=[bgg])
                    S.op("pool", lambda e: e.tensor_tensor(out=gT[:, f, :], in0=yu[:], in1=gg[:], op=ALU.mult), reads=[byu, bgg], writes=[bgT[f]])
                if prev_s2[0] is not None:
                    prev_s2[0]()
                prev_s2[0] = s2
            prev_s2[0]()
            prev_s2[0] = None
            for j in range(4):
                tt = tb * 4 + j
                r0 = s * T + tt * 128
                S.dma("sp", xt[:], Xin[r0:r0 + 128, :], writes=[bxt])
                for c in range(2):
                    p = 4 + (pi % 2)
                    pi += 1
                    for f in range(22):
                        S.op("pe", lambda e: e.matmul(C.ps[p][:], lhsT=gT[:, f, j * 128:(j + 1) * 128], rhs=Wd[:, f, c * 512:(c + 1) * 512], start=(f == 0), stop=(f == 21)),
                             reads=[bWd, bgT[f]], writes=[C.bps[p]])
                    S.op("dve", lambda e: e.tensor_tensor(out=x2[:, c * 512:(c + 1) * 512], in0=C.ps[p][:], in1=xt[:, c * 512:(c + 1) * 512], op=ALU.add),
                         reads=[C.bps[p], bxt], writes=[bx2])
                if layer == 0:
                    S.dma("sp", C.X2[r0:r0 + 128, :], x2[:], reads=[bx2])
                    hb, bhb = hbs[tt % 2], bhbs[tt % 2]
                    rmsnorm(C, nb, x2[:], bx2, grow, bg, hb[:], bhb)

                    def fin(hb=hb, bhb=bhb, tt=tt, s=s):
                        transpose_to(C, hb, bhb, 8, lambda a, b: h3T[:, a:b, :], bh3, "act")
                        S.dma("sp", C.HTb[s, :, :, tt * 128:(tt + 1) * 128].rearrange("k p t -> p k t"), h3T[:], reads=[bh3])
                    deferred.append(fin)
                    if len(deferred) > 1:
                        deferred.pop(0)()
                else:
                    rmsnorm(C, nb, x2[:], bx2, grow, bg, yo[:], byo)
                    S.dma("sp", C.y[r0:r0 + 128, :], yo[:], reads=[byo])
            while deferred:
                deferred.pop(0)()
    S.barrier()
    sc.close()


def p5_inproj1(C):
    nc, S = C.nc, C.S
    sc = Scope(nc)
    alloc_psum(C, sc, 6, 2)
    W, bW = load_w(C, sc, "w1", C.w_in1, 8, 6144)
    hT = [sc.sb("h3T", [128, 8, 512], BF16) for _ in range(2)]
    bh = bufs(2)
    cs = [sc.sb("cs1", [128, 2, 512], F32) for _ in range(2)]
    bcs = bufs(2)
    tm = [sc.sb("tm", [128, 512], F32) for _ in range(4)]
    btm = bufs(4)
    ob = [sc.sb("ob", [128, 512], BF16) for _ in range(4)]
    bob = bufs(4)
    kr = [sc.sb("kr", [128, 2, 512], BF16) for _ in range(2)]
    bkr = bufs(2)
    ktm = [sc.sb("ktm", [128, 256], BF16) for _ in range(2)]
    bktm = bufs(2)
    oi = 0
    pi = 0
    ki = 0
    for s in range(C.nseq):
        for tb in range(NB):
            h, bhc = hT[tb % 2], bh[tb % 2]
            csc, bcsc = cs[tb % 2], bcs[tb % 2]
            for k in range(0, 8, 2):
                S.dma("sp", h[:, k:k + 2, :], C.HTb[s, k:k + 2, :, tb * 512:(tb + 1) * 512].rearrange("k p t -> p k t"), writes=[bhc])
            S.dma("sp", csc[:, 0, :], C.cos1[:, tb * 512:(tb + 1) * 512], writes=[bcsc])
            S.dma("sp", csc[:, 1, :], C.sin1[:, tb * 512:(tb + 1) * 512], writes=[bcsc])
            for qk in range(2):
                for hd in range(4):
                    pa, pb = pi % 4, (pi + 1) % 4
                    pi += 2
                    for (p, c0) in ((pa, qk * 1024 + hd * 256), (pb, qk * 1024 + hd * 256 + 128)):
                        for k in range(8):
                            S.op("pe", lambda e: e.matmul(C.ps[p][:], lhsT=W[:, k, c0:c0 + 128], rhs=h[:, k, :], start=(k == 0), stop=(k == 7)),
                                 reads=[bW, bhc], writes=[C.bps[p]])
                    outs = []
                    for half in range(2):
                        ta, bta = tm[(oi * 2) % 4], btm[(oi * 2) % 4]
                        tb_, btb = tm[(oi * 2 + 1) % 4], btm[(oi * 2 + 1) % 4]
                        o, bo = ob[oi % 4], bob[oi % 4]
                        oi += 1
                        S.op("dve", lambda e: e.tensor_tensor(out=ta[:], in0=C.ps[pa][:], in1=csc[:, half, :], op=ALU.mult), reads=[C.bps[pa], bcsc], writes=[bta])
                        S.op("dve", lambda e: e.tensor_tensor(out=tb_[:], in0=C.ps[pb][:], in1=csc[:, 1 - half, :], op=ALU.mult), reads=[C.bps[pb], bcsc], writes=[btb])
                        if qk == 0:
                            S.op("pool", lambda e: e.tensor_tensor(out=o[:], in0=ta[:], in1=tb_[:], op=(ALU.subtract if half == 0 else ALU.add)), reads=[bta, btb], writes=[bo])
                            S.dma("sp", C.QTR[s, hd * 2 + half, :, tb * 512:(tb + 1) * 512], o[:], reads=[bo])
                        else:
                            kk, bkk = kr[ki % 2], bkr[ki % 2]
                            S.op("pool", lambda e: e.tensor_tensor(out=kk[:, half, :], in0=ta[:], in1=tb_[:], op=(ALU.subtract if half == 0 else ALU.add)), reads=[bta, btb], writes=[bkk])
                            S.dma("sp", C.KTR[s, hd * 2 + half, :, tb * 512:(tb + 1) * 512], kk[:, half, :], reads=[bkk])
                    if qk == 1:
                        kk, bkk = kr[ki % 2], bkr[ki % 2]
                        ki += 1
                        for j in range(4):
                            r0 = s * T + (tb * 4 + j) * 128
                            kt_, bkt_ = ktm[j % 2], bktm[j % 2]
                            hh = C.pti % C.npt
                            C.pti += 1
                            for half in range(2):
                                S.op("pe", lambda e: e.transpose(out=C.pt[hh][:, half, :], in_=kk[:, half, j * 128:(j + 1) * 128], identity=C.idt[:]),
                                     reads=[bkk, C.bidt], writes=[C.bpt[hh]])
                            S.op("act", lambda e: e.copy(out=kt_[:].rearrange("p (a b) -> p a b", a=2), in_=C.pt[hh][:, 0:2, :]), reads=[C.bpt[hh]], writes=[bkt_])
                            S.dma("sp", C.KTM[r0:r0 + 128, hd * 256:(hd + 1) * 256], kt_[:], reads=[bkt_])
            for j in range(4):
                r0 = s * T + (tb * 4 + j) * 128
                for c in range(8):
                    p = pi % 4
                    pi += 1
                    c0 = 2048 + c * 512
                    for k in range(8):
                        S.op("pe", lambda e: e.matmul(C.ps[p][:], lhsT=h[:, k, j * 128:(j + 1) * 128], rhs=W[:, k, c0:c0 + 512], start=(k == 0), stop=(k == 7)),
                             reads=[bW, bhc], writes=[C.bps[p]])
                    o, bo = ob[oi % 4], bob[oi % 4]
                    oi += 1
                    if c < 4:
                        S.op("dve", lambda e: e.tensor_copy(out=o[:], in_=C.ps[p][:]), reads=[C.bps[p]], writes=[bo])
                        S.dma("sp", C.VR[r0:r0 + 128, c * 512:(c + 1) * 512], o[:], reads=[bo])
                    else:
                        S.op("act", lambda e: e.activation(out=o[:], in_=C.ps[p][:], func=AF.Silu), reads=[C.bps[p]], writes=[bo])
                        S.dma("sp", C.SG[r0:r0 + 128, (c - 4) * 512:(c - 3) * 512], o[:], reads=[bo])
    S.barrier()
    sc.close()


def p6_retention(C):
    nc, S = C.nc, C.S
    sc = Scope(nc)
    alloc_psum(C, sc, 0, 1)
    pin = sc.psum("pin", [128, 512], F32)
    bpin = Buf()
    pout = [sc.psum("pout", [128, 512], F32) for _ in range(2)]
    bpout = bufs(2)
    pst = [sc.psum("pst", [128, 2, 512], F32) for _ in range(2)]
    bpst = bufs(2)
    Wo, bWo = load_w(C, sc, "wo1", C.w_out1, 16, D)
    grow, bg = load_grow(C, sc, 3)
    nb = NormBufs(sc, 1)
    rc = sc.sb("retc", [128, 7, 128], F32)
    brc = Buf()
    S.dma("sp", rc[:], C.retc[0:7].rearrange("c p q -> p c q"), writes=[brc])
    lg = sc.sb("lg", [128, 8], F32)
    blg = Buf()
    S.dma("sp", lg[:], C.decay.partition_broadcast(128), writes=[blg])
    S.op("act", lambda e: e.activation(out=lg[:], in_=lg[:], func=AF.Exp), reads=[blg], writes=[blg])
    S.op("act", lambda e: e.activation(out=lg[:], in_=lg[:], func=AF.Ln, bias=1.0), reads=[blg], writes=[blg])
    S.op("dve", lambda e: e.tensor_scalar(out=lg[:], in0=lg[:], scalar1=-1.0, scalar2=None, op0=ALU.mult), reads=[blg], writes=[blg])
    DT = sc.sb("DT", [128, 8, 128], F32)
    QD = sc.sb("QD", [128, 8, 128], F32)
    KD = sc.sb("KD", [128, 8], F32)
    SD = sc.sb("SD", [128, 8], F32)
    bDT, bQD, bKD, bSD = Buf(), Buf(), Buf(), Buf()
    for d in range(2):
        for hd in range(4):
            i = d * 4 + hd
            S.op("act", lambda e: e.activation(out=DT[:, i, :], in_=rc[:, 2 * d, :], func=AF.Exp, scale=lg[:, i:i + 1]), reads=[brc, blg], writes=[bDT])
            S.op("dve", lambda e: e.tensor_tensor(out=DT[:, i, :], in0=DT[:, i, :], in1=rc[:, 2 * d + 1, :], op=ALU.mult), reads=[brc, bDT], writes=[bDT])
            S.op("act", lambda e: e.activation(out=QD[:, i, :], in_=rc[:, 4 + d, :], func=AF.Exp, scale=lg[:, i:i + 1]), reads=[brc, blg], writes=[bQD])
            S.op("dve", lambda e: e.tensor_scalar(out=QD[:, i, :], in0=QD[:, i, :], scalar1=1.0 / 16, scalar2=None, op0=ALU.mult), reads=[bQD], writes=[bQD])
            S.op("act", lambda e: e.activation(out=KD[:, i:i + 1], in_=rc[:, 6, d:d + 1], func=AF.Exp, scale=lg[:, i:i + 1]), reads=[brc, blg], writes=[bKD])
            S.op("act", lambda e: e.activation(out=SD[:, i:i + 1], in_=rc[:, 6, 2:3], func=AF.Exp, scale=lg[:, i:i + 1]), reads=[brc, blg], writes=[bSD])

    QDT = sc.sb("QDT", [128, 2, 8, 128], F32)
    KDT = sc.sb("KDT", [128, 2, 1024], F32)
    onesf = sc.sb("onesf", [128, 256], F32)
    bQDT, bKDT, bof = Buf(), Buf(), Buf()
    S.op("pool", lambda e: e.memset(onesf[:], 1.0), writes=[bof])
    for d in range(2):
        for hd in range(4):
            i = d * 4 + hd
            for t in range(2):
                S.op("pool", lambda e: e.tensor_copy(out=QDT[:, d, hd * 2 + t, :], in_=QD[:, i, :]), reads=[bQD], writes=[bQDT])
            S.op("dve", lambda e: e.tensor_scalar(out=KDT[:, d, hd * 256:(hd + 1) * 256], in0=onesf[:], scalar1=KD[:, i:i + 1], scalar2=None, op0=ALU.mult),
                 reads=[bof, bKD], writes=[bKDT])

    S32 = sc.sb("S32", [128, 4, 2, 512], F32)
    Sbf = sc.sb("Sbf", [128, 4, 2, 512], BF16)
    bS32, bSbf = bufs(4), bufs(4)
    qT = [sc.sb("qT", [128, 8, 128], BF16) for _ in range(2)]
    kT = [sc.sb("kT", [128, 8, 128], BF16) for _ in range(2)]
    kM = [sc.sb("kM", [128, D], BF16) for _ in range(2)]
    vM = [sc.sb("vM", [128, 2048], BF16) for _ in range(2)]
    bq, bk, bkm, bv = bufs(2), bufs(2), bufs(2), bufs(2)
    iT = [sc.sb("iT", [128, 4, 128], BF16) for _ in range(2)]
    biT = bufs(2)
    qd = [sc.sb("qd", [128, 8, 128], BF16) for _ in range(2)]
    bqd = bufs(2)
    kd = [sc.sb("kd", [128, 1024], BF16) for _ in range(2)]
    bkd = bufs(2)
    rf = [sc.sb("rf", [128, 512], F32) for _ in range(2)]
    brf = bufs(2)
    rfl = [sc.sb("rfl", [128, 2048], F32) for _ in range(2)]
    brfl = bufs(2)
    sgl = [sc.sb("sgl", [128, 2048], BF16) for _ in range(2)]
    bsgl = bufs(2)
    rr = [sc.sb("rr", [128, 512], F32) for _ in range(2)]
    brr = bufs(2)
    st = [sc.sb("st", [128, 8], F32) for _ in range(2)]
    bst = bufs(2)
    rg = sc.sb("rg", [128, 2048], BF16)
    brg = bufs(4)
    rgT = sc.sb("rgT", [128, 16, 128], BF16)
    brgT = Buf()
    xt = [sc.sb("xt", [128, D], F32) for _ in range(2)]
    bxt = bufs(2)
    x3 = sc.sb("x3", [128, D], F32)
    bx3 = Buf()
    hb = sc.sb("hb", [128, D], BF16)
    bhb = Buf()
    h4T = sc.sb("h4T", [128, 8, 128], BF16)
    bh4 = Buf()
    cnt = dict(r=0, o=0)

    def mk_chunk(s, d, n, c, ci):
        r0 = s * T + c * 128
        j = ci % 2
        q_, k_, km_, v_ = qT[j], kT[j], kM[j], vM[j]

        def a():
            S.dma("sp", q_[:], C.QTR[s, :, :, c * 128:(c + 1) * 128].rearrange("k p t -> p k t"), writes=[bq[j]])
            S.dma("sp", k_[:], C.KTR[s, :, :, c * 128:(c + 1) * 128].rearrange("k p t -> p k t"), writes=[bk[j]])
            S.dma("sp", km_[:], C.KTM[r0:r0 + 128, :], writes=[bkm[j]])
            S.dma("sp", v_[:], C.VR[r0:r0 + 128, :], writes=[bv[j]])
            if d == 1:
                S.dma("sp", rfl[j][:], C.RF[r0:r0 + 128, :], writes=[brfl[j]])
                S.dma("sp", sgl[j][:], C.SG[r0:r0 + 128, :], writes=[bsgl[j]])
                S.dma("sp", xt[j][:], C.X2[r0:r0 + 128, :], writes=[bxt[j]])
            for hd in range(4):
                for i in range(2):
                    S.op("pe", lambda e: e.matmul(pin[:, hd * 128:(hd + 1) * 128], lhsT=k_[:, hd * 2 + i, :], rhs=q_[:, hd * 2 + i, :], start=(i == 0), stop=(i == 1)),
                         reads=[bk[j], bq[j]], writes=[bpin])
            S.op("dve", lambda e: e.tensor_tensor(out=iT[j][:], in0=pin[:].rearrange("p (h q) -> p h q", h=4), in1=DT[:, d * 4:(d + 1) * 4, :], op=ALU.mult),
                 reads=[bpin, bDT], writes=[biT[j]])
            if n > 0:
                S.op("pool", lambda e: e.tensor_tensor(out=qd[j][:], in0=q_[:], in1=QDT[:, d, :, :], op=ALU.mult), reads=[bq[j], bQDT], writes=[bqd[j]])
            S.op("pool", lambda e: e.tensor_tensor(out=kd[j][:], in0=km_[:], in1=KDT[:, d, :], op=ALU.mult), reads=[bkm[j], bKDT], writes=[bkd[j]])

        def b():
            for hd in range(4):
                di = d * 4 + hd
                o_ = cnt["o"] % 2
                cnt["o"] += 1
                po, bpo = pout[o_], bpout[o_]
                ps_, bps_ = pst[o_], bpst[o_]
                vh = v_[:, hd * 512:(hd + 1) * 512]
                S.op("pe", lambda e: e.matmul(po[:], lhsT=iT[j][:, hd, :], rhs=vh, start=True, stop=(n == 0)), reads=[biT[j], bv[j]], writes=[bpo])
                if n > 0:
                    for i in range(2):
                        S.op("pe", lambda e: e.matmul(po[:], lhsT=qd[j][:, hd * 2 + i, :], rhs=Sbf[:, hd, i, :], start=False, stop=(i == 1)),
                             reads=[bqd[j], bSbf[hd]], writes=[bpo])
                for i in range(2):
                    S.op("pe", lambda e: e.matmul(ps_[:, i, :], lhsT=kd[j][:, hd * 256 + i * 128:hd * 256 + (i + 1) * 128], rhs=vh, start=True, stop=True),
                         reads=[bkd[j], bv[j]], writes=[bps_])
                if n == 0:
                    S.op("dve", lambda e: e.tensor_copy(out=S32[:, hd, :, :], in_=ps_[:]), reads=[bps_], writes=[bS32[hd]])
                else:
                    S.op("dve", lambda e: e.scalar_tensor_tensor(out=S32[:, hd, :, :], in0=S32[:, hd, :, :], scalar=SD[:, di:di + 1], in1=ps_[:], op0=ALU.mult, op1=ALU.add),
                         reads=[bps_, bS32[hd], bSD], writes=[bS32[hd]])
                S.op("act", lambda e: e.copy(out=Sbf[:, hd, :, :], in_=S32[:, hd, :, :]), reads=[bS32[hd]], writes=[bSbf[hd]])
                ri = cnt["r"] % 2
                cnt["r"] += 1
                if d == 0:
                    S.op("act", lambda e: e.copy(out=rf[ri][:], in_=po[:]), reads=[bpo], writes=[brf[ri]])
                    S.dma("sp", C.RF[r0:r0 + 128, hd * 512:(hd + 1) * 512], rf[ri][:], reads=[brf[ri]])
                else:
                    r_, br_ = rr[ri], brr[ri]
                    s_, bs_ = st[ri], bst[ri]
                    S.op("dve", lambda e: e.tensor_tensor(out=r_[:], in0=po[:], in1=rfl[j][:, hd * 512:(hd + 1) * 512], op=ALU.add), reads=[bpo, brfl[j]], writes=[br_])
                    S.op("dve", lambda e: e.bn_stats(out=s_[:, 0:6], in_=r_[:]), reads=[br_], writes=[bs_])
                    S.op("dve", lambda e: e.bn_aggr(out=s_[:, 6:8], in_=s_[:, 0:6]), reads=[bs_], writes=[bs_])
                    S.op("dve", lambda e: e.tensor_scalar(out=s_[:, 7:8], in0=s_[:, 7:8], scalar1=EPS, scalar2=None, op0=ALU.add), reads=[bs_], writes=[bs_])
                    S.op("act", lambda e: e.activation(out=s_[:, 7:8], in_=s_[:, 7:8], func=AF.Sqrt), reads=[bs_], writes=[bs_])
                    S.op("dve", lambda e: e.reciprocal(out=s_[:, 7:8], in_=s_[:, 7:8]), reads=[bs_], writes=[bs_])
                    S.op("dve", lambda e: e.tensor_scalar(out=r_[:], in0=r_[:], scalar1=s_[:, 6:7], scalar2=s_[:, 7:8], op0=ALU.subtract, op1=ALU.mult), reads=[br_, bs_], writes=[br_])
                    S.op("pool", lambda e: e.tensor_tensor(out=rg[:, hd * 512:(hd + 1) * 512], in0=r_[:], in1=sgl[j][:, hd * 512:(hd + 1) * 512], op=ALU.mult),
                         reads=[br_, bsgl[j]], writes=[brg[hd]])
            if d == 1:
                for hd in range(4):
                    transpose_to(C, rg[:, hd * 512:(hd + 1) * 512], brg[hd], 4, lambda a_, b_: rgT[:, hd * 4 + a_:hd * 4 + b_, :], brgT, "act")
                for c2 in range(2):
                    o_ = cnt["o"] % 2
                    cnt["o"] += 1
                    po, bpo = pout[o_], bpout[o_]
                    for k in range(16):
                        S.op("pe", lambda e: e.matmul(po[:], lhsT=rgT[:, k, :], rhs=Wo[:, k, c2 * 512:(c2 + 1) * 512], start=(k == 0), stop=(k == 15)),
                             reads=[bWo, brgT], writes=[bpo])
                    S.op("dve", lambda e: e.tensor_tensor(out=x3[:, c2 * 512:(c2 + 1) * 512], in0=po[:], in1=xt[j][:, c2 * 512:(c2 + 1) * 512], op=ALU.add),
                         reads=[bpo, bxt[j]], writes=[bx3])
                S.dma("sp", C.X3[r0:r0 + 128, :], x3[:], reads=[bx3])
                rmsnorm(C, nb, x3[:], bx3, grow, bg, hb[:], bhb)
                transpose_to(C, hb, bhb, 8, lambda a_, b_: h4T[:, a_:b_, :], bh4, "act")
                S.dma("sp", C.HTb[s, :, :, c * 128:(c + 1) * 128].rearrange("k p t -> p k t"), h4T[:], reads=[bh4])
        return a, b

    ci = 0
    for s in range(C.nseq):
        for d in range(2):
            chunks = list(range(NT)) if d == 0 else list(range(NT - 1, -1, -1))
            st_ = []
            for n, c in enumerate(chunks):
                st_.append(mk_chunk(s, d, n, c, ci))
                ci += 1
            for n in range(NT + 1):
                if n < NT:
                    st_[n][0]()
                if n >= 1:
                    st_[n - 1][1]()
            S.barrier()
    S.barrier()
    sc.close()


def _prep_shared(inp):
    f = np.float32
    w_in = np.asarray(inp["even_w_in"], f)[0]
    def swap(cols):
        c = cols.reshape(D, 8, 2, 32)
        return c[:, :, ::-1, :].reshape(D, 512)
    dq = w_in[:, 1536:2048]
    dk = w_in[:, 2048:2560]
    w_in0 = np.ascontiguousarray(np.concatenate([w_in, swap(dq), swap(dk)], axis=1))
    cw = np.asarray(inp["ffn_conv_w"], f)
    cb = np.asarray(inp["ffn_conv_b"], f)
    cwT = []
    for l in range(2):
        a = np.concatenate([cw[l], cb[l][None]], 0)
        a = a.reshape(4, 44, 128).transpose(2, 1, 0)
        cwT.append(np.ascontiguousarray(a.reshape(128, 44 * 4)))
    norms = np.ascontiguousarray(np.concatenate([
        np.asarray(inp["attn_norm"], f)[0:1], np.asarray(inp["ffn_norm"], f)[0:1],
        np.asarray(inp["attn_norm"], f)[1:2], np.asarray(inp["ffn_norm"], f)[1:2],
        np.asarray(inp["final_norm"], f)[None]], 0))
    rpb = np.asarray(inp["na_rpb"], f)[0]
    rpbT = np.ascontiguousarray(rpb[:, ::-1, :].transpose(2, 0, 1).reshape(31, 8 * 15))
    decay = np.ascontiguousarray(np.concatenate([np.asarray(inp["ret_decay_fwd"], f)[0], np.asarray(inp["ret_decay_bwd"], f)[0]]))
    sh = dict(
        w_in0=w_in0, w_out0=np.ascontiguousarray(np.asarray(inp["even_w_out"], f)[0]),
        w_up0=np.ascontiguousarray(np.asarray(inp["ffn_w_up"], f)[0]), w_up1=np.ascontiguousarray(np.asarray(inp["ffn_w_up"], f)[1]),
        w_dn0=np.ascontiguousarray(np.asarray(inp["ffn_w_down"], f)[0]), w_dn1=np.ascontiguousarray(np.asarray(inp["ffn_w_down"], f)[1]),
        cwT0=cwT[0], cwT1=cwT[1],
        w_in1=np.ascontiguousarray(np.asarray(inp["ret_w_in"], f)[0]), w_out1=np.ascontiguousarray(np.asarray(inp["ret_w_out"], f)[0]),
        norms=norms, rpbT=rpbT, decay=decay,
    )
    sh.update(_consts())
    return sh


def _assign():
    seqs = [("p", i) for i in range(BATCH)] + [("s", i) for i in range(DEC_BATCH)]
    slots = [[] for _ in range(NCORES)]
    for i, sq in enumerate(seqs):
        slots[i % NCORES].append(sq)
    return slots


def kernel(**inputs):
    nseq = 3
    sh = _prep_shared(inputs)
    xp = np.asarray(inputs["x_prompt"], np.float32)
    xs = np.asarray(inputs["x_sample"], np.float32)
    slots = _assign()
    in_maps = []
    for c in range(NCORES):
        xc = np.zeros((nseq * T, D), np.float32)
        for j, (kind, i) in enumerate(slots[c]):
            xc[j * T:(j + 1) * T] = xp[i] if kind == "p" else xs[i]
        for j in range(len(slots[c]), nseq):
            xc[j * T:(j + 1) * T] = xc[0:T]
        m = dict(sh)
        m["x"] = xc
        in_maps.append(m)
    nc, _ = build(nseq)
    res = run_bass_kernel_spmd(nc, in_maps, core_ids=list(range(NCORES)))
    yp = np.zeros((BATCH, T, D), np.float32)
    ys = np.zeros((DEC_BATCH, T, D), np.float32)
    for c in range(NCORES):
        yc = np.asarray(res.results[c]["y"]).reshape(nseq, T, D)
        for j, (kind, i) in enumerate(slots[c]):
            if kind == "p":
                yp[i] = yc[j]
            else:
                ys[i] = yc[j]
    return (yp, ys)
```

```python
import math
from contextlib import ExitStack

import numpy as np
import ml_dtypes
import concourse.bass as bass
import concourse.mybir as mybir
from concourse.bass_utils import run_bass_kernel_spmd

F32 = mybir.dt.float32
BF16 = mybir.dt.bfloat16
AF = mybir.ActivationFunctionType
ALU = mybir.AluOpType

T = 4096
D = 1024
NT = 32
NB = 8
FF = 2816
NCORES = 8
EPS = 1e-6
BATCH, DEC_BATCH = 16, 4


class Buf:
    __slots__ = ("w", "r")

    def __init__(self):
        self.w = None
        self.r = {}


def bufs(n):
    return [Buf() for _ in range(n)]


class Sched:
    LIMIT = 30000

    def __init__(self, nc, n_dma_sems=40):
        self.nc = nc
        self.eng = dict(pe=nc.tensor, act=nc.scalar, dve=nc.vector, pool=nc.gpsimd, sp=nc.sync)
        self.csem, self.ccnt, self.nsem = {}, {}, 0
        for e in self.eng:
            self._newsem(e)
        self.known = {e: {} for e in self.eng}
        self.dsems, self.dcnt, self.dnext = {}, {}, {}
        for e, n in (("sp", n_dma_sems), ("pool", 12), ("act", 8)):
            self.dsems[e] = [nc.alloc_semaphore(f"dq_{e}{i}") for i in range(n)]
            self.dcnt[e] = [0] * n
            self.dnext[e] = 0
        self.ninstr = 0
        self.out_tks = []
        import os
        self.cap = int(os.environ.get('KCAP', '1000000000'))
        self.nreal = 0

    def _newsem(self, e):
        self.nsem += 1
        self.csem[e] = self.nc.alloc_semaphore(f"c_{e}_{self.nsem}")
        self.ccnt[e] = 0

    def _wait(self, e, tk):
        if tk is None:
            return
        sem, val = tk
        k = self.known[e]
        if k.get(sem.num, 0) >= val:
            return
        if e == "pe" and sem is self.csem["pe"]:
            return
        self.eng[e].wait_ge(sem, val)
        self.ninstr += 1
        k[sem.num] = val

    def _deps(self, e, reads, writes):
        for b in reads:
            self._wait(e, b.w)
        for b in writes:
            self._wait(e, b.w)
            for tk in list(b.r.values()):
                self._wait(e, tk)

    def _mark(self, tk, reads, writes):
        sem, val = tk
        for b in reads:
            b.r[sem.num] = tk
        for b in writes:
            b.w = tk
            b.r = {}

    def op(self, e, fn, reads=(), writes=()):
        self.nreal += 1
        if self.nreal > self.cap:
            return None
        self._deps(e, reads, writes)
        ins = fn(self.eng[e])
        if self.ccnt[e] >= self.LIMIT:
            self._newsem(e)
        self.ccnt[e] += 1
        sem = self.csem[e]
        ins.then_inc(sem, 1)
        self.ninstr += 1
        tk = (sem, self.ccnt[e])
        self._mark(tk, reads, writes)
        return tk

    def dma(self, e, out, in_, reads=(), writes=(), final=False, **kw):
        self.nreal += 1
        if self.nreal > self.cap:
            return None
        self._deps(e, reads, writes)
        i = self.dnext[e]
        self.dnext[e] = (i + 1) % len(self.dsems[e])
        sem = self.dsems[e][i]
        if self.dcnt[e][i] > 0:
            self._wait(e, (sem, self.dcnt[e][i]))
        self.dcnt[e][i] += 16
        self.eng[e].dma_start(out=out, in_=in_, **kw).then_inc(sem, 16)
        self.ninstr += 1
        tk = (sem, self.dcnt[e][i])
        self._mark(tk, reads, writes)
        return tk

    def barrier(self):
        for e in self.eng:
            for e2 in self.eng:
                if e2 != e and self.ccnt[e2] > 0:
                    self._wait(e, (self.csem[e2], self.ccnt[e2]))
            for q in self.dsems:
                for i, sem in enumerate(self.dsems[q]):
                    if self.dcnt[q][i] > 0:
                        self._wait(e, (sem, self.dcnt[q][i]))


class Scope:
    cnt = [0]

    def __init__(self, nc):
        self.nc = nc
        self.es = ExitStack()

    def sb(self, name, shape, dt):
        Scope.cnt[0] += 1
        return self.es.enter_context(self.nc.sbuf_tensor(f"{name}_{Scope.cnt[0]}", list(shape), dt))

    def psum(self, name, shape, dt):
        Scope.cnt[0] += 1
        return self.es.enter_context(self.nc.psum_tensor(f"{name}_{Scope.cnt[0]}", list(shape), dt))

    def close(self):
        self.es.close()


def alloc_psum(C, sc, nps, npt):
    C.pt = [sc.psum("pt", [128, 8, 128], BF16) for _ in range(npt)]
    C.bpt = bufs(npt)
    C.npt = npt
    C.pti = 0
    C.ps = [sc.psum("ps", [128, 512], F32) for _ in range(nps)]
    C.bps = bufs(nps)


def _rope_tables(dh, reps):
    half = dh // 2
    inv = (1.0 / (np.float32(10000.0) ** (np.arange(0, dh, 2, dtype=np.float32) / np.float32(dh)))).astype(np.float32)
    ang = np.arange(T, dtype=np.float32)[None, :] * inv[:, None]
    c = np.cos(ang).astype(np.float32)
    s = np.sin(ang).astype(np.float32)
    cos = np.concatenate([c, c], 0)
    sin = np.concatenate([-s, s], 0)
    return np.tile(cos, (reps, 1)).copy(), np.tile(sin, (reps, 1)).copy()


def _dil_masks():
    m = np.zeros((20, 128, 512), np.float32)
    k = np.arange(128)[:, None]
    q = np.arange(512)[None, :]
    for i in range(20):
        d = (i * 128 - 1024) + k - q
        ad = np.abs(d)
        m[i] = (ad <= 64).astype(np.float32) + ((d % 4 == 0) & (ad <= 256)) + ((d % 16 == 0) & (ad <= 1024))
    return m.astype(ml_dtypes.bfloat16)


def _na_onehot():
    L = np.zeros((31, 64, 128), np.float32)
    for cq in range(64):
        cs = min(max(cq - 8, 0), 48)
        for ck in range(cs, cs + 16):
            b = ck - cq + 15
            L[b, cq, ck] = 1.0
            L[b, cq, ck + 64] = 1.0
    return L.reshape(31, 64 * 128)


def _ret_consts():
    k = np.arange(128, dtype=np.float32)[:, None]
    q = np.arange(128, dtype=np.float32)[None, :]
    c = np.zeros((8, 128, 128), np.float32)
    c[0] = np.maximum(q - k, 0)
    c[1] = (q >= k) / 16.0
    c[2] = np.maximum(k - q, 0)
    c[3] = (k > q) / 16.0
    c[4] = np.broadcast_to(q + 1.0, (128, 128))
    c[5] = np.broadcast_to(128.0 - q, (128, 128))
    c[6, :, 0] = 127.0 - k[:, 0]
    c[6, :, 1] = k[:, 0]
    c[6, :, 2] = 128.0
    return c


_CONSTS = {}


def _consts():
    if not _CONSTS:
        cos0, sin0 = _rope_tables(64, 2)
        cos1, sin1 = _rope_tables(256, 1)
        _CONSTS.update(
            ident=np.eye(128, dtype=np.float32).astype(ml_dtypes.bfloat16),
            cos0=cos0, sin0=sin0,
            cos1=np.ascontiguousarray(cos1[:128]), sin1=np.ascontiguousarray(sin1[128:]),
            dmask=_dil_masks(), naL=_na_onehot(), retc=_ret_consts(),
        )
    return _CONSTS


class Ctx:
    pass


def build(nseq, upto=99, debug=False):
    nc = bass.Bass("TRN2", target_bir_lowering=False)
    S = Sched(nc)
    C = Ctx()
    C.nc, C.S, C.nseq = nc, S, nseq
    NTOK = nseq * T

    def din(name, shape, dt=F32):
        return nc.dram_tensor(name, list(shape), dt, kind="ExternalInput").ap()

    def dscr(name, shape, dt, out=False):
        return nc.dram_tensor(name, list(shape), dt, kind="ExternalOutput" if (out or (debug and name in debug)) else "Internal").ap()

    C.x = din("x", [NTOK, D])
    C.w_in0 = din("w_in0", [D, 4096])
    C.w_out0 = din("w_out0", [D, D])
    C.w_up = [din(f"w_up{l}", [D, 2 * FF]) for l in range(2)]
    C.w_dn = [din(f"w_dn{l}", [FF, D]) for l in range(2)]
    C.cwT = [din(f"cwT{l}", [128, 44 * 4]) for l in range(2)]
    C.w_in1 = din("w_in1", [D, 6144])
    C.w_out1 = din("w_out1", [2048, D])
    C.norms = din("norms", [5, D])
    C.rpbT = din("rpbT", [31, 8 * 15])
    C.decay = din("decay", [8])
    C.ident = din("ident", [128, 128], BF16)
    C.cos0 = din("cos0", [128, T])
    C.sin0 = din("sin0", [128, T])
    C.cos1 = din("cos1", [128, T])
    C.sin1 = din("sin1", [128, T])
    C.dmask = din("dmask", [20, 128, 512], BF16)
    C.naL = din("naL", [31, 64 * 128])
    C.retc = din("retc", [8, 128, 128])

    C.y = dscr("y", [NTOK, D], F32, out=True)
    C.QK0 = dscr("QK0", [nseq, 16, 128, T], BF16)
    C.V0 = dscr("V0", [NTOK, D], BF16)
    C.OT0 = dscr("OT0", [nseq, 8, 128, T], BF16)
    C.X1 = dscr("X1", [NTOK, D], F32)
    C.X2 = dscr("X2", [NTOK, D], F32)
    C.X3 = dscr("X3", [NTOK, D], F32)
    C.HTa = dscr("HTa", [nseq, 8, 128, T], BF16)
    C.HTb = dscr("HTb", [nseq, 8, 128, T], BF16)
    C.QTR = dscr("QTR", [nseq, 8, 128, T], BF16)
    C.KTR = dscr("KTR", [nseq, 8, 128, T], BF16)
    C.KTM = dscr("KTM", [NTOK, D], BF16)
    C.VR = dscr("VR", [NTOK, 2048], BF16)
    C.SG = dscr("SG", [NTOK, 2048], BF16)
    C.RF = dscr("RF", [NTOK, 2048], F32)
    C.TAZ = dscr("TAZ", [8, 2, 128, 32 * 64], F32)

    C.idt = nc.alloc_sbuf_tensor("idt", [128, 128], BF16)
    C.bidt = Buf()
    S.dma("sp", C.idt[:], C.ident[:, :], writes=[C.bidt])

    phases = [p1_inproj0, p2_attn, p3_outproj0,
              lambda c: p4_ffn(c, 0), p5_inproj1, p6_retention, lambda c: p4_ffn(c, 1)]
    for i, ph in enumerate(phases):
        if i >= upto:
            break
        ph(C)
        S.barrier()
    S.barrier()
    return nc, S


def load_w(C, sc, name, w_dram, kc, f, eng="pool"):
    t = sc.sb(name, [128, kc, f], BF16)
    b = Buf()
    src = w_dram.rearrange("(k p) f -> p k f", p=128)
    step = max(1, 2048 // f) if f < 2048 else 1
    for k in range(0, kc, step):
        k2 = min(kc, k + step)
        C.S.dma(eng, t[:, k:k2, :], src[:, k:k2, :], writes=[b])
    return t, b


def load_grow(C, sc, idx):
    g = sc.sb("grow", [128, D], F32)
    b = Buf()
    C.S.dma("sp", g[:], C.norms[idx, :].partition_broadcast(128), writes=[b])
    return g, b


class NormBufs:
    def __init__(self, sc, n=2):
        self.n = n
        self.junk = [sc.sb("nj", [128, D], BF16) for _ in range(n)]
        self.ss = [sc.sb("nss", [128, 1], F32) for _ in range(n)]
        self.b = [bufs(2) for _ in range(n)]
        self.i = 0


def rmsnorm(C, nb, xt, bx, grow, bg, out, bout):
    S = C.S
    i = nb.i % nb.n
    nb.i += 1
    junk, ss, (bj, bs) = nb.junk[i], nb.ss[i], nb.b[i]
    S.op("act", lambda e: e.activation(out=junk[:], in_=xt, func=AF.Square, accum_out=ss[:]), reads=[bx], writes=[bj, bs])
    S.op("dve", lambda e: e.tensor_scalar(out=ss[:], in0=ss[:], scalar1=1.0 / D, scalar2=EPS, op0=ALU.mult, op1=ALU.add), reads=[bs], writes=[bs])
    S.op("act", lambda e: e.activation(out=ss[:], in_=ss[:], func=AF.Sqrt), reads=[bs], writes=[bs])
    S.op("dve", lambda e: e.reciprocal(out=ss[:], in_=ss[:]), reads=[bs], writes=[bs])
    S.op("dve", lambda e: e.scalar_tensor_tensor(out=out, in0=xt, scalar=ss[:, 0:1], in1=grow[:], op0=ALU.mult, op1=ALU.mult),
         reads=[bx, bs, bg], writes=[bout])


def transpose_to(C, src, bsrc, nk, dst_fn, bdst, evac_eng):
    S = C.S
    for g in range(0, nk, 8):
        h = C.pti % C.npt
        C.pti += 1
        n = min(8, nk - g)
        for k in range(g, g + n):
            S.op("pe", lambda e: e.transpose(out=C.pt[h][:, k - g, :], in_=src[:, k * 128:(k + 1) * 128], identity=C.idt[:]),
                 reads=[bsrc, C.bidt], writes=[C.bpt[h]])
        if evac_eng == "act":
            S.op("act", lambda e: e.copy(out=dst_fn(g, g + n), in_=C.pt[h][:, 0:n, :]), reads=[C.bpt[h]], writes=[bdst])
        else:
            S.op(evac_eng, lambda e: e.tensor_copy(out=dst_fn(g, g + n), in_=C.pt[h][:, 0:n, :]), reads=[C.bpt[h]], writes=[bdst])


def p1_inproj0(C):
    nc, S = C.nc, C.S
    sc = Scope(nc)
    alloc_psum(C, sc, 6, 2)
    W, bW = load_w(C, sc, "w0", C.w_in0, 8, 4096)
    grow, bg = load_grow(C, sc, 0)
    nb = NormBufs(sc)
    xt = [sc.sb("xt", [128, D], F32) for _ in range(2)]
    bxt = bufs(2)
    hb = [sc.sb("hb", [128, D], BF16) for _ in range(2)]
    bhb = bufs(2)
    hT = [sc.sb("hT", [128, 8, 512], BF16) for _ in range(2)]
    bhT = bufs(2)
    cs = [sc.sb("cs", [128, 2, 512], F32) for _ in range(2)]
    bcs = bufs(2)
    t1 = [sc.sb("t1", [128, 512], F32) for _ in range(2)]
    t2 = [sc.sb("t2", [128, 512], F32) for _ in range(2)]
    bt1, bt2 = bufs(2), bufs(2)
    ob = [sc.sb("ob", [128, 512], BF16) for _ in range(4)]
    bob = bufs(4)
    oi = 0
    pi = 0
    for s in range(C.nseq):
        for tb in range(NB):
            hTc, bhTc = hT[tb % 2], bhT[tb % 2]
            csc, bcsc = cs[tb % 2], bcs[tb % 2]
            S.dma("sp", csc[:, 0, :], C.cos0[:, tb * 512:(tb + 1) * 512], writes=[bcsc])
            S.dma("sp", csc[:, 1, :], C.sin0[:, tb * 512:(tb + 1) * 512], writes=[bcsc])
            for j in range(4):
                tt = tb * 4 + j
                r0 = s * T + tt * 128
                S.dma("sp", xt[tt % 2][:], C.x[r0:r0 + 128, :], writes=[bxt[tt % 2]])
                rmsnorm(C, nb, xt[tt % 2][:], bxt[tt % 2], grow, bg, hb[tt % 2][:], bhb[tt % 2])
                transpose_to(C, hb[tt % 2], bhb[tt % 2], 8, lambda a, b: hTc[:, a:b, j * 128:(j + 1) * 128], bhTc, "act")
            for ft in list(range(8)) + list(range(12, 20)):
                p = pi % 4
                pi += 1
                for k in range(8):
                    S.op("pe", lambda e: e.matmul(C.ps[p][:], lhsT=W[:, k, ft * 128:(ft + 1) * 128], rhs=hTc[:, k, :], start=(k == 0), stop=(k == 7)),
                         reads=[bW, bhTc], writes=[C.bps[p]])
                o, bo = ob[oi % 4], bob[oi % 4]
                oi += 1
                if ft < 8:
                    S.op("act", lambda e: e.copy(out=o[:], in_=C.ps[p][:]), reads=[C.bps[p]], writes=[bo])
                    dst = ft
                else:
                    p2 = pi % 4
                    pi += 1
                    fs = ft + 12
                    for k in range(8):
                        S.op("pe", lambda e: e.matmul(C.ps[p2][:], lhsT=W[:, k, fs * 128:(fs + 1) * 128], rhs=hTc[:, k, :], start=(k == 0), stop=(k == 7)),
                             reads=[bW, bhTc], writes=[C.bps[p2]])
                    a, ba = t1[oi % 2], bt1[oi % 2]
                    b, bb = t2[oi % 2], bt2[oi % 2]
                    S.op("dve", lambda e: e.tensor_tensor(out=a[:], in0=C.ps[p][:], in1=csc[:, 0, :], op=ALU.mult), reads=[C.bps[p], bcsc], writes=[ba])
                    S.op("dve", lambda e: e.tensor_tensor(out=b[:], in0=C.ps[p2][:], in1=csc[:, 1, :], op=ALU.mult), reads=[C.bps[p2], bcsc], writes=[bb])
                    S.op("pool", lambda e: e.tensor_tensor(out=o[:], in0=a[:], in1=b[:], op=ALU.add), reads=[ba, bb], writes=[bo])
                    dst = ft - 4
                S.dma("sp", C.QK0[s, dst, :, tb * 512:(tb + 1) * 512], o[:], reads=[bo])
            for j in range(4):
                r0 = s * T + (tb * 4 + j) * 128
                for ci, c0 in enumerate((1024, 2560)):
                    p = pi % 4
                    pi += 1
                    for k in range(8):
                        S.op("pe", lambda e: e.matmul(C.ps[p][:], lhsT=hTc[:, k, j * 128:(j + 1) * 128], rhs=W[:, k, c0:c0 + 512], start=(k == 0), stop=(k == 7)),
                             reads=[bW, bhTc], writes=[C.bps[p]])
                    o, bo = ob[oi % 4], bob[oi % 4]
                    oi += 1
                    if ci == 0:
                        S.op("act", lambda e: e.copy(out=o[:], in_=C.ps[p][:]), reads=[C.bps[p]], writes=[bo])
                    else:
                        S.op("dve", lambda e: e.tensor_copy(out=o[:], in_=C.ps[p][:]), reads=[C.bps[p]], writes=[bo])
                    S.dma("sp", C.V0[r0:r0 + 128, ci * 512:(ci + 1) * 512], o[:], reads=[bo])
    S.barrier()
    sc.close()


def _na_valid(rq, rk):
    rs = min(max(rq - 4, 0), 56)
    return rs <= rk < rs + 8


def p2_attn(C):
    nc, S = C.nc, C.S
    sc = Scope(nc)
    alloc_psum(C, sc, 8, 0)
    sc2 = Scope(nc)
    L = sc2.sb("naL", [31, 64 * 128], F32)
    PT = sc2.sb("naPT", [31, 8 * 15], F32)
    Z = [sc2.sb("naZ", [128, 2, 32 * 64], F32) for _ in range(2)]
    bL, bPT, bZ = Buf(), Buf(), bufs(2)
    for c in range(0, 64 * 128, 2048):
        S.dma("sp", L[:, c:c + 2048], C.naL[:, c:c + 2048], writes=[bL])
    S.dma("sp", PT[:], C.rpbT[:, :], writes=[bPT])
    S.op("act", lambda e: e.activation(out=PT[:], in_=PT[:], func=AF.Exp), reads=[bPT], writes=[bPT])
    for z in range(2):
        S.op("pool", lambda e: e.memset(Z[z][:], 0.0), writes=[bZ[z]])
    for h in range(8):
        z = h % 2
        for half in range(2):
            p = (h * 2 + half) % 4
            pv = C.ps[p][:].rearrange("p (s c) -> p s c", c=64)
            ns = 8 if half == 0 else 7
            for cq in range(64):
                S.op("pe", lambda e: e.matmul(pv[:, 0:ns, cq], lhsT=L[:, cq * 128:(cq + 1) * 128], rhs=PT[:, h * 15 + half * 8:h * 15 + half * 8 + ns],
                                              start=True, stop=True), reads=[bL, bPT], writes=[C.bps[p]])
            a0 = (8 + half * 8) * 64
            S.op("dve", lambda e: e.tensor_copy(out=Z[z][0:64, 0, a0:a0 + ns * 64], in_=C.ps[p][0:64, 0:ns * 64]), reads=[C.bps[p]], writes=[bZ[z]])
            S.op("dve", lambda e: e.tensor_copy(out=Z[z][64:128, 0, a0 + 64:a0 + 64 + ns * 64], in_=C.ps[p][64:128, 0:ns * 64]), reads=[C.bps[p]], writes=[bZ[z]])
            i0, i1 = (4, 8) if half == 0 else (0, 4)
            S.op("dve", lambda e: e.tensor_copy(out=Z[z][0:64, 1, a0 + i0 * 64:a0 + i1 * 64], in_=C.ps[p][0:64, i0 * 64:i1 * 64]), reads=[C.bps[p]], writes=[bZ[z]])
            S.op("dve", lambda e: e.tensor_copy(out=Z[z][64:128, 1, a0 + 64 + i0 * 64:a0 + 64 + i1 * 64], in_=C.ps[p][64:128, i0 * 64:i1 * 64]), reads=[C.bps[p]], writes=[bZ[z]])
        S.dma("sp", C.TAZ[h, :, :, :].rearrange("v p c -> p v c"), Z[z][:], reads=[bZ[z]])
    S.barrier()
    sc2.close()
    TAz = sc.sb("TAz", [128, 2, 2, 32 * 64], F32)
    bTA = Buf()

    DM = sc.sb("dmask", [128, 20, 512], BF16)
    bDM = Buf()
    for i in range(0, 20, 4):
        S.dma("sp", DM[:, i:i + 4, :], C.dmask[i:i + 4].rearrange("i p q -> p i q"), writes=[bDM])
    LA = 8
    NST = 4
    Qz = [[sc.sb("Qz", [128, T], BF16) for _ in range(2)] for _ in range(2)]
    KT = [sc.sb("KT", [128, T], BF16) for _ in range(2)]
    VA = [sc.sb("VA", [128, NT, 256], BF16) for _ in range(2)]
    bQ, bK, bV = bufs(2), bufs(2), bufs(2)
    for b_ in range(2):
        S.op("pool", lambda e: e.memset(Qz[b_][0][64:128, :], 0.0), writes=[bQ[b_]])
        S.op("pool", lambda e: e.memset(Qz[b_][1][0:64, :], 0.0), writes=[bQ[b_]])
        S.op("pool", lambda e: e.memset(VA[b_][:, :, 64:192], 1.0), writes=[bV[b_]])
    NE, NEM = 8, LA + 4
    E = [sc.sb("E", [128, 512], F32) for _ in range(NE)]
    bE = bufs(NE)
    Eb = [sc.sb("Eb", [128, 512], BF16) for _ in range(NE)]
    bEb = bufs(NE)
    Em = [sc.sb("Em", [128, 512], BF16) for _ in range(NEM)]
    bEm = bufs(NEM)
    rc = [sc.sb("rc", [128, 512], F32) for _ in range(2)]
    rs = [sc.sb("rs", [128, 512], F32) for _ in range(2)]
    brc, brs = bufs(2), bufs(2)
    oT = [sc.sb("oT", [128, 512], BF16) for _ in range(2)]
    boT = bufs(2)

    stA, stB = [], []
    pending = []

    def mk_load(s, g, gi):
        def f():
            b_ = gi % 2
            qt, kt = (g, 4 + g) if g < 4 else (8 + (g - 4), 12 + (g - 4))
            for c in range(0, T, 2048):
                S.dma("sp", Qz[b_][0][0:64, c:c + 2048], C.QK0[s, qt, 0:64, c:c + 2048], writes=[bQ[b_]])
                S.dma("sp", Qz[b_][1][64:128, c:c + 2048], C.QK0[s, qt, 64:128, c:c + 2048], writes=[bQ[b_]])
                S.dma("sp", KT[b_][:, c:c + 2048], C.QK0[s, kt, :, c:c + 2048], writes=[bK[b_]])
            if g < 4:
                for hh in range(2):
                    S.dma("sp", TAz[:, hh, :, :], C.TAZ[g * 2 + hh, :, :, :].rearrange("v p c -> p v c"), writes=[bTA])
            vsrc = C.V0[s * T:(s + 1) * T, g * 128:(g + 1) * 128].rearrange("(t p) c -> p t c", p=128)
            for c in range(0, NT, 8):
                S.dma("sp", VA[b_][:, c:c + 8, 0:64], vsrc[:, c:c + 8, 0:64], writes=[bV[b_]])
                S.dma("sp", VA[b_][:, c:c + 8, 192:256], vsrc[:, c:c + 8, 64:128], writes=[bV[b_]])
        return f

    def mk_tile(s, g, gi, qb, qi, hp, ki, kb, nk, ti, fin):
        isna = g < 4
        b_ = gi % 2
        Qg, Kg, Vg = Qz[b_][hp], KT[b_], VA[b_]
        bQg, bKg, bVg = bQ[b_], bK[b_], bV[b_]
        p = ti % NST
        e_, be_ = (E[ti % NE], bE[ti % NE]) if isna else (Eb[ti % NE], bEb[ti % NE])
        em, bem = Em[ti % NEM], bEm[ti % NEM]
        pA = NST + (qi % 2) * 2
        pB = NST + 1 + (qi % 2) * 2
        R = qb * 8

        def a():
            S.op("pe", lambda e: e.matmul(C.ps[p][:], lhsT=Kg[:, kb * 128:(kb + 1) * 128], rhs=Qg[:, qb * 512:(qb + 1) * 512], start=True, stop=True),
                 reads=[bKg, bQg], writes=[C.bps[p]])
            S.op("act", lambda e: e.activation(out=e_[:], in_=C.ps[p][:], func=AF.Exp, scale=0.125), reads=[C.bps[p]], writes=[be_])
            if isna:
                rk0 = kb * 2
                s0 = 7 - rk0 + R
                fast = all((_na_valid(R + f, rk0 + ph) == (4 <= s0 - ph + f <= 11)) for f in range(8) for ph in range(2))
                if fast:
                    S.op("dve", lambda e: e.tensor_tensor(out=em[:], in0=e_[:], in1=TAz[:, hp, 1, (s0 + 8) * 64:(s0 + 16) * 64], op=ALU.mult), reads=[be_, bTA], writes=[bem])
                else:
                    for ph in range(2):
                        rk = rk0 + ph
                        fs = [f for f in range(8) if _na_valid(R + f, rk)]
                        pp = slice(ph * 64, (ph + 1) * 64)
                        if fs:
                            f1, f2 = fs[0], fs[-1]
                            assert fs == list(range(f1, f2 + 1))
                            c1 = (s0 + 8 + f1) * 64
                            n = f2 - f1 + 1
                            S.op("dve", lambda e: e.tensor_tensor(out=em[pp, f1 * 64:(f2 + 1) * 64], in0=e_[pp, f1 * 64:(f2 + 1) * 64],
                                                                in1=TAz[pp, hp, 0, c1:c1 + n * 64], op=ALU.mult), reads=[be_, bTA], writes=[bem])
                            if f1 > 0:
                                S.op("pool", lambda e: e.memset(em[pp, 0:f1 * 64], 0.0), writes=[bem])
                            if f2 < 7:
                                S.op("pool", lambda e: e.memset(em[pp, (f2 + 1) * 64:512], 0.0), writes=[bem])
                        else:
                            S.op("pool", lambda e: e.memset(em[pp, :], 0.0), writes=[bem])
            else:
                mi = (kb * 128 - qb * 512 + 1024) // 128
                eng = "dve" if (ti % 4) != 3 else "pool"
                S.op(eng, lambda e: e.tensor_tensor(out=em[:], in0=e_[:], in1=DM[:, mi, :], op=ALU.mult), reads=[be_, bDM], writes=[bem])

        def b():
            first, last = ki == 0, ki == nk - 1
            pX = pA if hp == 0 else pB
            S.op("pe", lambda e: e.matmul(C.ps[pX][:], lhsT=Vg[:, kb, hp * 128:(hp + 1) * 128], rhs=em[:], start=first, stop=last),
                 reads=[bVg, bem], writes=[C.bps[pX]])
            if fin:
                j = qi % 2

                def f1():
                    S.op("act", lambda e: e.copy(out=rc[j][64:128, :], in_=C.ps[pA][64:128, :]), reads=[C.bps[pA]], writes=[brc[j]])
                    S.op("act", lambda e: e.copy(out=rc[j][0:64, :], in_=C.ps[pB][0:64, :]), reads=[C.bps[pB]], writes=[brc[j]])
                    S.dma("sp", rs[j][0:64, :], rc[j][64:128, :], reads=[brc[j]], writes=[brs[j]])
                    S.dma("sp", rs[j][64:128, :], rc[j][0:64, :], reads=[brc[j]], writes=[brs[j]])

                def f2():
                    S.op("dve", lambda e: e.reciprocal(out=rs[j][:], in_=rs[j][:]), reads=[brs[j]], writes=[brs[j]])
                    S.op("dve", lambda e: e.tensor_tensor(out=oT[j][0:64, :], in0=C.ps[pA][0:64, :], in1=rs[j][0:64, :], op=ALU.mult), reads=[C.bps[pA], brs[j]], writes=[boT[j]])
                    S.op("dve", lambda e: e.tensor_tensor(out=oT[j][64:128, :], in0=C.ps[pB][64:128, :], in1=rs[j][64:128, :], op=ALU.mult), reads=[C.bps[pB], brs[j]], writes=[boT[j]])
                    S.dma("sp", C.OT0[s, g, :, qb * 512:(qb + 1) * 512], oT[j][:], reads=[boT[j]])
                pending.append([2, f1])
                pending.append([8, f2])
        return a, b

    ti = 0
    qi = 0
    gi = 0
    for s in range(C.nseq):
        for g in range(8):
            ld = mk_load(s, g, gi)
            firstpair = True
            for qb in range(NB):
                if g < 4:
                    R = qb * 8
                    lo = min(max(R - 4, 0), 56)
                    hi = min(max(R + 7 - 4, 0), 56) + 8
                    kbs = list(range(lo // 2, (hi + 1) // 2))
                else:
                    kbs = list(range(max(0, qb * 4 - 8), min(NT, qb * 4 + 4 + 8)))
                nk = len(kbs)
                for ki, kb in enumerate(kbs):
                    for hp in range(2):
                        a, b = mk_tile(s, g, gi, qb, qi, hp, ki, kb, nk, ti, fin=(ki == nk - 1 and hp == 1))
                        if firstpair:
                            a0 = a
                            a = (lambda ld=ld, a0=a0: (ld(), a0()))
                            firstpair = False
                        stA.append(a)
                        stB.append(b)
                        ti += 1
                qi += 1
            gi += 1
    n = len(stA)
    for i in range(n + LA + 10):
        if i < n:
            stA[i]()
        if LA <= i < n + LA:
            stB[i - LA]()
        for pe_ in list(pending):
            pe_[0] -= 1
            if pe_[0] <= 0:
                pe_[1]()
                pending.remove(pe_)
    assert not pending
    S.barrier()
    sc.close()


def p3_outproj0(C):
    nc, S = C.nc, C.S
    sc = Scope(nc)
    alloc_psum(C, sc, 6, 2)
    W, bW = load_w(C, sc, "wo0", C.w_out0, 8, D)
    grow, bg = load_grow(C, sc, 1)
    nb = NormBufs(sc)
    oT = [sc.sb("oTb", [128, 8, 512], BF16) for _ in range(2)]
    boT = bufs(2)
    xt = [sc.sb("xt", [128, D], F32) for _ in range(2)]
    bxt = bufs(2)
    x1 = [sc.sb("x1", [128, D], F32) for _ in range(2)]
    bx1 = bufs(2)
    hb = [sc.sb("hb", [128, D], BF16) for _ in range(2)]
    bhb = bufs(2)
    hT = [sc.sb("hT", [128, 8, 128], BF16) for _ in range(2)]
    bhT = bufs(2)
    pi = 0
    for s in range(C.nseq):
        for tb in range(NB):
            o, bo = oT[tb % 2], boT[tb % 2]
            for k in range(0, 8, 2):
                S.dma("sp", o[:, k:k + 2, :], C.OT0[s, k:k + 2, :, tb * 512:(tb + 1) * 512].rearrange("k p t -> p k t"), writes=[bo])
            for j in range(4):
                tt = tb * 4 + j
                r0 = s * T + tt * 128
                i2 = tt % 2
                S.dma("sp", xt[i2][:], C.x[r0:r0 + 128, :], writes=[bxt[i2]])
                for c in range(2):
                    p = pi % 3
                    pi += 1
                    for k in range(8):
                        S.op("pe", lambda e: e.matmul(C.ps[p][:], lhsT=o[:, k, j * 128:(j + 1) * 128], rhs=W[:, k, c * 512:(c + 1) * 512], start=(k == 0), stop=(k == 7)),
                             reads=[bW, bo], writes=[C.bps[p]])
                    S.op("dve", lambda e: e.tensor_tensor(out=x1[i2][:, c * 512:(c + 1) * 512], in0=C.ps[p][:], in1=xt[i2][:, c * 512:(c + 1) * 512], op=ALU.add),
                         reads=[C.bps[p], bxt[i2]], writes=[bx1[i2]])
                S.dma("sp", C.X1[r0:r0 + 128, :], x1[i2][:], reads=[bx1[i2]])
                rmsnorm(C, nb, x1[i2][:], bx1[i2], grow, bg, hb[i2][:], bhb[i2])
                transpose_to(C, hb[i2], bhb[i2], 8, lambda a, b: hT[i2][:, a:b, :], bhT[i2], "act")
                S.dma("sp", C.HTa[s, :, :, tt * 128:(tt + 1) * 128].rearrange("k p t -> p k t"), hT[i2][:], reads=[bhT[i2]])
    S.barrier()
    sc.close()


def p4_ffn(C, layer):
    nc, S = C.nc, C.S
    sc = Scope(nc)
    alloc_psum(C, sc, 7, 1)
    HTin = C.HTa if layer == 0 else C.HTb
    Xin = C.X1 if layer == 0 else C.X3
    Wu, bWu = load_w(C, sc, "wu", C.w_up[layer], 8, 2 * FF)
    Wd, bWd = load_w(C, sc, "wd", C.w_dn[layer], 22, D)
    cw = sc.sb("cw", [128, 44, 4], F32)
    bcw = Buf()
    S.dma("sp", cw[:], C.cwT[layer].rearrange("p (t c) -> p t c", c=4), writes=[bcw])
    grow, bg = load_grow(C, sc, 2 if layer == 0 else 4)
    nb = NormBufs(sc, 1)
    hT = sc.sb("h2T", [128, 8, 514], BF16)
    bh = Buf()
    gT = sc.sb("gT", [128, 22, 512], BF16)
    bgT = bufs(22)
    ah = sc.sb("ahalo", [128, 2, 44], F32)
    bah = Buf()
    yus = [sc.sb("yu", [128, 512], F32) for _ in range(2)]
    ygs = [sc.sb("yg", [128, 512], F32) for _ in range(2)]
    ggs = [sc.sb("gg", [128, 512], F32) for _ in range(2)]
    byus, bygs, bggs = bufs(2), bufs(2), bufs(2)
    xt = sc.sb("xt", [128, D], F32)
    bxt = Buf()
    x2 = sc.sb("x2", [128, D], F32)
    bx2 = Buf()
    if layer == 0:
        hbs = [sc.sb("hb", [128, D], BF16) for _ in range(2)]
        h3T = sc.sb("h3T", [128, 8, 128], BF16)
        bhbs, bh3 = bufs(2), Buf()
    else:
        yo = sc.sb("yo", [128, D], F32)
        byo = Buf()
    S.op("pool", lambda e: e.memset(hT[:], 0.0), writes=[bh])
    pi = 0
    deferred = []
    prev_s2 = [None]
    for s in range(C.nseq):
        for tb in range(NB):
            t0 = tb * 512
            lo = max(t0 - 1, 0)
            hi = min(t0 + 513, T)
            c0 = lo - (t0 - 1)
            if tb == 0:
                S.op("pool", lambda e: e.memset(hT[:, :, 0:1], 0.0), writes=[bh])
            if tb == NB - 1:
                S.op("pool", lambda e: e.memset(hT[:, :, 513:514], 0.0), writes=[bh])
            for k in range(0, 8, 2):
                S.dma("sp", hT[:, k:k + 2, c0:c0 + (hi - lo)], HTin[s, k:k + 2, :, lo:hi].rearrange("k p t -> p k t"), writes=[bh])
            hal = hT[:, :, 0:514:513]
            for f0 in range(0, 44, 11):
                p = 6
                for f in range(f0, f0 + 11):
                    for k in range(8):
                        S.op("pe", lambda e: e.matmul(C.ps[p][:, (f - f0) * 2:(f - f0) * 2 + 2], lhsT=Wu[:, k, f * 128:(f + 1) * 128], rhs=hal[:, k, :],
                                                      start=(k == 0), stop=(k == 7)), reads=[bWu, bh], writes=[C.bps[p]])
                pv = C.ps[p][:, 0:22].rearrange("p (f c) -> p c f", c=2)
                S.op("dve", lambda e: e.tensor_tensor(out=ah[:, 0, f0:f0 + 11], in0=pv[:, 0, :], in1=cw[:, f0:f0 + 11, 0], op=ALU.mult), reads=[C.bps[p], bcw], writes=[bah])
                S.op("dve", lambda e: e.tensor_tensor(out=ah[:, 1, f0:f0 + 11], in0=pv[:, 1, :], in1=cw[:, f0:f0 + 11, 2], op=ALU.mult), reads=[C.bps[p], bcw], writes=[bah])
            for f in range(22):
                yu, yg, gg = yus[f % 2], ygs[f % 2], ggs[f % 2]
                byu, byg, bgg = byus[f % 2], bygs[f % 2], bggs[f % 2]
                for (ft, yt, byt) in ((f, yu, byu), (22 + f, yg, byg)):
                    p = pi % 4
                    pi += 1
                    for k in range(8):
                        S.op("pe", lambda e: e.matmul(C.ps[p][:], lhsT=Wu[:, k, ft * 128:(ft + 1) * 128], rhs=hT[:, k, 1:513], start=(k == 0), stop=(k == 7)),
                             reads=[bWu, bh], writes=[C.bps[p]])
                    A = C.ps[p]
                    S.op("act", lambda e: e.activation(out=yt[:], in_=A[:], func=AF.Identity, scale=cw[:, ft, 1:2], bias=cw[:, ft, 3:4]), reads=[C.bps[p], bcw], writes=[byt])
                    S.op("dve", lambda e: e.scalar_tensor_tensor(out=yt[:, 1:512], in0=A[:, 0:511], scalar=cw[:, ft, 0:1], in1=yt[:, 1:512], op0=ALU.mult, op1=ALU.add),
                         reads=[C.bps[p], bcw, byt], writes=[byt])
                    S.op("dve", lambda e: e.scalar_tensor_tensor(out=yt[:, 0:511], in0=A[:, 1:512], scalar=cw[:, ft, 2:3], in1=yt[:, 0:511], op0=ALU.mult, op1=ALU.add),
                         reads=[C.bps[p], bcw, byt], writes=[byt])
                    S.op("pool", lambda e: e.tensor_tensor(out=yt[:, 0:1], in0=yt[:, 0:1], in1=ah[:, 0, ft:ft + 1], op=ALU.add), reads=[byt, bah], writes=[byt])
                    S.op("pool", lambda e: e.tensor_tensor(out=yt[:, 511:512], in0=yt[:, 511:512], in1=ah[:, 1, ft:ft + 1], op=ALU.add), reads=[byt, bah], writes=[byt])
                def s2(f=f, yu=yu, yg=yg, gg=gg, byu=byu, byg=byg, bgg=bgg):
                    S.op("act", lambda e: e.activation(out=gg[:], in_=yg[:], func=AF.Gelu_apprx_tanh), reads=[byg], writes=[bgg])
                    S.op("pool", lambda e: e.tensor_tensor(out=gT[:, f, :], in0=yu[:], in1=gg[:], op=ALU.mult), reads=[byu, bgg], writes=[bgT[f]])
                if prev_s2[0] is not None:
                    prev_s2[0]()
                prev_s2[0] = s2
            prev_s2[0]()
            prev_s2[0] = None
            for j in range(4):
                tt = tb * 4 + j
                r0 = s * T + tt * 128
                S.dma("sp", xt[:], Xin[r0:r0 + 128, :], writes=[bxt])
                for c in range(2):
                    p = 4 + (pi % 2)
                    pi += 1
                    for f in range(22):
                        S.op("pe", lambda e: e.matmul(C.ps[p][:], lhsT=gT[:, f, j * 128:(j + 1) * 128], rhs=Wd[:, f, c * 512:(c + 1) * 512], start=(f == 0), stop=(f == 21)),
                             reads=[bWd, bgT[f]], writes=[C.bps[p]])
                    S.op("dve", lambda e: e.tensor_tensor(out=x2[:, c * 512:(c + 1) * 512], in0=C.ps[p][:], in1=xt[:, c * 512:(c + 1) * 512], op=ALU.add),
                         reads=[C.bps[p], bxt], writes=[bx2])
                if layer == 0:
                    S.dma("sp", C.X2[r0:r0 + 128, :], x2[:], reads=[bx2])
                    hb, bhb = hbs[tt % 2], bhbs[tt % 2]
                    rmsnorm(C, nb, x2[:], bx2, grow, bg, hb[:], bhb)

                    def fin(hb=hb, bhb=bhb, tt=tt, s=s):
                        transpose_to(C, hb, bhb, 8, lambda a, b: h3T[:, a:b, :], bh3, "act")
                        S.dma("sp", C.HTb[s, :, :, tt * 128:(tt + 1) * 128].rearrange("k p t -> p k t"), h3T[:], reads=[bh3])
                    deferred.append(fin)
                    if len(deferred) > 1:
                        deferred.pop(0)()
                else:
                    rmsnorm(C, nb, x2[:], bx2, grow, bg, yo[:], byo)
                    S.dma("sp", C.y[r0:r0 + 128, :], yo[:], reads=[byo])
            while deferred:
                deferred.pop(0)()
    S.barrier()
    sc.close()


def p5_inproj1(C):
    nc, S = C.nc, C.S
    sc = Scope(nc)
    alloc_psum(C, sc, 6, 2)
    W, bW = load_w(C, sc, "w1", C.w_in1, 8, 6144)
    hT = [sc.sb("h3T", [128, 8, 512], BF16) for _ in range(2)]
    bh = bufs(2)
    cs = [sc.sb("cs1", [128, 2, 512], F32) for _ in range(2)]
    bcs = bufs(2)
    tm = [sc.sb("tm", [128, 512], F32) for _ in range(4)]
    btm = bufs(4)
    ob = [sc.sb("ob", [128, 512], BF16) for _ in range(4)]
    bob = bufs(4)
    kr = [sc.sb("kr", [128, 2, 512], BF16) for _ in range(2)]
    bkr = bufs(2)
    ktm = [sc.sb("ktm", [128, 256], BF16) for _ in range(2)]
    bktm = bufs(2)
    oi = 0
    pi = 0
    ki = 0
    for s in range(C.nseq):
        for tb in range(NB):
            h, bhc = hT[tb % 2], bh[tb % 2]
            csc, bcsc = cs[tb % 2], bcs[tb % 2]
            for k in range(0, 8, 2):
                S.dma("sp", h[:, k:k + 2, :], C.HTb[s, k:k + 2, :, tb * 512:(tb + 1) * 512].rearrange("k p t -> p k t"), writes=[bhc])
            S.dma("sp", csc[:, 0, :], C.cos1[:, tb * 512:(tb + 1) * 512], writes=[bcsc])
            S.dma("sp", csc[:, 1, :], C.sin1[:, tb * 512:(tb + 1) * 512], writes=[bcsc])
            for qk in range(2):
                for hd in range(4):
                    pa, pb = pi % 4, (pi + 1) % 4
                    pi += 2
                    for (p, c0) in ((pa, qk * 1024 + hd * 256), (pb, qk * 1024 + hd * 256 + 128)):
                        for k in range(8):
                            S.op("pe", lambda e: e.matmul(C.ps[p][:], lhsT=W[:, k, c0:c0 + 128], rhs=h[:, k, :], start=(k == 0), stop=(k == 7)),
                                 reads=[bW, bhc], writes=[C.bps[p]])
                    outs = []
                    for half in range(2):
                        ta, bta = tm[(oi * 2) % 4], btm[(oi * 2) % 4]
                        tb_, btb = tm[(oi * 2 + 1) % 4], btm[(oi * 2 + 1) % 4]
                        o, bo = ob[oi % 4], bob[oi % 4]
                        oi += 1
                        S.op("dve", lambda e: e.tensor_tensor(out=ta[:], in0=C.ps[pa][:], in1=csc[:, half, :], op=ALU.mult), reads=[C.bps[pa], bcsc], writes=[bta])
                        S.op("dve", lambda e: e.tensor_tensor(out=tb_[:], in0=C.ps[pb][:], in1=csc[:, 1 - half, :], op=ALU.mult), reads=[C.bps[pb], bcsc], writes=[btb])
                        if qk == 0:
                            S.op("pool", lambda e: e.tensor_tensor(out=o[:], in0=ta[:], in1=tb_[:], op=(ALU.subtract if half == 0 else ALU.add)), reads=[bta, btb], writes=[bo])
                            S.dma("sp", C.QTR[s, hd * 2 + half, :, tb * 512:(tb + 1) * 512], o[:], reads=[bo])
                        else:
                            kk, bkk = kr[ki % 2], bkr[ki % 2]
                            S.op("pool", lambda e: e.tensor_tensor(out=kk[:, half, :], in0=ta[:], in1=tb_[:], op=(ALU.subtract if half == 0 else ALU.add)), reads=[bta, btb], writes=[bkk])
                            S.dma("sp", C.KTR[s, hd * 2 + half, :, tb * 512:(tb + 1) * 512], kk[:, half, :], reads=[bkk])
                    if qk == 1:
                        kk, bkk = kr[ki % 2], bkr[ki % 2]
                        ki += 1
                        for j in range(4):
                            r0 = s * T + (tb * 4 + j) * 128
                            kt_, bkt_ = ktm[j % 2], bktm[j % 2]
                            hh = C.pti % C.npt
                            C.pti += 1
                            for half in range(2):
                                S.op("pe", lambda e: e.transpose(out=C.pt[hh][:, half, :], in_=kk[:, half, j * 128:(j + 1) * 128], identity=C.idt[:]),
                                     reads=[bkk, C.bidt], writes=[C.bpt[hh]])
                            S.op("act", lambda e: e.copy(out=kt_[:].rearrange("p (a b) -> p a b", a=2), in_=C.pt[hh][:, 0:2, :]), reads=[C.bpt[hh]], writes=[bkt_])
                            S.dma("sp", C.KTM[r0:r0 + 128, hd * 256:(hd + 1) * 256], kt_[:], reads=[bkt_])
            for j in range(4):
                r0 = s * T + (tb * 4 + j) * 128
                for c in range(8):
                    p = pi % 4
                    pi += 1
                    c0 = 2048 + c * 512
                    for k in range(8):
                        S.op("pe", lambda e: e.matmul(C.ps[p][:], lhsT=h[:, k, j * 128:(j + 1) * 128], rhs=W[:, k, c0:c0 + 512], start=(k == 0), stop=(k == 7)),
                             reads=[bW, bhc], writes=[C.bps[p]])
                    o, bo = ob[oi % 4], bob[oi % 4]
                    oi += 1
                    if c < 4:
                        S.op("dve", lambda e: e.tensor_copy(out=o[:], in_=C.ps[p][:]), reads=[C.bps[p]], writes=[bo])
                        S.dma("sp", C.VR[r0:r0 + 128, c * 512:(c + 1) * 512], o[:], reads=[bo])
                    else:
                        S.op("act", lambda e: e.activation(out=o[:], in_=C.ps[p][:], func=AF.Silu), reads=[C.bps[p]], writes=[bo])
                        S.dma("sp", C.SG[r0:r0 + 128, (c - 4) * 512:(c - 3) * 512], o[:], reads=[bo])
    S.barrier()
    sc.close()


def p6_retention(C):
    nc, S = C.nc, C.S
    sc = Scope(nc)
    alloc_psum(C, sc, 0, 1)
    pin = sc.psum("pin", [128, 512], F32)
    bpin = Buf()
    pout = [sc.psum("pout", [128, 512], F32) for _ in range(2)]
    bpout = bufs(2)
    pst = [sc.psum("pst", [128, 2, 512], F32) for _ in range(2)]
    bpst = bufs(2)
    Wo, bWo = load_w(C, sc, "wo1", C.w_out1, 16, D)
    grow, bg = load_grow(C, sc, 3)
    nb = NormBufs(sc, 1)
    lg = sc.sb("lg", [128, 8], F32)
    DT = sc.sb("DT", [128, 8, 128], F32)
    KD = sc.sb("KD", [128, 8], F32)
    SD = sc.sb("SD", [128, 8], F32)
    QDT = sc.sb("QDT", [128, 2, 8, 128], F32)
    KDT = sc.sb("KDT", [128, 2, 1024], F32)
    sc2 = Scope(nc)
    rc = sc2.sb("retc", [128, 7, 128], F32)
    QD = sc2.sb("QD", [128, 8, 128], F32)
    onesf = sc2.sb("onesf", [128, 256], F32)
    brc = Buf()
    S.dma("sp", rc[:], C.retc[0:7].rearrange("c p q -> p c q"), writes=[brc])
    blg = Buf()
    S.dma("sp", lg[:], C.decay.partition_broadcast(128), writes=[blg])
    S.op("act", lambda e: e.activation(out=lg[:], in_=lg[:], func=AF.Exp), reads=[blg], writes=[blg])
    S.op("act", lambda e: e.activation(out=lg[:], in_=lg[:], func=AF.Ln, bias=1.0), reads=[blg], writes=[blg])
    S.op("dve", lambda e: e.tensor_scalar(out=lg[:], in0=lg[:], scalar1=-1.0, scalar2=None, op0=ALU.mult), reads=[blg], writes=[blg])
    bDT, bQD, bKD, bSD = Buf(), Buf(), Buf(), Buf()
    for d in range(2):
        for hd in range(4):
            i = d * 4 + hd
            S.op("act", lambda e: e.activation(out=DT[:, i, :], in_=rc[:, 2 * d, :], func=AF.Exp, scale=lg[:, i:i + 1]), reads=[brc, blg], writes=[bDT])
            S.op("dve", lambda e: e.tensor_tensor(out=DT[:, i, :], in0=DT[:, i, :], in1=rc[:, 2 * d + 1, :], op=ALU.mult), reads=[brc, bDT], writes=[bDT])
            S.op("act", lambda e: e.activation(out=QD[:, i, :], in_=rc[:, 4 + d, :], func=AF.Exp, scale=lg[:, i:i + 1]), reads=[brc, blg], writes=[bQD])
            S.op("dve", lambda e: e.tensor_scalar(out=QD[:, i, :], in0=QD[:, i, :], scalar1=1.0 / 16, scalar2=None, op0=ALU.mult), reads=[bQD], writes=[bQD])
            S.op("act", lambda e: e.activation(out=KD[:, i:i + 1], in_=rc[:, 6, d:d + 1], func=AF.Exp, scale=lg[:, i:i + 1]), reads=[brc, blg], writes=[bKD])
            S.op("act", lambda e: e.activation(out=SD[:, i:i + 1], in_=rc[:, 6, 2:3], func=AF.Exp, scale=lg[:, i:i + 1]), reads=[brc, blg], writes=[bSD])

    bQDT, bKDT, bof = Buf(), Buf(), Buf()
    S.op("pool", lambda e: e.memset(onesf[:], 1.0), writes=[bof])
    for d in range(2):
        for hd in range(4):
            i = d * 4 + hd
            for t in range(2):
                S.op("pool", lambda e: e.tensor_copy(out=QDT[:, d, hd * 2 + t, :], in_=QD[:, i, :]), reads=[bQD], writes=[bQDT])
            S.op("dve", lambda e: e.tensor_scalar(out=KDT[:, d, hd * 256:(hd + 1) * 256], in0=onesf[:], scalar1=KD[:, i:i + 1], scalar2=None, op0=ALU.mult),
                 reads=[bof, bKD], writes=[bKDT])

    S.barrier()
    sc2.close()

    S32 = sc.sb("S32", [128, 4, 2, 512], F32)
    Sbf = sc.sb("Sbf", [128, 4, 2, 512], BF16)
    bS32, bSbf = bufs(4), bufs(4)
    qT = [sc.sb("qT", [128, 8, 128], BF16) for _ in range(4)]
    kT = [sc.sb("kT", [128, 8, 128], BF16) for _ in range(4)]
    kM = [sc.sb("kM", [128, D], BF16) for _ in range(4)]
    vM = [sc.sb("vM", [128, 2048], BF16) for _ in range(4)]
    bq, bk, bkm, bv = bufs(4), bufs(4), bufs(4), bufs(4)
    iT = [sc.sb("iT", [128, 4, 128], BF16) for _ in range(4)]
    biT = bufs(4)
    qd = [sc.sb("qd", [128, 8, 128], BF16) for _ in range(4)]
    bqd = bufs(4)
    kd = [sc.sb("kd", [128, 1024], BF16) for _ in range(4)]
    bkd = bufs(4)
    rf = [sc.sb("rf", [128, 512], F32) for _ in range(4)]
    brf = bufs(4)
    rfl = [sc.sb("rfl", [128, 2048], F32) for _ in range(2)]
    brfl = bufs(2)
    sgl = [sc.sb("sgl", [128, 2048], BF16) for _ in range(2)]
    bsgl = bufs(2)
    rr = [sc.sb("rr", [128, 512], F32) for _ in range(2)]
    brr = bufs(2)
    st = [sc.sb("st", [128, 8], F32) for _ in range(2)]
    bst = bufs(2)
    rg = sc.sb("rg", [128, 2048], BF16)
    brg = bufs(4)
    rgT = sc.sb("rgT", [128, 16, 128], BF16)
    brgT = Buf()
    xt = [sc.sb("xt", [128, D], F32) for _ in range(2)]
    bxt = bufs(2)
    x3 = sc.sb("x3", [128, D], F32)
    bx3 = Buf()
    hb = sc.sb("hb", [128, D], BF16)
    bhb = Buf()
    h4T = sc.sb("h4T", [128, 8, 128], BF16)
    bh4 = Buf()
    cnt = dict(r=0, o=0)

    def mk_chunk(s, d, n, c, ci):
        r0 = s * T + c * 128
        j = ci % 4
        q_, k_, km_, v_ = qT[j], kT[j], kM[j], vM[j]

        def a():
            S.dma("sp", q_[:], C.QTR[s, :, :, c * 128:(c + 1) * 128].rearrange("k p t -> p k t"), writes=[bq[j]])
            S.dma("sp", k_[:], C.KTR[s, :, :, c * 128:(c + 1) * 128].rearrange("k p t -> p k t"), writes=[bk[j]])
            S.dma("sp", km_[:], C.KTM[r0:r0 + 128, :], writes=[bkm[j]])
            S.dma("sp", v_[:], C.VR[r0:r0 + 128, :], writes=[bv[j]])

        def ac():
            for hd in range(4):
                for i in range(2):
                    S.op("pe", lambda e: e.matmul(pin[:, hd * 128:(hd + 1) * 128], lhsT=k_[:, hd * 2 + i, :], rhs=q_[:, hd * 2 + i, :], start=(i == 0), stop=(i == 1)),
                         reads=[bk[j], bq[j]], writes=[bpin])
            S.op("dve", lambda e: e.tensor_tensor(out=iT[j][:], in0=pin[:].rearrange("p (h q) -> p h q", h=4), in1=DT[:, d * 4:(d + 1) * 4, :], op=ALU.mult),
                 reads=[bpin, bDT], writes=[biT[j]])
            if n > 0:
                S.op("pool", lambda e: e.tensor_tensor(out=qd[j][:], in0=q_[:], in1=QDT[:, d, :, :], op=ALU.mult), reads=[bq[j], bQDT], writes=[bqd[j]])
            S.op("pool", lambda e: e.tensor_tensor(out=kd[j][:], in0=km_[:], in1=KDT[:, d, :], op=ALU.mult), reads=[bkm[j], bKDT], writes=[bkd[j]])

        j2 = ci % 2

        def a2():
            if d == 1:
                S.dma("sp", rfl[j2][:], C.RF[r0:r0 + 128, :], writes=[brfl[j2]])
                S.dma("sp", sgl[j2][:], C.SG[r0:r0 + 128, :], writes=[bsgl[j2]])
                S.dma("sp", xt[j2][:], C.X2[r0:r0 + 128, :], writes=[bxt[j2]])

        def b():
            for hd in range(4):
                di = d * 4 + hd
                o_ = cnt["o"] % 2
                cnt["o"] += 1
                po, bpo = pout[o_], bpout[o_]
                ps_, bps_ = pst[o_], bpst[o_]
                vh = v_[:, hd * 512:(hd + 1) * 512]
                S.op("pe", lambda e: e.matmul(po[:], lhsT=iT[j][:, hd, :], rhs=vh, start=True, stop=(n == 0)), reads=[biT[j], bv[j]], writes=[bpo])
                if n > 0:
                    for i in range(2):
                        S.op("pe", lambda e: e.matmul(po[:], lhsT=qd[j][:, hd * 2 + i, :], rhs=Sbf[:, hd, i, :], start=False, stop=(i == 1)),
                             reads=[bqd[j], bSbf[hd]], writes=[bpo])
                rq = cnt["r"] % 4
                ri = cnt["r"] % 2
                cnt["r"] += 1
                if d == 0:
                    S.op("act", lambda e: e.copy(out=rf[rq][:], in_=po[:]), reads=[bpo], writes=[brf[rq]])
                    S.dma("sp", C.RF[r0:r0 + 128, hd * 512:(hd + 1) * 512], rf[rq][:], reads=[brf[rq]])
                else:
                    r_, br_ = rr[ri], brr[ri]
                    s_, bs_ = st[ri], bst[ri]
                    S.op("dve", lambda e: e.tensor_tensor(out=r_[:], in0=po[:], in1=rfl[j2][:, hd * 512:(hd + 1) * 512], op=ALU.add), reads=[bpo, brfl[j2]], writes=[br_])
                for i in range(2):
                    S.op("pe", lambda e: e.matmul(ps_[:, i, :], lhsT=kd[j][:, hd * 256 + i * 128:hd * 256 + (i + 1) * 128], rhs=vh, start=True, stop=True),
                         reads=[bkd[j], bv[j]], writes=[bps_])
                if n == 0:
                    S.op("dve", lambda e: e.tensor_copy(out=S32[:, hd, :, :], in_=ps_[:]), reads=[bps_], writes=[bS32[hd]])
                else:
                    S.op("dve", lambda e: e.scalar_tensor_tensor(out=S32[:, hd, :, :], in0=S32[:, hd, :, :], scalar=SD[:, di:di + 1], in1=ps_[:], op0=ALU.mult, op1=ALU.add),
                         reads=[bps_, bS32[hd], bSD], writes=[bS32[hd]])
                S.op("act", lambda e: e.copy(out=Sbf[:, hd, :, :], in_=S32[:, hd, :, :]), reads=[bS32[hd]], writes=[bSbf[hd]])
                if d == 1:
                    S.op("dve", lambda e: e.bn_stats(out=s_[:, 0:6], in_=r_[:]), reads=[br_], writes=[bs_])
                    S.op("dve", lambda e: e.bn_aggr(out=s_[:, 6:8], in_=s_[:, 0:6]), reads=[bs_], writes=[bs_])
                    S.op("dve", lambda e: e.tensor_scalar(out=s_[:, 7:8], in0=s_[:, 7:8], scalar1=EPS, scalar2=None, op0=ALU.add), reads=[bs_], writes=[bs_])
                    S.op("act", lambda e: e.activation(out=s_[:, 7:8], in_=s_[:, 7:8], func=AF.Sqrt), reads=[bs_], writes=[bs_])
                    S.op("dve", lambda e: e.reciprocal(out=s_[:, 7:8], in_=s_[:, 7:8]), reads=[bs_], writes=[bs_])
                    S.op("dve", lambda e: e.tensor_scalar(out=r_[:], in0=r_[:], scalar1=s_[:, 6:7], scalar2=s_[:, 7:8], op0=ALU.subtract, op1=ALU.mult), reads=[br_, bs_], writes=[br_])
                    S.op("pool", lambda e: e.tensor_tensor(out=rg[:, hd * 512:(hd + 1) * 512], in0=r_[:], in1=sgl[j2][:, hd * 512:(hd + 1) * 512], op=ALU.mult),
                         reads=[br_, bsgl[j2]], writes=[brg[hd]])
            if d == 1:
                for hd in range(4):
                    transpose_to(C, rg[:, hd * 512:(hd + 1) * 512], brg[hd], 4, lambda a_, b_: rgT[:, hd * 4 + a_:hd * 4 + b_, :], brgT, "act")
                for c2 in range(2):
                    o_ = cnt["o"] % 2
                    cnt["o"] += 1
                    po, bpo = pout[o_], bpout[o_]
                    for k in range(16):
                        S.op("pe", lambda e: e.matmul(po[:], lhsT=rgT[:, k, :], rhs=Wo[:, k, c2 * 512:(c2 + 1) * 512], start=(k == 0), stop=(k == 15)),
                             reads=[bWo, brgT], writes=[bpo])
                    S.op("dve", lambda e: e.tensor_tensor(out=x3[:, c2 * 512:(c2 + 1) * 512], in0=po[:], in1=xt[j2][:, c2 * 512:(c2 + 1) * 512], op=ALU.add),
                         reads=[bpo, bxt[j2]], writes=[bx3])
                S.dma("sp", C.X3[r0:r0 + 128, :], x3[:], reads=[bx3])
                rmsnorm(C, nb, x3[:], bx3, grow, bg, hb[:], bhb)
                transpose_to(C, hb, bhb, 8, lambda a_, b_: h4T[:, a_:b_, :], bh4, "act")
                S.dma("sp", C.HTb[s, :, :, c * 128:(c + 1) * 128].rearrange("k p t -> p k t"), h4T[:], reads=[bh4])
        return a, ac, a2, b

    ci = 0
    for s in range(C.nseq):
        for d in range(2):
            chunks = list(range(NT)) if d == 0 else list(range(NT - 1, -1, -1))
            st_ = []
            for n, c in enumerate(chunks):
                st_.append(mk_chunk(s, d, n, c, ci))
                ci += 1
            for k in range(NT + 4):
                if k >= 4:
                    st_[k - 4][3]()
                if k < NT:
                    st_[k][0]()
                if 2 <= k < NT + 2:
                    st_[k - 2][1]()
                if 3 <= k < NT + 3:
                    st_[k - 3][2]()
            S.barrier()
    S.barrier()
    sc.close()


def _prep_shared(inp):
    f = np.float32
    w_in = np.asarray(inp["even_w_in"], f)[0]
    def swap(cols):
        c = cols.reshape(D, 8, 2, 32)
        return c[:, :, ::-1, :].reshape(D, 512)
    dq = w_in[:, 1536:2048]
    dk = w_in[:, 2048:2560]
    w_in0 = np.ascontiguousarray(np.concatenate([w_in, swap(dq), swap(dk)], axis=1))
    cw = np.asarray(inp["ffn_conv_w"], f)
    cb = np.asarray(inp["ffn_conv_b"], f)
    cwT = []
    for l in range(2):
        a = np.concatenate([cw[l], cb[l][None]], 0)
        a = a.reshape(4, 44, 128).transpose(2, 1, 0)
        cwT.append(np.ascontiguousarray(a.reshape(128, 44 * 4)))
    norms = np.ascontiguousarray(np.concatenate([
        np.asarray(inp["attn_norm"], f)[0:1], np.asarray(inp["ffn_norm"], f)[0:1],
        np.asarray(inp["attn_norm"], f)[1:2], np.asarray(inp["ffn_norm"], f)[1:2],
        np.asarray(inp["final_norm"], f)[None]], 0))
    rpb = np.asarray(inp["na_rpb"], f)[0]
    rpbT = np.ascontiguousarray(rpb[:, ::-1, :].transpose(2, 0, 1).reshape(31, 8 * 15))
    decay = np.ascontiguousarray(np.concatenate([np.asarray(inp["ret_decay_fwd"], f)[0], np.asarray(inp["ret_decay_bwd"], f)[0]]))
    sh = dict(
        w_in0=w_in0, w_out0=np.ascontiguousarray(np.asarray(inp["even_w_out"], f)[0]),
        w_up0=np.ascontiguousarray(np.asarray(inp["ffn_w_up"], f)[0]), w_up1=np.ascontiguousarray(np.asarray(inp["ffn_w_up"], f)[1]),
        w_dn0=np.ascontiguousarray(np.asarray(inp["ffn_w_down"], f)[0]), w_dn1=np.ascontiguousarray(np.asarray(inp["ffn_w_down"], f)[1]),
        cwT0=cwT[0], cwT1=cwT[1],
        w_in1=np.ascontiguousarray(np.asarray(inp["ret_w_in"], f)[0]), w_out1=np.ascontiguousarray(np.asarray(inp["ret_w_out"], f)[0]),
        norms=norms, rpbT=rpbT, decay=decay,
    )
    sh.update(_consts())
    return sh


def _assign():
    seqs = [("p", i) for i in range(BATCH)] + [("s", i) for i in range(DEC_BATCH)]
    slots = [[] for _ in range(NCORES)]
    for i, sq in enumerate(seqs):
        slots[i % NCORES].append(sq)
    return slots


def kernel(**inputs):
    nseq = 3
    sh = _prep_shared(inputs)
    xp = np.asarray(inputs["x_prompt"], np.float32)
    xs = np.asarray(inputs["x_sample"], np.float32)
    slots = _assign()
    in_maps = []
    for c in range(NCORES):
        xc = np.zeros((nseq * T, D), np.float32)
        for j, (kind, i) in enumerate(slots[c]):
            xc[j * T:(j + 1) * T] = xp[i] if kind == "p" else xs[i]
        for j in range(len(slots[c]), nseq):
            xc[j * T:(j + 1) * T] = xc[0:T]
        m = dict(sh)
        m["x"] = xc
        in_maps.append(m)
    nc, _ = build(nseq)
    res = run_bass_kernel_spmd(nc, in_maps, core_ids=list(range(NCORES)))
    yp = np.zeros((BATCH, T, D), np.float32)
    ys = np.zeros((DEC_BATCH, T, D), np.float32)
    for c in range(NCORES):
        yc = np.asarray(res.results[c]["y"]).reshape(nseq, T, D)
        for j, (kind, i) in enumerate(slots[c]):
            if kind == "p":
                yp[i] = yc[j]
            else:
                ys[i] = yc[j]
    return (yp, ys)
```

```python
import math
from contextlib import ExitStack

import numpy as np
import ml_dtypes
import concourse.bass as bass
import concourse.mybir as mybir
from concourse.bass_utils import run_bass_kernel_spmd

F32 = mybir.dt.float32
BF16 = mybir.dt.bfloat16
AF = mybir.ActivationFunctionType
ALU = mybir.AluOpType

T = 4096
D = 1024
NT = 32
NB = 8
FF = 2816
NCORES = 8
EPS = 1e-6
BATCH, DEC_BATCH = 16, 4


class Buf:
    __slots__ = ("w", "r", "ws")

    def __init__(self):
        self.w = None
        self.r = {}
        self.ws = []


def bufs(n):
    return [Buf() for _ in range(n)]


class Sched:
    LIMIT = 30000

    def __init__(self, nc, n_dma_sems=40):
        self.nc = nc
        self.eng = dict(pe=nc.tensor, act=nc.scalar, dve=nc.vector, pool=nc.gpsimd, sp=nc.sync)
        self.csem, self.ccnt, self.nsem = {}, {}, 0
        for e in self.eng:
            self._newsem(e)
        self.known = {e: {} for e in self.eng}
        self.dsems, self.dcnt, self.dnext = {}, {}, {}
        for e, n in (("sp", n_dma_sems), ("pool", 12), ("act", 8)):
            self.dsems[e] = [nc.alloc_semaphore(f"dq_{e}{i}") for i in range(n)]
            self.dcnt[e] = [0] * n
            self.dnext[e] = 0
        self.ninstr = 0
        self.out_tks = []
        import os
        self.cap = int(os.environ.get('KCAP', '1000000000'))
        self.nreal = 0

    def _newsem(self, e):
        self.nsem += 1
        self.csem[e] = self.nc.alloc_semaphore(f"c_{e}_{self.nsem}")
        self.ccnt[e] = 0

    def _wait(self, e, tk):
        if tk is None:
            return
        sem, val = tk
        k = self.known[e]
        if k.get(sem.num, 0) >= val:
            return
        if e == "pe" and sem is self.csem["pe"]:
            return
        self.eng[e].wait_ge(sem, val)
        self.ninstr += 1
        k[sem.num] = val

    def _deps(self, e, reads, writes, join=False):
        for b in reads:
            self._wait(e, b.w)
            for tk in b.ws:
                self._wait(e, tk)
        for b in writes:
            if not join:
                self._wait(e, b.w)
                for tk in b.ws:
                    self._wait(e, tk)
            for tk in list(b.r.values()):
                self._wait(e, tk)

    def _mark(self, tk, reads, writes, join=False):
        sem, val = tk
        for b in reads:
            b.r[sem.num] = tk
        for b in writes:
            if join:
                b.ws.append(tk)
            else:
                b.w = tk
                b.ws = []
            b.r = {}

    def op(self, e, fn, reads=(), writes=()):
        self.nreal += 1
        if self.nreal > self.cap:
            return None
        self._deps(e, reads, writes)
        ins = fn(self.eng[e])
        if self.ccnt[e] >= self.LIMIT:
            self._newsem(e)
        self.ccnt[e] += 1
        sem = self.csem[e]
        ins.then_inc(sem, 1)
        self.ninstr += 1
        tk = (sem, self.ccnt[e])
        self._mark(tk, reads, writes)
        return tk

    def dma(self, e, out, in_, reads=(), writes=(), join=False, **kw):
        self.nreal += 1
        if self.nreal > self.cap:
            return None
        self._deps(e, reads, writes, join)
        i = self.dnext[e]
        self.dnext[e] = (i + 1) % len(self.dsems[e])
        sem = self.dsems[e][i]
        if self.dcnt[e][i] > 0:
            self._wait(e, (sem, self.dcnt[e][i]))
        self.dcnt[e][i] += 16
        self.eng[e].dma_start(out=out, in_=in_, **kw).then_inc(sem, 16)
        self.ninstr += 1
        tk = (sem, self.dcnt[e][i])
        self._mark(tk, reads, writes, join)
        return tk

    def barrier(self):
        for e in self.eng:
            for e2 in self.eng:
                if e2 != e and self.ccnt[e2] > 0:
                    self._wait(e, (self.csem[e2], self.ccnt[e2]))
            for q in self.dsems:
                for i, sem in enumerate(self.dsems[q]):
                    if self.dcnt[q][i] > 0:
                        self._wait(e, (sem, self.dcnt[q][i]))


class Scope:
    cnt = [0]

    def __init__(self, nc):
        self.nc = nc
        self.es = ExitStack()

    def sb(self, name, shape, dt):
        Scope.cnt[0] += 1
        return self.es.enter_context(self.nc.sbuf_tensor(f"{name}_{Scope.cnt[0]}", list(shape), dt))

    def psum(self, name, shape, dt):
        Scope.cnt[0] += 1
        return self.es.enter_context(self.nc.psum_tensor(f"{name}_{Scope.cnt[0]}", list(shape), dt))

    def close(self):
        self.es.close()


def alloc_psum(C, sc, nps, npt):
    C.pt = [sc.psum("pt", [128, 8, 128], BF16) for _ in range(npt)]
    C.bpt = bufs(npt)
    C.npt = npt
    C.pti = 0
    C.ps = [sc.psum("ps", [128, 512], F32) for _ in range(nps)]
    C.bps = bufs(nps)


def _rope_tables(dh, reps):
    half = dh // 2
    inv = (1.0 / (np.float32(10000.0) ** (np.arange(0, dh, 2, dtype=np.float32) / np.float32(dh)))).astype(np.float32)
    ang = np.arange(T, dtype=np.float32)[None, :] * inv[:, None]
    c = np.cos(ang).astype(np.float32)
    s = np.sin(ang).astype(np.float32)
    cos = np.concatenate([c, c], 0)
    sin = np.concatenate([-s, s], 0)
    return np.tile(cos, (reps, 1)).copy(), np.tile(sin, (reps, 1)).copy()


def _dil_masks():
    m = np.zeros((20, 128, 512), np.float32)
    k = np.arange(128)[:, None]
    q = np.arange(512)[None, :]
    for i in range(20):
        d = (i * 128 - 1024) + k - q
        ad = np.abs(d)
        m[i] = (ad <= 64).astype(np.float32) + ((d % 4 == 0) & (ad <= 256)) + ((d % 16 == 0) & (ad <= 1024))
    return m.astype(ml_dtypes.bfloat16)


def _na_onehot():
    L = np.zeros((31, 64, 128), np.float32)
    for cq in range(64):
        cs = min(max(cq - 8, 0), 48)
        for ck in range(cs, cs + 16):
            b = ck - cq + 15
            L[b, cq, ck] = 1.0
            L[b, cq, ck + 64] = 1.0
    return L.reshape(31, 64 * 128)


def _ret_consts():
    k = np.arange(128, dtype=np.float32)[:, None]
    q = np.arange(128, dtype=np.float32)[None, :]
    c = np.zeros((8, 128, 128), np.float32)
    c[0] = np.maximum(q - k, 0)
    c[1] = (q >= k) / 16.0
    c[2] = np.maximum(k - q, 0)
    c[3] = (k > q) / 16.0
    c[4] = np.broadcast_to(q + 1.0, (128, 128))
    c[5] = np.broadcast_to(128.0 - q, (128, 128))
    c[6, :, 0] = 127.0 - k[:, 0]
    c[6, :, 1] = k[:, 0]
    c[6, :, 2] = 128.0
    return c


_CONSTS = {}


def _consts():
    if not _CONSTS:
        cos0, sin0 = _rope_tables(64, 2)
        cos1, sin1 = _rope_tables(256, 1)
        _CONSTS.update(
            ident=np.eye(128, dtype=np.float32).astype(ml_dtypes.bfloat16),
            cos0=cos0, sin0=sin0,
            cos1=np.ascontiguousarray(cos1[:128]), sin1=np.ascontiguousarray(sin1[128:]),
            dmask=_dil_masks(), naL=_na_onehot(), retc=_ret_consts(),
        )
    return _CONSTS


class Ctx:
    pass


def build(nseq, upto=99, debug=False):
    nc = bass.Bass("TRN2", target_bir_lowering=False)
    S = Sched(nc)
    C = Ctx()
    C.nc, C.S, C.nseq = nc, S, nseq
    NTOK = nseq * T

    def din(name, shape, dt=F32):
        return nc.dram_tensor(name, list(shape), dt, kind="ExternalInput").ap()

    def dscr(name, shape, dt, out=False):
        return nc.dram_tensor(name, list(shape), dt, kind="ExternalOutput" if (out or (debug and name in debug)) else "Internal").ap()

    C.x = din("x", [NTOK, D])
    C.w_in0 = din("w_in0", [D, 4096])
    C.w_out0 = din("w_out0", [D, D])
    C.w_up = [din(f"w_up{l}", [D, 2 * FF]) for l in range(2)]
    C.w_dn = [din(f"w_dn{l}", [FF, D]) for l in range(2)]
    C.cwT = [din(f"cwT{l}", [128, 44 * 4]) for l in range(2)]
    C.w_in1 = din("w_in1", [D, 6144])
    C.w_out1 = din("w_out1", [2048, D])
    C.norms = din("norms", [5, D])
    C.rpbT = din("rpbT", [31, 8 * 15])
    C.decay = din("decay", [8])
    C.ident = din("ident", [128, 128], BF16)
    C.cos0 = din("cos0", [128, T])
    C.sin0 = din("sin0", [128, T])
    C.cos1 = din("cos1", [128, T])
    C.sin1 = din("sin1", [128, T])
    C.dmask = din("dmask", [20, 128, 512], BF16)
    C.naL = din("naL", [31, 64 * 128])
    C.retc = din("retc", [8, 128, 128])

    C.y = dscr("y", [NTOK, D], F32, out=True)
    C.QK0 = dscr("QK0", [nseq, 16, 128, T], BF16)
    C.V0 = dscr("V0", [NTOK, D], BF16)
    C.OT0 = dscr("OT0", [nseq, 8, 128, T], BF16)
    C.X1 = dscr("X1", [NTOK, D], F32)
    C.X2 = dscr("X2", [NTOK, D], F32)
    C.X3 = dscr("X3", [NTOK, D], F32)
    C.HTa = dscr("HTa", [nseq, 8, 128, T], BF16)
    C.HTb = dscr("HTb", [nseq, 8, 128, T], BF16)
    C.QTR = dscr("QTR", [nseq, 8, 128, T], BF16)
    C.KTR = dscr("KTR", [nseq, 8, 128, T], BF16)
    C.KTM = dscr("KTM", [NTOK, D], BF16)
    C.VR = dscr("VR", [NTOK, 2048], BF16)
    C.SG = dscr("SG", [NTOK, 2048], BF16)
    C.RF = dscr("RF", [NTOK, 2048], F32)
    C.TAZ = dscr("TAZ", [8, 2, 128, 32 * 64], F32)

    C.idt = nc.alloc_sbuf_tensor("idt", [128, 128], BF16)
    C.bidt = Buf()
    S.dma("sp", C.idt[:], C.ident[:, :], writes=[C.bidt])

    phases = [p1_inproj0, p2_attn, p3_outproj0,
              lambda c: p4_ffn(c, 0), p5_inproj1, p6_retention, lambda c: p4_ffn(c, 1)]
    for i, ph in enumerate(phases):
        if i >= upto:
            break
        ph(C)
        S.barrier()
    S.barrier()
    return nc, S


def load_w(C, sc, name, w_dram, kc, f, eng="pool"):
    t = sc.sb(name, [128, kc, f], BF16)
    b = Buf()
    src = w_dram.rearrange("(k p) f -> p k f", p=128)
    step = max(1, 2048 // f) if f < 2048 else 1
    for k in range(0, kc, step):
        k2 = min(kc, k + step)
        C.S.dma(eng, t[:, k:k2, :], src[:, k:k2, :], writes=[b], join=(k > 0))
    return t, b


def load_grow(C, sc, idx):
    g = sc.sb("grow", [128, D], F32)
    b = Buf()
    C.S.dma("sp", g[:], C.norms[idx, :].partition_broadcast(128), writes=[b])
    return g, b


class NormBufs:
    def __init__(self, sc, n=2):
        self.n = n
        self.junk = [sc.sb("nj", [128, D], BF16) for _ in range(n)]
        self.ss = [sc.sb("nss", [128, 1], F32) for _ in range(n)]
        self.b = [bufs(2) for _ in range(n)]
        self.i = 0


def rmsnorm(C, nb, xt, bx, grow, bg, out, bout):
    S = C.S
    i = nb.i % nb.n
    nb.i += 1
    junk, ss, (bj, bs) = nb.junk[i], nb.ss[i], nb.b[i]
    S.op("act", lambda e: e.activation(out=junk[:], in_=xt, func=AF.Square, accum_out=ss[:]), reads=[bx], writes=[bj, bs])
    S.op("dve", lambda e: e.tensor_scalar(out=ss[:], in0=ss[:], scalar1=1.0 / D, scalar2=EPS, op0=ALU.mult, op1=ALU.add), reads=[bs], writes=[bs])
    S.op("act", lambda e: e.activation(out=ss[:], in_=ss[:], func=AF.Sqrt), reads=[bs], writes=[bs])
    S.op("dve", lambda e: e.reciprocal(out=ss[:], in_=ss[:]), reads=[bs], writes=[bs])
    S.op("dve", lambda e: e.scalar_tensor_tensor(out=out, in0=xt, scalar=ss[:, 0:1], in1=grow[:], op0=ALU.mult, op1=ALU.mult),
         reads=[bx, bs, bg], writes=[bout])


def transpose_to(C, src, bsrc, nk, dst_fn, bdst, evac_eng):
    S = C.S
    for g in range(0, nk, 8):
        h = C.pti % C.npt
        C.pti += 1
        n = min(8, nk - g)
        for k in range(g, g + n):
            S.op("pe", lambda e: e.transpose(out=C.pt[h][:, k - g, :], in_=src[:, k * 128:(k + 1) * 128], identity=C.idt[:]),
                 reads=[bsrc, C.bidt], writes=[C.bpt[h]])
        if evac_eng == "act":
            S.op("act", lambda e: e.copy(out=dst_fn(g, g + n), in_=C.pt[h][:, 0:n, :]), reads=[C.bpt[h]], writes=[bdst])
        else:
            S.op(evac_eng, lambda e: e.tensor_copy(out=dst_fn(g, g + n), in_=C.pt[h][:, 0:n, :]), reads=[C.bpt[h]], writes=[bdst])


def p1_inproj0(C):
    nc, S = C.nc, C.S
    sc = Scope(nc)
    alloc_psum(C, sc, 6, 2)
    W, bW = load_w(C, sc, "w0", C.w_in0, 8, 4096)
    grow, bg = load_grow(C, sc, 0)
    nb = NormBufs(sc)
    xt = [sc.sb("xt", [128, D], F32) for _ in range(2)]
    bxt = bufs(2)
    hb = [sc.sb("hb", [128, D], BF16) for _ in range(2)]
    bhb = bufs(2)
    hT = [sc.sb("hT", [128, 8, 512], BF16) for _ in range(2)]
    bhT = bufs(2)
    cs = [sc.sb("cs", [128, 2, 512], F32) for _ in range(2)]
    bcs = bufs(2)
    t1 = [sc.sb("t1", [128, 512], F32) for _ in range(2)]
    t2 = [sc.sb("t2", [128, 512], F32) for _ in range(2)]
    bt1, bt2 = bufs(2), bufs(2)
    ob = [sc.sb("ob", [128, 512], BF16) for _ in range(4)]
    bob = bufs(4)
    oi = 0
    pi = 0
    for s in range(C.nseq):
        for tb in range(NB):
            hTc, bhTc = hT[tb % 2], bhT[tb % 2]
            csc, bcsc = cs[tb % 2], bcs[tb % 2]
            S.dma("sp", csc[:, 0, :], C.cos0[:, tb * 512:(tb + 1) * 512], writes=[bcsc])
            S.dma("sp", csc[:, 1, :], C.sin0[:, tb * 512:(tb + 1) * 512], writes=[bcsc], join=True)
            for j in range(4):
                tt = tb * 4 + j
                r0 = s * T + tt * 128
                S.dma("sp", xt[tt % 2][:], C.x[r0:r0 + 128, :], writes=[bxt[tt % 2]])
                rmsnorm(C, nb, xt[tt % 2][:], bxt[tt % 2], grow, bg, hb[tt % 2][:], bhb[tt % 2])
                transpose_to(C, hb[tt % 2], bhb[tt % 2], 8, lambda a, b: hTc[:, a:b, j * 128:(j + 1) * 128], bhTc, "act")
            for ft in list(range(8)) + list(range(12, 20)):
                p = pi % 4
                pi += 1
                for k in range(8):
                    S.op("pe", lambda e: e.matmul(C.ps[p][:], lhsT=W[:, k, ft * 128:(ft + 1) * 128], rhs=hTc[:, k, :], start=(k == 0), stop=(k == 7)),
                         reads=[bW, bhTc], writes=[C.bps[p]])
                o, bo = ob[oi % 4], bob[oi % 4]
                oi += 1
                if ft < 8:
                    S.op("act", lambda e: e.copy(out=o[:], in_=C.ps[p][:]), reads=[C.bps[p]], writes=[bo])
                    dst = ft
                else:
                    p2 = pi % 4
                    pi += 1
                    fs = ft + 12
                    for k in range(8):
                        S.op("pe", lambda e: e.matmul(C.ps[p2][:], lhsT=W[:, k, fs * 128:(fs + 1) * 128], rhs=hTc[:, k, :], start=(k == 0), stop=(k == 7)),
                             reads=[bW, bhTc], writes=[C.bps[p2]])
                    a, ba = t1[oi % 2], bt1[oi % 2]
                    b, bb = t2[oi % 2], bt2[oi % 2]
                    S.op("dve", lambda e: e.tensor_tensor(out=a[:], in0=C.ps[p][:], in1=csc[:, 0, :], op=ALU.mult), reads=[C.bps[p], bcsc], writes=[ba])
                    S.op("dve", lambda e: e.tensor_tensor(out=b[:], in0=C.ps[p2][:], in1=csc[:, 1, :], op=ALU.mult), reads=[C.bps[p2], bcsc], writes=[bb])
                    S.op("pool", lambda e: e.tensor_tensor(out=o[:], in0=a[:], in1=b[:], op=ALU.add), reads=[ba, bb], writes=[bo])
                    dst = ft - 4
                S.dma("sp", C.QK0[s, dst, :, tb * 512:(tb + 1) * 512], o[:], reads=[bo])
            for j in range(4):
                r0 = s * T + (tb * 4 + j) * 128
                for ci, c0 in enumerate((1024, 2560)):
                    p = pi % 4
                    pi += 1
                    for k in range(8):
                        S.op("pe", lambda e: e.matmul(C.ps[p][:], lhsT=hTc[:, k, j * 128:(j + 1) * 128], rhs=W[:, k, c0:c0 + 512], start=(k == 0), stop=(k == 7)),
                             reads=[bW, bhTc], writes=[C.bps[p]])
                    o, bo = ob[oi % 4], bob[oi % 4]
                    oi += 1
                    if ci == 0:
                        S.op("act", lambda e: e.copy(out=o[:], in_=C.ps[p][:]), reads=[C.bps[p]], writes=[bo])
                    else:
                        S.op("dve", lambda e: e.tensor_copy(out=o[:], in_=C.ps[p][:]), reads=[C.bps[p]], writes=[bo])
                    S.dma("sp", C.V0[r0:r0 + 128, ci * 512:(ci + 1) * 512], o[:], reads=[bo])
    S.barrier()
    sc.close()


def _na_valid(rq, rk):
    rs = min(max(rq - 4, 0), 56)
    return rs <= rk < rs + 8


def p2_attn(C):
    nc, S = C.nc, C.S
    sc = Scope(nc)
    alloc_psum(C, sc, 8, 0)
    sc2 = Scope(nc)
    L = sc2.sb("naL", [31, 64 * 128], F32)
    PT = sc2.sb("naPT", [31, 8 * 15], F32)
    Z = [sc2.sb("naZ", [128, 2, 32 * 64], F32) for _ in range(2)]
    bL, bPT, bZ = Buf(), Buf(), bufs(2)
    for c in range(0, 64 * 128, 2048):
        S.dma("sp", L[:, c:c + 2048], C.naL[:, c:c + 2048], writes=[bL], join=(c > 0))
    S.dma("sp", PT[:], C.rpbT[:, :], writes=[bPT])
    S.op("act", lambda e: e.activation(out=PT[:], in_=PT[:], func=AF.Exp), reads=[bPT], writes=[bPT])
    for z in range(2):
        S.op("pool", lambda e: e.memset(Z[z][:], 0.0), writes=[bZ[z]])
    for h in range(8):
        z = h % 2
        for half in range(2):
            p = (h * 2 + half) % 4
            pv = C.ps[p][:].rearrange("p (s c) -> p s c", c=64)
            ns = 8 if half == 0 else 7
            for cq in range(64):
                S.op("pe", lambda e: e.matmul(pv[:, 0:ns, cq], lhsT=L[:, cq * 128:(cq + 1) * 128], rhs=PT[:, h * 15 + half * 8:h * 15 + half * 8 + ns],
                                              start=True, stop=True), reads=[bL, bPT], writes=[C.bps[p]])
            a0 = (8 + half * 8) * 64
            S.op("dve", lambda e: e.tensor_copy(out=Z[z][0:64, 0, a0:a0 + ns * 64], in_=C.ps[p][0:64, 0:ns * 64]), reads=[C.bps[p]], writes=[bZ[z]])
            S.op("dve", lambda e: e.tensor_copy(out=Z[z][64:128, 0, a0 + 64:a0 + 64 + ns * 64], in_=C.ps[p][64:128, 0:ns * 64]), reads=[C.bps[p]], writes=[bZ[z]])
            i0, i1 = (4, 8) if half == 0 else (0, 4)
            S.op("dve", lambda e: e.tensor_copy(out=Z[z][0:64, 1, a0 + i0 * 64:a0 + i1 * 64], in_=C.ps[p][0:64, i0 * 64:i1 * 64]), reads=[C.bps[p]], writes=[bZ[z]])
            S.op("dve", lambda e: e.tensor_copy(out=Z[z][64:128, 1, a0 + 64 + i0 * 64:a0 + 64 + i1 * 64], in_=C.ps[p][64:128, i0 * 64:i1 * 64]), reads=[C.bps[p]], writes=[bZ[z]])
        S.dma("sp", C.TAZ[h, :, :, :].rearrange("v p c -> p v c"), Z[z][:], reads=[bZ[z]])
    S.barrier()
    sc2.close()
    TAzI = [sc.sb("TAzI", [128, 2, 32 * 64], F32) for _ in range(2)]
    bTAI = bufs(2)
    TAzF = sc.sb("TAzF", [128, 2, 32 * 64], F32)
    bTAF = Buf()

    DM = sc.sb("dmask", [128, 20, 512], BF16)
    bDM = Buf()
    for i in range(0, 20, 4):
        S.dma("sp", DM[:, i:i + 4, :], C.dmask[i:i + 4].rearrange("i p q -> p i q"), writes=[bDM], join=(i > 0))
    LA = 8
    NST = 4
    Qz = [[sc.sb("Qz", [128, T], BF16) for _ in range(2)] for _ in range(2)]
    KT = [sc.sb("KT", [128, T], BF16) for _ in range(2)]
    VA = [sc.sb("VA", [128, NT, 256], BF16) for _ in range(2)]
    bQ, bK, bV = bufs(2), bufs(2), bufs(2)
    for b_ in range(2):
        S.op("pool", lambda e: e.memset(Qz[b_][0][64:128, :], 0.0), writes=[bQ[b_]])
        S.op("pool", lambda e: e.memset(Qz[b_][1][0:64, :], 0.0), writes=[bQ[b_]])
        S.op("pool", lambda e: e.memset(VA[b_][:, :, 64:192], 1.0), writes=[bV[b_]])
    NE, NEM = 8, LA + 4
    E = [sc.sb("E", [128, 512], F32) for _ in range(NE)]
    bE = bufs(NE)
    Eb = [sc.sb("Eb", [128, 512], BF16) for _ in range(NE)]
    bEb = bufs(NE)
    Em = [sc.sb("Em", [128, 512], BF16) for _ in range(NEM)]
    bEm = bufs(NEM)
    rc = [sc.sb("rc", [128, 512], F32) for _ in range(2)]
    rs = [sc.sb("rs", [128, 512], F32) for _ in range(2)]
    brc, brs = bufs(2), bufs(2)
    oT = [sc.sb("oT", [128, 512], BF16) for _ in range(2)]
    boT = bufs(2)

    stA, stB = [], []
    pending = []

    def mk_load(s, g, gi):
        def f():
            b_ = gi % 2
            qt, kt = (g, 4 + g) if g < 4 else (8 + (g - 4), 12 + (g - 4))
            for c in range(0, T, 2048):
                S.dma("sp", Qz[b_][0][0:64, c:c + 2048], C.QK0[s, qt, 0:64, c:c + 2048], writes=[bQ[b_]], join=(c > 0))
                S.dma("sp", Qz[b_][1][64:128, c:c + 2048], C.QK0[s, qt, 64:128, c:c + 2048], writes=[bQ[b_]], join=True)
                S.dma("sp", KT[b_][:, c:c + 2048], C.QK0[s, kt, :, c:c + 2048], writes=[bK[b_]], join=(c > 0))
            if g < 4:
                for hh in range(2):
                    S.dma("sp", TAzI[b_][:, hh, :], C.TAZ[g * 2 + hh, 1, :, :], writes=[bTAI[b_]], join=(hh > 0))
            if g == 0:
                ldF(0)()
            vsrc = C.V0[s * T:(s + 1) * T, g * 128:(g + 1) * 128].rearrange("(t p) c -> p t c", p=128)
            for c in range(0, NT, 8):
                S.dma("sp", VA[b_][:, c:c + 8, 0:64], vsrc[:, c:c + 8, 0:64], writes=[bV[b_]], join=(c > 0))
                S.dma("sp", VA[b_][:, c:c + 8, 192:256], vsrc[:, c:c + 8, 64:128], writes=[bV[b_]], join=True)
        return f

    def ldF(g):
        def f():
            for hh in range(2):
                S.dma("sp", TAzF[:, hh, :], C.TAZ[g * 2 + hh, 0, :, :], writes=[bTAF], join=(hh > 0))
        return f

    def mk_tile(s, g, gi, qb, qi, hp, ki, kb, nk, ti, fin):
        isna = g < 4
        b_ = gi % 2
        Qg, Kg, Vg = Qz[b_][hp], KT[b_], VA[b_]
        bQg, bKg, bVg = bQ[b_], bK[b_], bV[b_]
        p = ti % NST
        e_, be_ = (E[ti % NE], bE[ti % NE]) if isna else (Eb[ti % NE], bEb[ti % NE])
        em, bem = Em[ti % NEM], bEm[ti % NEM]
        pA = NST + (qi % 2) * 2
        pB = NST + 1 + (qi % 2) * 2
        R = qb * 8

        def a():
            S.op("pe", lambda e: e.matmul(C.ps[p][:], lhsT=Kg[:, kb * 128:(kb + 1) * 128], rhs=Qg[:, qb * 512:(qb + 1) * 512], start=True, stop=True),
                 reads=[bKg, bQg], writes=[C.bps[p]])
            S.op("act", lambda e: e.activation(out=e_[:], in_=C.ps[p][:], func=AF.Exp, scale=0.125), reads=[C.bps[p]], writes=[be_])
            if isna:
                rk0 = kb * 2
                s0 = 7 - rk0 + R
                fast = all((_na_valid(R + f, rk0 + ph) == (4 <= s0 - ph + f <= 11)) for f in range(8) for ph in range(2))
                if fast:
                    S.op("dve" if (ti % 3) != 2 else "pool", lambda e: e.tensor_tensor(out=em[:], in0=e_[:], in1=TAzI[b_][:, hp, (s0 + 8) * 64:(s0 + 16) * 64], op=ALU.mult), reads=[be_, bTAI[b_]], writes=[bem])
                else:
                    assert qb in (0, 7)
                    for ph in range(2):
                        rk = rk0 + ph
                        fs = [f for f in range(8) if _na_valid(R + f, rk)]
                        pp = slice(ph * 64, (ph + 1) * 64)
                        if fs:
                            f1, f2 = fs[0], fs[-1]
                            assert fs == list(range(f1, f2 + 1))
                            c1 = (s0 + 8 + f1) * 64
                            n = f2 - f1 + 1
                            S.op("dve", lambda e: e.tensor_tensor(out=em[pp, f1 * 64:(f2 + 1) * 64], in0=e_[pp, f1 * 64:(f2 + 1) * 64],
                                                                in1=TAzF[pp, hp, c1:c1 + n * 64], op=ALU.mult), reads=[be_, bTAF], writes=[bem])
                            if f1 > 0:
                                S.op("pool", lambda e: e.memset(em[pp, 0:f1 * 64], 0.0), writes=[bem])
                            if f2 < 7:
                                S.op("pool", lambda e: e.memset(em[pp, (f2 + 1) * 64:512], 0.0), writes=[bem])
                        else:
                            S.op("pool", lambda e: e.memset(em[pp, :], 0.0), writes=[bem])
            else:
                mi = (kb * 128 - qb * 512 + 1024) // 128
                eng = "dve" if (ti % 3) != 2 else "pool"
                S.op(eng, lambda e: e.tensor_tensor(out=em[:], in0=e_[:], in1=DM[:, mi, :], op=ALU.mult), reads=[be_, bDM], writes=[bem])

        def b():
            first, last = ki == 0, ki == nk - 1
            pX = pA if hp == 0 else pB
            S.op("pe", lambda e: e.matmul(C.ps[pX][:], lhsT=Vg[:, kb, hp * 128:(hp + 1) * 128], rhs=em[:], start=first, stop=last),
                 reads=[bVg, bem], writes=[C.bps[pX]])
            if fin:
                j = qi % 2

                def f1():
                    S.op("act", lambda e: e.copy(out=rc[j][64:128, :], in_=C.ps[pA][64:128, :]), reads=[C.bps[pA]], writes=[brc[j]])
                    S.op("act", lambda e: e.copy(out=rc[j][0:64, :], in_=C.ps[pB][0:64, :]), reads=[C.bps[pB]], writes=[brc[j]])
                    S.dma("sp", rs[j][0:64, :], rc[j][64:128, :], reads=[brc[j]], writes=[brs[j]])
                    S.dma("sp", rs[j][64:128, :], rc[j][0:64, :], reads=[brc[j]], writes=[brs[j]])

                def f2():
                    S.op("dve", lambda e: e.reciprocal(out=rs[j][:], in_=rs[j][:]), reads=[brs[j]], writes=[brs[j]])

                def f3():
                    S.op("dve", lambda e: e.tensor_tensor(out=oT[j][0:64, :], in0=C.ps[pA][0:64, :], in1=rs[j][0:64, :], op=ALU.mult), reads=[C.bps[pA], brs[j]], writes=[boT[j]])

                def f4():
                    S.op("dve", lambda e: e.tensor_tensor(out=oT[j][64:128, :], in0=C.ps[pB][64:128, :], in1=rs[j][64:128, :], op=ALU.mult), reads=[C.bps[pB], brs[j]], writes=[boT[j]])
                    S.dma("sp", C.OT0[s, g, :, qb * 512:(qb + 1) * 512], oT[j][:], reads=[boT[j]])
                pending.append([2, f1])
                pending.append([7, f2])
                pending.append([9, f3])
                pending.append([10, f4])
        return a, b

    ti = 0
    qi = 0
    gi = 0
    pairs = [(s_, g_) for s_ in range(C.nseq) for g_ in range(8)]
    for pi_, (s, g) in enumerate(pairs):
        hooks = {}
        if pi_ == 0:
            hooks[0] = [mk_load(s, g, gi)]
        if pi_ + 1 < len(pairs):
            ns_, ng_ = pairs[pi_ + 1]
            hooks.setdefault(4, []).append(mk_load(ns_, ng_, gi + 1))
            if 1 <= ng_ <= 3:
                hooks.setdefault(2, []).append(ldF(ng_))
        qorder = [0, 7, 1, 2, 3, 4, 5, 6] if g < 4 else list(range(NB))
        for qn, qb in enumerate(qorder):
            if g < 4:
                R = qb * 8
                lo = min(max(R - 4, 0), 56)
                hi = min(max(R + 7 - 4, 0), 56) + 8
                kbs = list(range(lo // 2, (hi + 1) // 2))
            else:
                kbs = list(range(max(0, qb * 4 - 8), min(NT, qb * 4 + 4 + 8)))
            nk = len(kbs)
            for ki, kb in enumerate(kbs):
                for hp in range(2):
                    a, b = mk_tile(s, g, gi, qb, qi, hp, ki, kb, nk, ti, fin=(ki == nk - 1 and hp == 1))
                    if ki == 0 and hp == 0 and qn in hooks:
                        hk = hooks[qn]
                        a = (lambda hk=hk, a0=a: ([h() for h in hk], a0()))
                    stA.append(a)
                    stB.append(b)
                    ti += 1
            qi += 1
        gi += 1
    n = len(stA)
    for i in range(n + LA + 12):
        if i < n:
            stA[i]()
        if LA <= i < n + LA:
            stB[i - LA]()
        for pe_ in list(pending):
            pe_[0] -= 1
            if pe_[0] <= 0:
                pe_[1]()
                pending.remove(pe_)
    assert not pending
    S.barrier()
    sc.close()


def p3_outproj0(C):
    nc, S = C.nc, C.S
    sc = Scope(nc)
    alloc_psum(C, sc, 6, 2)
    W, bW = load_w(C, sc, "wo0", C.w_out0, 8, D)
    grow, bg = load_grow(C, sc, 1)
    nb = NormBufs(sc)
    oT = [sc.sb("oTb", [128, 8, 512], BF16) for _ in range(2)]
    boT = bufs(2)
    xt = [sc.sb("xt", [128, D], F32) for _ in range(2)]
    bxt = bufs(2)
    x1 = [sc.sb("x1", [128, D], F32) for _ in range(2)]
    bx1 = bufs(2)
    hb = [sc.sb("hb", [128, D], BF16) for _ in range(2)]
    bhb = bufs(2)
    hT = [sc.sb("hT", [128, 8, 128], BF16) for _ in range(2)]
    bhT = bufs(2)
    pi = 0
    for s in range(C.nseq):
        for tb in range(NB):
            o, bo = oT[tb % 2], boT[tb % 2]
            for k in range(0, 8, 2):
                S.dma("sp", o[:, k:k + 2, :], C.OT0[s, k:k + 2, :, tb * 512:(tb + 1) * 512].rearrange("k p t -> p k t"), writes=[bo], join=(k > 0))
            for j in range(4):
                tt = tb * 4 + j
                r0 = s * T + tt * 128
                i2 = tt % 2
                S.dma("sp", xt[i2][:], C.x[r0:r0 + 128, :], writes=[bxt[i2]])
                for c in range(2):
                    p = pi % 3
                    pi += 1
                    for k in range(8):
                        S.op("pe", lambda e: e.matmul(C.ps[p][:], lhsT=o[:, k, j * 128:(j + 1) * 128], rhs=W[:, k, c * 512:(c + 1) * 512], start=(k == 0), stop=(k == 7)),
                             reads=[bW, bo], writes=[C.bps[p]])
                    S.op("dve", lambda e: e.tensor_tensor(out=x1[i2][:, c * 512:(c + 1) * 512], in0=C.ps[p][:], in1=xt[i2][:, c * 512:(c + 1) * 512], op=ALU.add),
                         reads=[C.bps[p], bxt[i2]], writes=[bx1[i2]])
                S.dma("sp", C.X1[r0:r0 + 128, :], x1[i2][:], reads=[bx1[i2]])
                rmsnorm(C, nb, x1[i2][:], bx1[i2], grow, bg, hb[i2][:], bhb[i2])
                transpose_to(C, hb[i2], bhb[i2], 8, lambda a, b: hT[i2][:, a:b, :], bhT[i2], "act")
                S.dma("sp", C.HTa[s, :, :, tt * 128:(tt + 1) * 128].rearrange("k p t -> p k t"), hT[i2][:], reads=[bhT[i2]])
    S.barrier()
    sc.close()


def p4_ffn(C, layer):
    nc, S = C.nc, C.S
    sc = Scope(nc)
    alloc_psum(C, sc, 7, 1)
    HTin = C.HTa if layer == 0 else C.HTb
    Xin = C.X1 if layer == 0 else C.X3
    Wu, bWu = load_w(C, sc, "wu", C.w_up[layer], 8, 2 * FF)
    Wd, bWd = load_w(C, sc, "wd", C.w_dn[layer], 22, D)
    cw = sc.sb("cw", [128, 44, 4], F32)
    bcw = Buf()
    S.dma("sp", cw[:], C.cwT[layer].rearrange("p (t c) -> p t c", c=4), writes=[bcw])
    grow, bg = load_grow(C, sc, 2 if layer == 0 else 4)
    nb = NormBufs(sc, 1)
    hT = sc.sb("h2T", [128, 8, 514], BF16)
    bh = Buf()
    gT = sc.sb("gT", [128, 22, 512], BF16)
    bgT = bufs(22)
    ah = sc.sb("ahalo", [128, 2, 44], F32)
    bah = Buf()
    yus = [sc.sb("yu", [128, 512], F32) for _ in range(2)]
    ygs = [sc.sb("yg", [128, 512], F32) for _ in range(2)]
    ggs = [sc.sb("gg", [128, 512], F32) for _ in range(2)]
    byus, bygs, bggs = bufs(2), bufs(2), bufs(2)
    xt = sc.sb("xt", [128, D], F32)
    bxt = Buf()
    x2 = sc.sb("x2", [128, D], F32)
    bx2 = Buf()
    if layer == 0:
        hbs = [sc.sb("hb", [128, D], BF16) for _ in range(2)]
        h3T = sc.sb("h3T", [128, 8, 128], BF16)
        bhbs, bh3 = bufs(2), Buf()
    else:
        yo = sc.sb("yo", [128, D], F32)
        byo = Buf()
    S.op("pool", lambda e: e.memset(hT[:], 0.0), writes=[bh])
    pi = 0
    deferred = []
    prev_s2 = [None]
    for s in range(C.nseq):
        for tb in range(NB):
            t0 = tb * 512
            lo = max(t0 - 1, 0)
            hi = min(t0 + 513, T)
            c0 = lo - (t0 - 1)
            if tb == 0:
                S.op("pool", lambda e: e.memset(hT[:, :, 0:1], 0.0), writes=[bh])
            if tb == NB - 1:
                S.op("pool", lambda e: e.memset(hT[:, :, 513:514], 0.0), writes=[bh])
            for k in range(0, 8, 2):
                S.dma("sp", hT[:, k:k + 2, c0:c0 + (hi - lo)], HTin[s, k:k + 2, :, lo:hi].rearrange("k p t -> p k t"), writes=[bh], join=(k > 0))
            hal = hT[:, :, 0:514:513]
            for f0 in range(0, 44, 11):
                p = 6
                for f in range(f0, f0 + 11):
                    for k in range(8):
                        S.op("pe", lambda e: e.matmul(C.ps[p][:, (f - f0) * 2:(f - f0) * 2 + 2], lhsT=Wu[:, k, f * 128:(f + 1) * 128], rhs=hal[:, k, :],
                                                      start=(k == 0), stop=(k == 7)), reads=[bWu, bh], writes=[C.bps[p]])
                pv = C.ps[p][:, 0:22].rearrange("p (f c) -> p c f", c=2)
                S.op("dve", lambda e: e.tensor_tensor(out=ah[:, 0, f0:f0 + 11], in0=pv[:, 0, :], in1=cw[:, f0:f0 + 11, 0], op=ALU.mult), reads=[C.bps[p], bcw], writes=[bah])
                S.op("dve", lambda e: e.tensor_tensor(out=ah[:, 1, f0:f0 + 11], in0=pv[:, 1, :], in1=cw[:, f0:f0 + 11, 2], op=ALU.mult), reads=[C.bps[p], bcw], writes=[bah])
            for f in range(22):
                yu, yg, gg = yus[f % 2], ygs[f % 2], ggs[f % 2]
                byu, byg, bgg = byus[f % 2], bygs[f % 2], bggs[f % 2]
                for (ft, yt, byt) in ((f, yu, byu), (22 + f, yg, byg)):
                    p = pi % 4
                    pi += 1
                    for k in range(8):
                        S.op("pe", lambda e: e.matmul(C.ps[p][:], lhsT=Wu[:, k, ft * 128:(ft + 1) * 128], rhs=hT[:, k, 1:513], start=(k == 0), stop=(k == 7)),
                             reads=[bWu, bh], writes=[C.bps[p]])
                    A = C.ps[p]
                    S.op("act", lambda e: e.activation(out=yt[:], in_=A[:], func=AF.Identity, scale=cw[:, ft, 1:2], bias=cw[:, ft, 3:4]), reads=[C.bps[p], bcw], writes=[byt])
                    S.op("dve", lambda e: e.scalar_tensor_tensor(out=yt[:, 1:512], in0=A[:, 0:511], scalar=cw[:, ft, 0:1], in1=yt[:, 1:512], op0=ALU.mult, op1=ALU.add),
                         reads=[C.bps[p], bcw, byt], writes=[byt])
                    S.op("dve", lambda e: e.scalar_tensor_tensor(out=yt[:, 0:511], in0=A[:, 1:512], scalar=cw[:, ft, 2:3], in1=yt[:, 0:511], op0=ALU.mult, op1=ALU.add),
                         reads=[C.bps[p], bcw, byt], writes=[byt])
                    S.op("pool", lambda e: e.tensor_tensor(out=yt[:, 0:1], in0=yt[:, 0:1], in1=ah[:, 0, ft:ft + 1], op=ALU.add), reads=[byt, bah], writes=[byt])
                    S.op("pool", lambda e: e.tensor_tensor(out=yt[:, 511:512], in0=yt[:, 511:512], in1=ah[:, 1, ft:ft + 1], op=ALU.add), reads=[byt, bah], writes=[byt])
                def s2(f=f, yu=yu, yg=yg, gg=gg, byu=byu, byg=byg, bgg=bgg):
                    S.op("act", lambda e: e.activation(out=gg[:], in_=yg[:], func=AF.Gelu_apprx_tanh), reads=[byg], writes=[bgg])
                    S.op("pool", lambda e: e.tensor_tensor(out=gT[:, f, :], in0=yu[:], in1=gg[:], op=ALU.mult), reads=[byu, bgg], writes=[bgT[f]])
                if prev_s2[0] is not None:
                    prev_s2[0]()
                prev_s2[0] = s2
            prev_s2[0]()
            prev_s2[0] = None
            for j in range(4):
                tt = tb * 4 + j
                r0 = s * T + tt * 128
                S.dma("sp", xt[:], Xin[r0:r0 + 128, :], writes=[bxt])
                for c in range(2):
                    p = 4 + (pi % 2)
                    pi += 1
                    for f in range(22):
                        S.op("pe", lambda e: e.matmul(C.ps[p][:], lhsT=gT[:, f, j * 128:(j + 1) * 128], rhs=Wd[:, f, c * 512:(c + 1) * 512], start=(f == 0), stop=(f == 21)),
                             reads=[bWd, bgT[f]], writes=[C.bps[p]])
                    S.op("dve", lambda e: e.tensor_tensor(out=x2[:, c * 512:(c + 1) * 512], in0=C.ps[p][:], in1=xt[:, c * 512:(c + 1) * 512], op=ALU.add),
                         reads=[C.bps[p], bxt], writes=[bx2])
                if layer == 0:
                    S.dma("sp", C.X2[r0:r0 + 128, :], x2[:], reads=[bx2])
                    hb, bhb = hbs[tt % 2], bhbs[tt % 2]
                    rmsnorm(C, nb, x2[:], bx2, grow, bg, hb[:], bhb)

                    def fin(hb=hb, bhb=bhb, tt=tt, s=s):
                        transpose_to(C, hb, bhb, 8, lambda a, b: h3T[:, a:b, :], bh3, "act")
                        S.dma("sp", C.HTb[s, :, :, tt * 128:(tt + 1) * 128].rearrange("k p t -> p k t"), h3T[:], reads=[bh3])
                    deferred.append(fin)
                    if len(deferred) > 1:
                        deferred.pop(0)()
                else:
                    rmsnorm(C, nb, x2[:], bx2, grow, bg, yo[:], byo)
                    S.dma("sp", C.y[r0:r0 + 128, :], yo[:], reads=[byo])
            while deferred:
                deferred.pop(0)()
    S.barrier()
    sc.close()


def p5_inproj1(C):
    nc, S = C.nc, C.S
    sc = Scope(nc)
    alloc_psum(C, sc, 6, 2)
    W, bW = load_w(C, sc, "w1", C.w_in1, 8, 6144)
    hT = [sc.sb("h3T", [128, 8, 512], BF16) for _ in range(2)]
    bh = bufs(2)
    cs = [sc.sb("cs1", [128, 2, 512], F32) for _ in range(2)]
    bcs = bufs(2)
    tm = [sc.sb("tm", [128, 512], F32) for _ in range(4)]
    btm = bufs(4)
    ob = [sc.sb("ob", [128, 512], BF16) for _ in range(4)]
    bob = bufs(4)
    kr = [sc.sb("kr", [128, 2, 512], BF16) for _ in range(2)]
    bkr = bufs(2)
    ktm = [sc.sb("ktm", [128, 256], BF16) for _ in range(2)]
    bktm = bufs(2)
    oi = 0
    pi = 0
    ki = 0
    for s in range(C.nseq):
        for tb in range(NB):
            h, bhc = hT[tb % 2], bh[tb % 2]
            csc, bcsc = cs[tb % 2], bcs[tb % 2]
            for k in range(0, 8, 2):
                S.dma("sp", h[:, k:k + 2, :], C.HTb[s, k:k + 2, :, tb * 512:(tb + 1) * 512].rearrange("k p t -> p k t"), writes=[bhc], join=(k > 0))
            S.dma("sp", csc[:, 0, :], C.cos1[:, tb * 512:(tb + 1) * 512], writes=[bcsc])
            S.dma("sp", csc[:, 1, :], C.sin1[:, tb * 512:(tb + 1) * 512], writes=[bcsc], join=True)
            for qk in range(2):
                for hd in range(4):
                    pa, pb = pi % 4, (pi + 1) % 4
                    pi += 2
                    for (p, c0) in ((pa, qk * 1024 + hd * 256), (pb, qk * 1024 + hd * 256 + 128)):
                        for k in range(8):
                            S.op("pe", lambda e: e.matmul(C.ps[p][:], lhsT=W[:, k, c0:c0 + 128], rhs=h[:, k, :], start=(k == 0), stop=(k == 7)),
                                 reads=[bW, bhc], writes=[C.bps[p]])
                    outs = []
                    for half in range(2):
                        ta, bta = tm[(oi * 2) % 4], btm[(oi * 2) % 4]
                        tb_, btb = tm[(oi * 2 + 1) % 4], btm[(oi * 2 + 1) % 4]
                        o, bo = ob[oi % 4], bob[oi % 4]
                        oi += 1
                        S.op("dve", lambda e: e.tensor_tensor(out=ta[:], in0=C.ps[pa][:], in1=csc[:, half, :], op=ALU.mult), reads=[C.bps[pa], bcsc], writes=[bta])
                        S.op("dve", lambda e: e.tensor_tensor(out=tb_[:], in0=C.ps[pb][:], in1=csc[:, 1 - half, :], op=ALU.mult), reads=[C.bps[pb], bcsc], writes=[btb])
                        if qk == 0:
                            S.op("pool", lambda e: e.tensor_tensor(out=o[:], in0=ta[:], in1=tb_[:], op=(ALU.subtract if half == 0 else ALU.add)), reads=[bta, btb], writes=[bo])
                            S.dma("sp", C.QTR[s, hd * 2 + half, :, tb * 512:(tb + 1) * 512], o[:], reads=[bo])
                        else:
                            kk, bkk = kr[ki % 2], bkr[ki % 2]
                            S.op("pool", lambda e: e.tensor_tensor(out=kk[:, half, :], in0=ta[:], in1=tb_[:], op=(ALU.subtract if half == 0 else ALU.add)), reads=[bta, btb], writes=[bkk])
                            S.dma("sp", C.KTR[s, hd * 2 + half, :, tb * 512:(tb + 1) * 512], kk[:, half, :], reads=[bkk])
                    if qk == 1:
                        kk, bkk = kr[ki % 2], bkr[ki % 2]
                        ki += 1
                        for j in range(4):
                            r0 = s * T + (tb * 4 + j) * 128
                            kt_, bkt_ = ktm[j % 2], bktm[j % 2]
                            hh = C.pti % C.npt
                            C.pti += 1
                            for half in range(2):
                                S.op("pe", lambda e: e.transpose(out=C.pt[hh][:, half, :], in_=kk[:, half, j * 128:(j + 1) * 128], identity=C.idt[:]),
                                     reads=[bkk, C.bidt], writes=[C.bpt[hh]])
                            S.op("act", lambda e: e.copy(out=kt_[:].rearrange("p (a b) -> p a b", a=2), in_=C.pt[hh][:, 0:2, :]), reads=[C.bpt[hh]], writes=[bkt_])
                            S.dma("sp", C.KTM[r0:r0 + 128, hd * 256:(hd + 1) * 256], kt_[:], reads=[bkt_])
            for j in range(4):
                r0 = s * T + (tb * 4 + j) * 128
                for c in range(8):
                    p = pi % 4
                    pi += 1
                    c0 = 2048 + c * 512
                    for k in range(8):
                        S.op("pe", lambda e: e.matmul(C.ps[p][:], lhsT=h[:, k, j * 128:(j + 1) * 128], rhs=W[:, k, c0:c0 + 512], start=(k == 0), stop=(k == 7)),
                             reads=[bW, bhc], writes=[C.bps[p]])
                    o, bo = ob[oi % 4], bob[oi % 4]
                    oi += 1
                    if c < 4:
                        S.op("dve", lambda e: e.tensor_copy(out=o[:], in_=C.ps[p][:]), reads=[C.bps[p]], writes=[bo])
                        S.dma("sp", C.VR[r0:r0 + 128, c * 512:(c + 1) * 512], o[:], reads=[bo])
                    else:
                        S.op("act", lambda e: e.activation(out=o[:], in_=C.ps[p][:], func=AF.Silu), reads=[C.bps[p]], writes=[bo])
                        S.dma("sp", C.SG[r0:r0 + 128, (c - 4) * 512:(c - 3) * 512], o[:], reads=[bo])
    S.barrier()
    sc.close()


def p6_retention(C):
    nc, S = C.nc, C.S
    sc = Scope(nc)
    alloc_psum(C, sc, 0, 1)
    pin = sc.psum("pin", [128, 512], F32)
    bpin = Buf()
    pout = [sc.psum("pout", [128, 512], F32) for _ in range(2)]
    bpout = bufs(2)
    pst = [sc.psum("pst", [128, 2, 512], F32) for _ in range(2)]
    bpst = bufs(2)
    Wo, bWo = load_w(C, sc, "wo1", C.w_out1, 16, D)
    grow, bg = load_grow(C, sc, 3)
    nb = NormBufs(sc, 1)
    lg = sc.sb("lg", [128, 8], F32)
    DT = sc.sb("DT", [128, 8, 128], F32)
    KD = sc.sb("KD", [128, 8], F32)
    SD = sc.sb("SD", [128, 8], F32)
    QDT = sc.sb("QDT", [128, 2, 8, 128], F32)
    KDT = sc.sb("KDT", [128, 2, 1024], F32)
    sc2 = Scope(nc)
    rc = sc2.sb("retc", [128, 7, 128], F32)
    QD = sc2.sb("QD", [128, 8, 128], F32)
    onesf = sc2.sb("onesf", [128, 256], F32)
    brc = Buf()
    S.dma("sp", rc[:], C.retc[0:7].rearrange("c p q -> p c q"), writes=[brc])
    blg = Buf()
    S.dma("sp", lg[:], C.decay.partition_broadcast(128), writes=[blg])
    S.op("act", lambda e: e.activation(out=lg[:], in_=lg[:], func=AF.Exp), reads=[blg], writes=[blg])
    S.op("act", lambda e: e.activation(out=lg[:], in_=lg[:], func=AF.Ln, bias=1.0), reads=[blg], writes=[blg])
    S.op("dve", lambda e: e.tensor_scalar(out=lg[:], in0=lg[:], scalar1=-1.0, scalar2=None, op0=ALU.mult), reads=[blg], writes=[blg])
    bDT, bQD, bKD, bSD = Buf(), Buf(), Buf(), Buf()
    for d in range(2):
        for hd in range(4):
            i = d * 4 + hd
            S.op("act", lambda e: e.activation(out=DT[:, i, :], in_=rc[:, 2 * d, :], func=AF.Exp, scale=lg[:, i:i + 1]), reads=[brc, blg], writes=[bDT])
            S.op("dve", lambda e: e.tensor_tensor(out=DT[:, i, :], in0=DT[:, i, :], in1=rc[:, 2 * d + 1, :], op=ALU.mult), reads=[brc, bDT], writes=[bDT])
            S.op("act", lambda e: e.activation(out=QD[:, i, :], in_=rc[:, 4 + d, :], func=AF.Exp, scale=lg[:, i:i + 1]), reads=[brc, blg], writes=[bQD])
            S.op("dve", lambda e: e.tensor_scalar(out=QD[:, i, :], in0=QD[:, i, :], scalar1=1.0 / 16, scalar2=None, op0=ALU.mult), reads=[bQD], writes=[bQD])
            S.op("act", lambda e: e.activation(out=KD[:, i:i + 1], in_=rc[:, 6, d:d + 1], func=AF.Exp, scale=lg[:, i:i + 1]), reads=[brc, blg], writes=[bKD])
            S.op("act", lambda e: e.activation(out=SD[:, i:i + 1], in_=rc[:, 6, 2:3], func=AF.Exp, scale=lg[:, i:i + 1]), reads=[brc, blg], writes=[bSD])

    bQDT, bKDT, bof = Buf(), Buf(), Buf()
    S.op("pool", lambda e: e.memset(onesf[:], 1.0), writes=[bof])
    for d in range(2):
        for hd in range(4):
            i = d * 4 + hd
            for t in range(2):
                S.op("pool", lambda e: e.tensor_copy(out=QDT[:, d, hd * 2 + t, :], in_=QD[:, i, :]), reads=[bQD], writes=[bQDT])
            S.op("dve", lambda e: e.tensor_scalar(out=KDT[:, d, hd * 256:(hd + 1) * 256], in0=onesf[:], scalar1=KD[:, i:i + 1], scalar2=None, op0=ALU.mult),
                 reads=[bof, bKD], writes=[bKDT])

    S.barrier()
    sc2.close()

    S32 = sc.sb("S32", [128, 4, 2, 512], F32)
    Sbf = sc.sb("Sbf", [128, 4, 2, 512], BF16)
    bS32, bSbf = bufs(4), bufs(4)
    qT = [sc.sb("qT", [128, 8, 128], BF16) for _ in range(4)]
    kT = [sc.sb("kT", [128, 8, 128], BF16) for _ in range(4)]
    kM = [sc.sb("kM", [128, D], BF16) for _ in range(4)]
    vM = [sc.sb("vM", [128, 2048], BF16) for _ in range(4)]
    bq, bk, bkm, bv = bufs(4), bufs(4), bufs(4), bufs(4)
    iT = [sc.sb("iT", [128, 4, 128], BF16) for _ in range(4)]
    biT = bufs(4)
    qd = [sc.sb("qd", [128, 8, 128], BF16) for _ in range(4)]
    bqd = bufs(4)
    kd = [sc.sb("kd", [128, 1024], BF16) for _ in range(4)]
    bkd = bufs(4)
    rf = [sc.sb("rf", [128, 512], F32) for _ in range(4)]
    brf = bufs(4)
    rfl = [sc.sb("rfl", [128, 2048], F32) for _ in range(2)]
    brfl = bufs(2)
    sgl = [sc.sb("sgl", [128, 2048], BF16) for _ in range(2)]
    bsgl = bufs(2)
    rr = [sc.sb("rr", [128, 512], F32) for _ in range(2)]
    brr = bufs(2)
    st = [sc.sb("st", [128, 8], F32) for _ in range(2)]
    bst = bufs(2)
    rg = sc.sb("rg", [128, 2048], BF16)
    brg = bufs(4)
    rgT = sc.sb("rgT", [128, 16, 128], BF16)
    brgT = Buf()
    xt = [sc.sb("xt", [128, D], F32) for _ in range(2)]
    bxt = bufs(2)
    x3 = sc.sb("x3", [128, D], F32)
    bx3 = Buf()
    hb = sc.sb("hb", [128, D], BF16)
    bhb = Buf()
    h4T = sc.sb("h4T", [128, 8, 128], BF16)
    bh4 = Buf()
    cnt = dict(r=0, o=0)

    def mk_chunk(s, d, n, c, ci):
        r0 = s * T + c * 128
        j = ci % 4
        q_, k_, km_, v_ = qT[j], kT[j], kM[j], vM[j]

        def a():
            S.dma("sp", q_[:], C.QTR[s, :, :, c * 128:(c + 1) * 128].rearrange("k p t -> p k t"), writes=[bq[j]])
            S.dma("sp", k_[:], C.KTR[s, :, :, c * 128:(c + 1) * 128].rearrange("k p t -> p k t"), writes=[bk[j]])
            S.dma("sp", km_[:], C.KTM[r0:r0 + 128, :], writes=[bkm[j]])
            S.dma("sp", v_[:], C.VR[r0:r0 + 128, :], writes=[bv[j]])

        def ac():
            for hd in range(4):
                for i in range(2):
                    S.op("pe", lambda e: e.matmul(pin[:, hd * 128:(hd + 1) * 128], lhsT=k_[:, hd * 2 + i, :], rhs=q_[:, hd * 2 + i, :], start=(i == 0), stop=(i == 1)),
                         reads=[bk[j], bq[j]], writes=[bpin])
            S.op("dve", lambda e: e.tensor_tensor(out=iT[j][:], in0=pin[:].rearrange("p (h q) -> p h q", h=4), in1=DT[:, d * 4:(d + 1) * 4, :], op=ALU.mult),
                 reads=[bpin, bDT], writes=[biT[j]])
            if n > 0:
                S.op("pool", lambda e: e.tensor_tensor(out=qd[j][:], in0=q_[:], in1=QDT[:, d, :, :], op=ALU.mult), reads=[bq[j], bQDT], writes=[bqd[j]])
            S.op("pool", lambda e: e.tensor_tensor(out=kd[j][:], in0=km_[:], in1=KDT[:, d, :], op=ALU.mult), reads=[bkm[j], bKDT], writes=[bkd[j]])

        j2 = ci % 2

        def a2():
            if d == 1:
                S.dma("sp", rfl[j2][:], C.RF[r0:r0 + 128, :], writes=[brfl[j2]])
                S.dma("sp", sgl[j2][:], C.SG[r0:r0 + 128, :], writes=[bsgl[j2]])
                S.dma("sp", xt[j2][:], C.X2[r0:r0 + 128, :], writes=[bxt[j2]])

        def b():
            for hd in range(4):
                di = d * 4 + hd
                o_ = cnt["o"] % 2
                cnt["o"] += 1
                po, bpo = pout[o_], bpout[o_]
                ps_, bps_ = pst[o_], bpst[o_]
                vh = v_[:, hd * 512:(hd + 1) * 512]
                S.op("pe", lambda e: e.matmul(po[:], lhsT=iT[j][:, hd, :], rhs=vh, start=True, stop=(n == 0)), reads=[biT[j], bv[j]], writes=[bpo])
                if n > 0:
                    for i in range(2):
                        S.op("pe", lambda e: e.matmul(po[:], lhsT=qd[j][:, hd * 2 + i, :], rhs=Sbf[:, hd, i, :], start=False, stop=(i == 1)),
                             reads=[bqd[j], bSbf[hd]], writes=[bpo])
                rq = cnt["r"] % 4
                ri = cnt["r"] % 2
                cnt["r"] += 1
                if d == 0:
                    S.op("act", lambda e: e.copy(out=rf[rq][:], in_=po[:]), reads=[bpo], writes=[brf[rq]])
                    S.dma("sp", C.RF[r0:r0 + 128, hd * 512:(hd + 1) * 512], rf[rq][:], reads=[brf[rq]])
                else:
                    r_, br_ = rr[ri], brr[ri]
                    s_, bs_ = st[ri], bst[ri]
                    S.op("dve", lambda e: e.tensor_tensor(out=r_[:], in0=po[:], in1=rfl[j2][:, hd * 512:(hd + 1) * 512], op=ALU.add), reads=[bpo, brfl[j2]], writes=[br_])
                for i in range(2):
                    S.op("pe", lambda e: e.matmul(ps_[:, i, :], lhsT=kd[j][:, hd * 256 + i * 128:hd * 256 + (i + 1) * 128], rhs=vh, start=True, stop=True),
                         reads=[bkd[j], bv[j]], writes=[bps_])
                if n == 0:
                    S.op("dve", lambda e: e.tensor_copy(out=S32[:, hd, :, :], in_=ps_[:]), reads=[bps_], writes=[bS32[hd]])
                else:
                    S.op("dve", lambda e: e.scalar_tensor_tensor(out=S32[:, hd, :, :], in0=S32[:, hd, :, :], scalar=SD[:, di:di + 1], in1=ps_[:], op0=ALU.mult, op1=ALU.add),
                         reads=[bps_, bS32[hd], bSD], writes=[bS32[hd]])
                S.op("act", lambda e: e.copy(out=Sbf[:, hd, :, :], in_=S32[:, hd, :, :]), reads=[bS32[hd]], writes=[bSbf[hd]])
                if d == 1:
                    S.op("dve", lambda e: e.bn_stats(out=s_[:, 0:6], in_=r_[:]), reads=[br_], writes=[bs_])
                    S.op("dve", lambda e: e.bn_aggr(out=s_[:, 6:8], in_=s_[:, 0:6]), reads=[bs_], writes=[bs_])
                    S.op("dve", lambda e: e.tensor_scalar(out=s_[:, 7:8], in0=s_[:, 7:8], scalar1=EPS, scalar2=None, op0=ALU.add), reads=[bs_], writes=[bs_])
                    S.op("act", lambda e: e.activation(out=s_[:, 7:8], in_=s_[:, 7:8], func=AF.Sqrt), reads=[bs_], writes=[bs_])
                    S.op("dve", lambda e: e.reciprocal(out=s_[:, 7:8], in_=s_[:, 7:8]), reads=[bs_], writes=[bs_])
                    S.op("dve", lambda e: e.tensor_scalar(out=r_[:], in0=r_[:], scalar1=s_[:, 6:7], scalar2=s_[:, 7:8], op0=ALU.subtract, op1=ALU.mult), reads=[br_, bs_], writes=[br_])
                    S.op("pool", lambda e: e.tensor_tensor(out=rg[:, hd * 512:(hd + 1) * 512], in0=r_[:], in1=sgl[j2][:, hd * 512:(hd + 1) * 512], op=ALU.mult),
                         reads=[br_, bsgl[j2]], writes=[brg[hd]])
            if d == 1:
                for hd in range(4):
                    transpose_to(C, rg[:, hd * 512:(hd + 1) * 512], brg[hd], 4, lambda a_, b_: rgT[:, hd * 4 + a_:hd * 4 + b_, :], brgT, "act")
                for c2 in range(2):
                    o_ = cnt["o"] % 2
                    cnt["o"] += 1
                    po, bpo = pout[o_], bpout[o_]
                    for k in range(16):
                        S.op("pe", lambda e: e.matmul(po[:], lhsT=rgT[:, k, :], rhs=Wo[:, k, c2 * 512:(c2 + 1) * 512], start=(k == 0), stop=(k == 15)),
                             reads=[bWo, brgT], writes=[bpo])
                    S.op("dve", lambda e: e.tensor_tensor(out=x3[:, c2 * 512:(c2 + 1) * 512], in0=po[:], in1=xt[j2][:, c2 * 512:(c2 + 1) * 512], op=ALU.add),
                         reads=[bpo, bxt[j2]], writes=[bx3])
                S.dma("sp", C.X3[r0:r0 + 128, :], x3[:], reads=[bx3])
                rmsnorm(C, nb, x3[:], bx3, grow, bg, hb[:], bhb)
                transpose_to(C, hb, bhb, 8, lambda a_, b_: h4T[:, a_:b_, :], bh4, "act")
                S.dma("sp", C.HTb[s, :, :, c * 128:(c + 1) * 128].rearrange("k p t -> p k t"), h4T[:], reads=[bh4])
        return a, ac, a2, b

    ci = 0
    for s in range(C.nseq):
        for d in range(2):
            chunks = list(range(NT)) if d == 0 else list(range(NT - 1, -1, -1))
            st_ = []
            for n, c in enumerate(chunks):
                st_.append(mk_chunk(s, d, n, c, ci))
                ci += 1
            for k in range(NT + 4):
                if k >= 4:
                    st_[k - 4][3]()
                if k < NT:
                    st_[k][0]()
                if 2 <= k < NT + 2:
                    st_[k - 2][1]()
                if 3 <= k < NT + 3:
                    st_[k - 3][2]()
            S.barrier()
    S.barrier()
    sc.close()


def _prep_shared(inp):
    f = np.float32
    w_in = np.asarray(inp["even_w_in"], f)[0]
    def swap(cols):
        c = cols.reshape(D, 8, 2, 32)
        return c[:, :, ::-1, :].reshape(D, 512)
    dq = w_in[:, 1536:2048]
    dk = w_in[:, 2048:2560]
    w_in0 = np.ascontiguousarray(np.concatenate([w_in, swap(dq), swap(dk)], axis=1))
    cw = np.asarray(inp["ffn_conv_w"], f)
    cb = np.asarray(inp["ffn_conv_b"], f)
    cwT = []
    for l in range(2):
        a = np.concatenate([cw[l], cb[l][None]], 0)
        a = a.reshape(4, 44, 128).transpose(2, 1, 0)
        cwT.append(np.ascontiguousarray(a.reshape(128, 44 * 4)))
    norms = np.ascontiguousarray(np.concatenate([
        np.asarray(inp["attn_norm"], f)[0:1], np.asarray(inp["ffn_norm"], f)[0:1],
        np.asarray(inp["attn_norm"], f)[1:2], np.asarray(inp["ffn_norm"], f)[1:2],
        np.asarray(inp["final_norm"], f)[None]], 0))
    rpb = np.asarray(inp["na_rpb"], f)[0]
    rpbT = np.ascontiguousarray(rpb[:, ::-1, :].transpose(2, 0, 1).reshape(31, 8 * 15))
    decay = np.ascontiguousarray(np.concatenate([np.asarray(inp["ret_decay_fwd"], f)[0], np.asarray(inp["ret_decay_bwd"], f)[0]]))
    sh = dict(
        w_in0=w_in0, w_out0=np.ascontiguousarray(np.asarray(inp["even_w_out"], f)[0]),
        w_up0=np.ascontiguousarray(np.asarray(inp["ffn_w_up"], f)[0]), w_up1=np.ascontiguousarray(np.asarray(inp["ffn_w_up"], f)[1]),
        w_dn0=np.ascontiguousarray(np.asarray(inp["ffn_w_down"], f)[0]), w_dn1=np.ascontiguousarray(np.asarray(inp["ffn_w_down"], f)[1]),
        cwT0=cwT[0], cwT1=cwT[1],
        w_in1=np.ascontiguousarray(np.asarray(inp["ret_w_in"], f)[0]), w_out1=np.ascontiguousarray(np.asarray(inp["ret_w_out"], f)[0]),
        norms=norms, rpbT=rpbT, decay=decay,
    )
    sh.update(_consts())
    return sh


def _assign():
    seqs = [("p", i) for i in range(BATCH)] + [("s", i) for i in range(DEC_BATCH)]
    slots = [[] for _ in range(NCORES)]
    for i, sq in enumerate(seqs):
        slots[i % NCORES].append(sq)
    return slots


def kernel(**inputs):
    nseq = 3
    sh = _prep_shared(inputs)
    xp = np.asarray(inputs["x_prompt"], np.float32)
    xs = np.asarray(inputs["x_sample"], np.float32)
    slots = _assign()
    in_maps = []
    for c in range(NCORES):
        xc = np.zeros((nseq * T, D), np.float32)
        for j, (kind, i) in enumerate(slots[c]):
            xc[j * T:(j + 1) * T] = xp[i] if kind == "p" else xs[i]
        for j in range(len(slots[c]), nseq):
            xc[j * T:(j + 1) * T] = xc[0:T]
        m = dict(sh)
        m["x"] = xc
        in_maps.append(m)
    nc, _ = build(nseq)
    res = run_bass_kernel_spmd(nc, in_maps, core_ids=list(range(NCORES)))
    yp = np.zeros((BATCH, T, D), np.float32)
    ys = np.zeros((DEC_BATCH, T, D), np.float32)
    for c in range(NCORES):
        yc = np.asarray(res.results[c]["y"]).reshape(nseq, T, D)
        for j, (kind, i) in enumerate(slots[c]):
            if kind == "p":
                yp[i] = yc[j]
            else:
                ys[i] = yc[j]
    return (yp, ys)
```

```python
import math
from contextlib import ExitStack

import numpy as np
import ml_dtypes
import concourse.bass as bass
import concourse.mybir as mybir
from concourse.bass_utils import run_bass_kernel_spmd

F32 = mybir.dt.float32
BF16 = mybir.dt.bfloat16
AF = mybir.ActivationFunctionType
ALU = mybir.AluOpType

T = 4096
D = 1024
NT = 32
NB = 8
FF = 2816
NCORES = 8
EPS = 1e-6
BATCH, DEC_BATCH = 16, 4


class Buf:
    __slots__ = ("w", "r", "ws")

    def __init__(self):
        self.w = None
        self.r = {}
        self.ws = []


def bufs(n):
    return [Buf() for _ in range(n)]


class Sched:
    LIMIT = 30000

    def __init__(self, nc, n_dma_sems=40):
        self.nc = nc
        self.eng = dict(pe=nc.tensor, act=nc.scalar, dve=nc.vector, pool=nc.gpsimd, sp=nc.sync)
        self.csem, self.ccnt, self.nsem = {}, {}, 0
        for e in self.eng:
            self._newsem(e)
        self.known = {e: {} for e in self.eng}
        self.dsems, self.dcnt, self.dnext = {}, {}, {}
        for e, n in (("sp", n_dma_sems), ("pool", 12), ("act", 8)):
            self.dsems[e] = [nc.alloc_semaphore(f"dq_{e}{i}") for i in range(n)]
            self.dcnt[e] = [0] * n
            self.dnext[e] = 0
        self.ninstr = 0
        self.out_tks = []
        import os
        self.cap = int(os.environ.get('KCAP', '1000000000'))
        self.nreal = 0

    def _newsem(self, e):
        self.nsem += 1
        self.csem[e] = self.nc.alloc_semaphore(f"c_{e}_{self.nsem}")
        self.ccnt[e] = 0

    def _wait(self, e, tk):
        if tk is None:
            return
        sem, val = tk
        k = self.known[e]
        if k.get(sem.num, 0) >= val:
            return
        if e == "pe" and sem is self.csem["pe"]:
            return
        self.eng[e].wait_ge(sem, val)
        self.ninstr += 1
        k[sem.num] = val

    def _deps(self, e, reads, writes, join=False):
        for b in reads:
            self._wait(e, b.w)
            for tk in b.ws:
                self._wait(e, tk)
        for b in writes:
            if not join:
                self._wait(e, b.w)
                for tk in b.ws:
                    self._wait(e, tk)
            for tk in list(b.r.values()):
                self._wait(e, tk)

    def _mark(self, tk, reads, writes, join=False):
        sem, val = tk
        for b in reads:
            b.r[sem.num] = tk
        for b in writes:
            if join:
                b.ws.append(tk)
            else:
                b.w = tk
                b.ws = []
            b.r = {}

    def op(self, e, fn, reads=(), writes=()):
        self.nreal += 1
        if self.nreal > self.cap:
            return None
        self._deps(e, reads, writes)
        ins = fn(self.eng[e])
        if self.ccnt[e] >= self.LIMIT:
            self._newsem(e)
        self.ccnt[e] += 1
        sem = self.csem[e]
        ins.then_inc(sem, 1)
        self.ninstr += 1
        tk = (sem, self.ccnt[e])
        self._mark(tk, reads, writes)
        return tk

    def dma(self, e, out, in_, reads=(), writes=(), join=False, **kw):
        self.nreal += 1
        if self.nreal > self.cap:
            return None
        self._deps(e, reads, writes, join)
        i = self.dnext[e]
        self.dnext[e] = (i + 1) % len(self.dsems[e])
        sem = self.dsems[e][i]
        if self.dcnt[e][i] > 0:
            self._wait(e, (sem, self.dcnt[e][i]))
        self.dcnt[e][i] += 16
        self.eng[e].dma_start(out=out, in_=in_, **kw).then_inc(sem, 16)
        self.ninstr += 1
        tk = (sem, self.dcnt[e][i])
        self._mark(tk, reads, writes, join)
        return tk

    def barrier(self):
        for e in self.eng:
            for e2 in self.eng:
                if e2 != e and self.ccnt[e2] > 0:
                    self._wait(e, (self.csem[e2], self.ccnt[e2]))
            for q in self.dsems:
                for i, sem in enumerate(self.dsems[q]):
                    if self.dcnt[q][i] > 0:
                        self._wait(e, (sem, self.dcnt[q][i]))


class Scope:
    cnt = [0]

    def __init__(self, nc):
        self.nc = nc
        self.es = ExitStack()

    def sb(self, name, shape, dt):
        Scope.cnt[0] += 1
        return self.es.enter_context(self.nc.sbuf_tensor(f"{name}_{Scope.cnt[0]}", list(shape), dt))

    def psum(self, name, shape, dt):
        Scope.cnt[0] += 1
        return self.es.enter_context(self.nc.psum_tensor(f"{name}_{Scope.cnt[0]}", list(shape), dt))

    def close(self):
        self.es.close()


def alloc_psum(C, sc, nps, npt):
    C.pt = [sc.psum("pt", [128, 8, 128], BF16) for _ in range(npt)]
    C.bpt = bufs(npt)
    C.npt = npt
    C.pti = 0
    C.ps = [sc.psum("ps", [128, 512], F32) for _ in range(nps)]
    C.bps = bufs(nps)


def _rope_tables(dh, reps):
    half = dh // 2
    inv = (1.0 / (np.float32(10000.0) ** (np.arange(0, dh, 2, dtype=np.float32) / np.float32(dh)))).astype(np.float32)
    ang = np.arange(T, dtype=np.float32)[None, :] * inv[:, None]
    c = np.cos(ang).astype(np.float32)
    s = np.sin(ang).astype(np.float32)
    cos = np.concatenate([c, c], 0)
    sin = np.concatenate([-s, s], 0)
    return np.tile(cos, (reps, 1)).copy(), np.tile(sin, (reps, 1)).copy()


def _dil_masks():
    m = np.zeros((20, 128, 512), np.float32)
    k = np.arange(128)[:, None]
    q = np.arange(512)[None, :]
    for i in range(20):
        d = (i * 128 - 1024) + k - q
        ad = np.abs(d)
        m[i] = (ad <= 64).astype(np.float32) + ((d % 4 == 0) & (ad <= 256)) + ((d % 16 == 0) & (ad <= 1024))
    return m.astype(ml_dtypes.bfloat16)


def _na_onehot():
    L = np.zeros((31, 64, 128), np.float32)
    for cq in range(64):
        cs = min(max(cq - 8, 0), 48)
        for ck in range(cs, cs + 16):
            b = ck - cq + 15
            L[b, cq, ck] = 1.0
            L[b, cq, ck + 64] = 1.0
    return L.reshape(31, 64 * 128)


def _ret_consts():
    k = np.arange(128, dtype=np.float32)[:, None]
    q = np.arange(128, dtype=np.float32)[None, :]
    c = np.zeros((8, 128, 128), np.float32)
    c[0] = np.maximum(q - k, 0)
    c[1] = (q >= k) / 16.0
    c[2] = np.maximum(k - q, 0)
    c[3] = (k > q) / 16.0
    c[4] = np.broadcast_to(q + 1.0, (128, 128))
    c[5] = np.broadcast_to(128.0 - q, (128, 128))
    c[6, :, 0] = 127.0 - k[:, 0]
    c[6, :, 1] = k[:, 0]
    c[6, :, 2] = 128.0
    return c


_CONSTS = {}


def _consts():
    if not _CONSTS:
        cos0, sin0 = _rope_tables(64, 2)
        cos1, sin1 = _rope_tables(256, 1)
        _CONSTS.update(
            ident=np.eye(128, dtype=np.float32).astype(ml_dtypes.bfloat16),
            cos0=cos0, sin0=sin0,
            cos1=np.ascontiguousarray(cos1[:128]), sin1=np.ascontiguousarray(sin1[128:]),
            dmask=_dil_masks(), naL=_na_onehot(), retc=_ret_consts(),
        )
    return _CONSTS


class Ctx:
    pass


def build(nseq, upto=99, debug=False):
    nc = bass.Bass("TRN2", target_bir_lowering=False)
    S = Sched(nc)
    C = Ctx()
    C.nc, C.S, C.nseq = nc, S, nseq
    NTOK = nseq * T

    def din(name, shape, dt=F32):
        return nc.dram_tensor(name, list(shape), dt, kind="ExternalInput").ap()

    def dscr(name, shape, dt, out=False):
        return nc.dram_tensor(name, list(shape), dt, kind="ExternalOutput" if (out or (debug and name in debug)) else "Internal").ap()

    C.x = din("x", [NTOK, D])
    C.w_in0 = din("w_in0", [D, 4096])
    C.w_out0 = din("w_out0", [D, D])
    C.w_up = [din(f"w_up{l}", [D, 2 * FF]) for l in range(2)]
    C.w_dn = [din(f"w_dn{l}", [FF, D]) for l in range(2)]
    C.cwT = [din(f"cwT{l}", [128, 44 * 4]) for l in range(2)]
    C.w_in1 = din("w_in1", [D, 6144])
    C.w_out1 = din("w_out1", [2048, D])
    C.norms = din("norms", [5, D])
    C.rpbT = din("rpbT", [31, 8 * 15])
    C.decay = din("decay", [8])
    C.ident = din("ident", [128, 128], BF16)
    C.cos0 = din("cos0", [128, T])
    C.sin0 = din("sin0", [128, T])
    C.cos1 = din("cos1", [128, T])
    C.sin1 = din("sin1", [128, T])
    C.dmask = din("dmask", [20, 128, 512], BF16)
    C.naL = din("naL", [31, 64 * 128])
    C.retc = din("retc", [8, 128, 128])

    C.y = dscr("y", [NTOK, D], F32, out=True)
    C.QK0 = dscr("QK0", [nseq, 16, 128, T], BF16)
    C.V0 = dscr("V0", [NTOK, D], BF16)
    C.OT0 = dscr("OT0", [nseq, 8, 128, T], BF16)
    C.X1 = dscr("X1", [NTOK, D], F32)
    C.X2 = dscr("X2", [NTOK, D], F32)
    C.X3 = dscr("X3", [NTOK, D], F32)
    C.HTa = dscr("HTa", [nseq, 8, 128, T], BF16)
    C.HTb = dscr("HTb", [nseq, 8, 128, T], BF16)
    C.QTR = dscr("QTR", [nseq, 8, 128, T], BF16)
    C.KTR = dscr("KTR", [nseq, 8, 128, T], BF16)
    C.KTM = dscr("KTM", [NTOK, D], BF16)
    C.VR = dscr("VR", [NTOK, 2048], BF16)
    C.SG = dscr("SG", [NTOK, 2048], BF16)
    C.RF = dscr("RF", [NTOK, 2048], F32)
    C.TAZ = dscr("TAZ", [8, 2, 128, 32 * 64], F32)

    C.idt = nc.alloc_sbuf_tensor("idt", [128, 128], BF16)
    C.bidt = Buf()
    S.dma("sp", C.idt[:], C.ident[:, :], writes=[C.bidt])

    phases = [p1_inproj0, p2_attn, p3_outproj0,
              lambda c: p4_ffn(c, 0), p5_inproj1, p6_retention, lambda c: p4_ffn(c, 1)]
    for i, ph in enumerate(phases):
        if i >= upto:
            break
        ph(C)
        S.barrier()
    S.barrier()
    return nc, S


def load_w(C, sc, name, w_dram, kc, f, eng="pool"):
    t = sc.sb(name, [128, kc, f], BF16)
    b = Buf()
    src = w_dram.rearrange("(k p) f -> p k f", p=128)
    step = max(1, 2048 // f) if f < 2048 else 1
    for k in range(0, kc, step):
        k2 = min(kc, k + step)
        C.S.dma(eng, t[:, k:k2, :], src[:, k:k2, :], writes=[b], join=(k > 0))
    return t, b


def load_grow(C, sc, idx):
    g = sc.sb("grow", [128, D], F32)
    b = Buf()
    C.S.dma("sp", g[:], C.norms[idx, :].partition_broadcast(128), writes=[b])
    return g, b


class NormBufs:
    def __init__(self, sc, n=2):
        self.n = n
        self.junk = [sc.sb("nj", [128, D], BF16) for _ in range(n)]
        self.ss = [sc.sb("nss", [128, 1], F32) for _ in range(n)]
        self.b = [bufs(2) for _ in range(n)]
        self.i = 0


def rmsnorm(C, nb, xt, bx, grow, bg, out, bout):
    S = C.S
    i = nb.i % nb.n
    nb.i += 1
    junk, ss, (bj, bs) = nb.junk[i], nb.ss[i], nb.b[i]
    S.op("act", lambda e: e.activation(out=junk[:], in_=xt, func=AF.Square, accum_out=ss[:]), reads=[bx], writes=[bj, bs])
    S.op("dve", lambda e: e.tensor_scalar(out=ss[:], in0=ss[:], scalar1=1.0 / D, scalar2=EPS, op0=ALU.mult, op1=ALU.add), reads=[bs], writes=[bs])
    S.op("act", lambda e: e.activation(out=ss[:], in_=ss[:], func=AF.Sqrt), reads=[bs], writes=[bs])
    S.op("dve", lambda e: e.reciprocal(out=ss[:], in_=ss[:]), reads=[bs], writes=[bs])
    S.op("dve", lambda e: e.scalar_tensor_tensor(out=out, in0=xt, scalar=ss[:, 0:1], in1=grow[:], op0=ALU.mult, op1=ALU.mult),
         reads=[bx, bs, bg], writes=[bout])


def transpose_to(C, src, bsrc, nk, dst_fn, bdst, evac_eng):
    S = C.S
    for g in range(0, nk, 8):
        h = C.pti % C.npt
        C.pti += 1
        n = min(8, nk - g)
        for k in range(g, g + n):
            S.op("pe", lambda e: e.transpose(out=C.pt[h][:, k - g, :], in_=src[:, k * 128:(k + 1) * 128], identity=C.idt[:]),
                 reads=[bsrc, C.bidt], writes=[C.bpt[h]])
        if evac_eng == "act":
            S.op("act", lambda e: e.copy(out=dst_fn(g, g + n), in_=C.pt[h][:, 0:n, :]), reads=[C.bpt[h]], writes=[bdst])
        else:
            S.op(evac_eng, lambda e: e.tensor_copy(out=dst_fn(g, g + n), in_=C.pt[h][:, 0:n, :]), reads=[C.bpt[h]], writes=[bdst])


def p1_inproj0(C):
    nc, S = C.nc, C.S
    sc = Scope(nc)
    alloc_psum(C, sc, 6, 2)
    W, bW = load_w(C, sc, "w0", C.w_in0, 8, 4096)
    grow, bg = load_grow(C, sc, 0)
    nb = NormBufs(sc)
    xt = [sc.sb("xt", [128, D], F32) for _ in range(2)]
    bxt = bufs(2)
    hb = [sc.sb("hb", [128, D], BF16) for _ in range(2)]
    bhb = bufs(2)
    hT = [sc.sb("hT", [128, 8, 512], BF16) for _ in range(2)]
    bhT = bufs(2)
    cs = [sc.sb("cs", [128, 2, 512], F32) for _ in range(2)]
    bcs = bufs(2)
    t1 = [sc.sb("t1", [128, 512], F32) for _ in range(2)]
    t2 = [sc.sb("t2", [128, 512], F32) for _ in range(2)]
    bt1, bt2 = bufs(2), bufs(2)
    ob = [sc.sb("ob", [128, 512], BF16) for _ in range(4)]
    bob = bufs(4)
    oi = 0
    pi = 0
    for s in range(C.nseq):
        for tb in range(NB):
            hTc, bhTc = hT[tb % 2], bhT[tb % 2]
            csc, bcsc = cs[tb % 2], bcs[tb % 2]
            S.dma("sp", csc[:, 0, :], C.cos0[:, tb * 512:(tb + 1) * 512], writes=[bcsc])
            S.dma("sp", csc[:, 1, :], C.sin0[:, tb * 512:(tb + 1) * 512], writes=[bcsc], join=True)
            for j in range(4):
                tt = tb * 4 + j
                r0 = s * T + tt * 128
                S.dma("sp", xt[tt % 2][:], C.x[r0:r0 + 128, :], writes=[bxt[tt % 2]])
                rmsnorm(C, nb, xt[tt % 2][:], bxt[tt % 2], grow, bg, hb[tt % 2][:], bhb[tt % 2])
                transpose_to(C, hb[tt % 2], bhb[tt % 2], 8, lambda a, b: hTc[:, a:b, j * 128:(j + 1) * 128], bhTc, "act")
            for ft in list(range(8)) + list(range(12, 20)):
                p = pi % 4
                pi += 1
                for k in range(8):
                    S.op("pe", lambda e: e.matmul(C.ps[p][:], lhsT=W[:, k, ft * 128:(ft + 1) * 128], rhs=hTc[:, k, :], start=(k == 0), stop=(k == 7)),
                         reads=[bW, bhTc], writes=[C.bps[p]])
                o, bo = ob[oi % 4], bob[oi % 4]
                oi += 1
                if ft < 8:
                    S.op("act", lambda e: e.copy(out=o[:], in_=C.ps[p][:]), reads=[C.bps[p]], writes=[bo])
                    dst = ft
                else:
                    p2 = pi % 4
                    pi += 1
                    fs = ft + 12
                    for k in range(8):
                        S.op("pe", lambda e: e.matmul(C.ps[p2][:], lhsT=W[:, k, fs * 128:(fs + 1) * 128], rhs=hTc[:, k, :], start=(k == 0), stop=(k == 7)),
                             reads=[bW, bhTc], writes=[C.bps[p2]])
                    a, ba = t1[oi % 2], bt1[oi % 2]
                    b, bb = t2[oi % 2], bt2[oi % 2]
                    S.op("dve", lambda e: e.tensor_tensor(out=a[:], in0=C.ps[p][:], in1=csc[:, 0, :], op=ALU.mult), reads=[C.bps[p], bcsc], writes=[ba])
                    S.op("dve", lambda e: e.tensor_tensor(out=b[:], in0=C.ps[p2][:], in1=csc[:, 1, :], op=ALU.mult), reads=[C.bps[p2], bcsc], writes=[bb])
                    S.op("pool", lambda e: e.tensor_tensor(out=o[:], in0=a[:], in1=b[:], op=ALU.add), reads=[ba, bb], writes=[bo])
                    dst = ft - 4
                S.dma("sp", C.QK0[s, dst, :, tb * 512:(tb + 1) * 512], o[:], reads=[bo])
            for j in range(4):
                r0 = s * T + (tb * 4 + j) * 128
                for ci, c0 in enumerate((1024, 2560)):
                    p = pi % 4
                    pi += 1
                    for k in range(8):
                        S.op("pe", lambda e: e.matmul(C.ps[p][:], lhsT=hTc[:, k, j * 128:(j + 1) * 128], rhs=W[:, k, c0:c0 + 512], start=(k == 0), stop=(k == 7)),
                             reads=[bW, bhTc], writes=[C.bps[p]])
                    o, bo = ob[oi % 4], bob[oi % 4]
                    oi += 1
                    if ci == 0:
                        S.op("act", lambda e: e.copy(out=o[:], in_=C.ps[p][:]), reads=[C.bps[p]], writes=[bo])
                    else:
                        S.op("dve", lambda e: e.tensor_copy(out=o[:], in_=C.ps[p][:]), reads=[C.bps[p]], writes=[bo])
                    S.dma("sp", C.V0[r0:r0 + 128, ci * 512:(ci + 1) * 512], o[:], reads=[bo])
    S.barrier()
    sc.close()


def _na_valid(rq, rk):
    rs = min(max(rq - 4, 0), 56)
    return rs <= rk < rs + 8


def p2_attn(C):
    nc, S = C.nc, C.S
    sc = Scope(nc)
    alloc_psum(C, sc, 8, 0)
    sc2 = Scope(nc)
    L = sc2.sb("naL", [31, 64 * 128], F32)
    PT = sc2.sb("naPT", [31, 8 * 15], F32)
    Z = [sc2.sb("naZ", [128, 2, 32 * 64], F32) for _ in range(2)]
    bL, bPT, bZ = Buf(), Buf(), bufs(2)
    for c in range(0, 64 * 128, 2048):
        S.dma("sp", L[:, c:c + 2048], C.naL[:, c:c + 2048], writes=[bL], join=(c > 0))
    S.dma("sp", PT[:], C.rpbT[:, :], writes=[bPT])
    S.op("act", lambda e: e.activation(out=PT[:], in_=PT[:], func=AF.Exp), reads=[bPT], writes=[bPT])
    for z in range(2):
        S.op("pool", lambda e: e.memset(Z[z][:], 0.0), writes=[bZ[z]])
    for h in range(8):
        z = h % 2
        for half in range(2):
            p = (h * 2 + half) % 4
            pv = C.ps[p][:].rearrange("p (s c) -> p s c", c=64)
            ns = 8 if half == 0 else 7
            for cq in range(64):
                S.op("pe", lambda e: e.matmul(pv[:, 0:ns, cq], lhsT=L[:, cq * 128:(cq + 1) * 128], rhs=PT[:, h * 15 + half * 8:h * 15 + half * 8 + ns],
                                              start=True, stop=True), reads=[bL, bPT], writes=[C.bps[p]])
            a0 = (8 + half * 8) * 64
            S.op("dve", lambda e: e.tensor_copy(out=Z[z][0:64, 0, a0:a0 + ns * 64], in_=C.ps[p][0:64, 0:ns * 64]), reads=[C.bps[p]], writes=[bZ[z]])
            S.op("dve", lambda e: e.tensor_copy(out=Z[z][64:128, 0, a0 + 64:a0 + 64 + ns * 64], in_=C.ps[p][64:128, 0:ns * 64]), reads=[C.bps[p]], writes=[bZ[z]])
            i0, i1 = (4, 8) if half == 0 else (0, 4)
            S.op("dve", lambda e: e.tensor_copy(out=Z[z][0:64, 1, a0 + i0 * 64:a0 + i1 * 64], in_=C.ps[p][0:64, i0 * 64:i1 * 64]), reads=[C.bps[p]], writes=[bZ[z]])
            S.op("dve", lambda e: e.tensor_copy(out=Z[z][64:128, 1, a0 + 64 + i0 * 64:a0 + 64 + i1 * 64], in_=C.ps[p][64:128, i0 * 64:i1 * 64]), reads=[C.bps[p]], writes=[bZ[z]])
        S.dma("sp", C.TAZ[h, :, :, :].rearrange("v p c -> p v c"), Z[z][:], reads=[bZ[z]])
    S.barrier()
    sc2.close()
    TAzI = [sc.sb("TAzI", [128, 2, 32 * 64], F32) for _ in range(2)]
    bTAI = bufs(2)
    TAzF = sc.sb("TAzF", [128, 2, 32 * 64], F32)
    bTAF = Buf()

    DM = sc.sb("dmask", [128, 20, 512], BF16)
    bDM = Buf()
    for i in range(0, 20, 4):
        S.dma("sp", DM[:, i:i + 4, :], C.dmask[i:i + 4].rearrange("i p q -> p i q"), writes=[bDM], join=(i > 0))
    LA = 8
    NST = 4
    Qz = [[sc.sb("Qz", [128, T], BF16) for _ in range(2)] for _ in range(2)]
    KT = [sc.sb("KT", [128, T], BF16) for _ in range(2)]
    VA = [sc.sb("VA", [128, NT, 256], BF16) for _ in range(2)]
    bQ, bK, bV = bufs(2), bufs(2), bufs(2)
    for b_ in range(2):
        S.op("pool", lambda e: e.memset(Qz[b_][0][64:128, :], 0.0), writes=[bQ[b_]])
        S.op("pool", lambda e: e.memset(Qz[b_][1][0:64, :], 0.0), writes=[bQ[b_]])
        S.op("pool", lambda e: e.memset(VA[b_][:, :, 64:192], 1.0), writes=[bV[b_]])
    NE, NEM = 8, LA + 4
    E = [sc.sb("E", [128, 512], F32) for _ in range(NE)]
    bE = bufs(NE)
    Eb = [sc.sb("Eb", [128, 512], BF16) for _ in range(NE)]
    bEb = bufs(NE)
    Em = [sc.sb("Em", [128, 512], BF16) for _ in range(NEM)]
    bEm = bufs(NEM)
    rc = [sc.sb("rc", [128, 512], F32) for _ in range(2)]
    rs = [sc.sb("rs", [128, 512], F32) for _ in range(2)]
    brc, brs = bufs(2), bufs(2)
    oT = [sc.sb("oT", [128, 512], BF16) for _ in range(2)]
    boT = bufs(2)

    stA, stB = [], []
    pending = []

    def mk_load(s, g, gi):
        def f():
            b_ = gi % 2
            qt, kt = (g, 4 + g) if g < 4 else (8 + (g - 4), 12 + (g - 4))
            for c in range(0, T, 2048):
                S.dma("sp", Qz[b_][0][0:64, c:c + 2048], C.QK0[s, qt, 0:64, c:c + 2048], writes=[bQ[b_]], join=(c > 0))
                S.dma("sp", Qz[b_][1][64:128, c:c + 2048], C.QK0[s, qt, 64:128, c:c + 2048], writes=[bQ[b_]], join=True)
                S.dma("sp", KT[b_][:, c:c + 2048], C.QK0[s, kt, :, c:c + 2048], writes=[bK[b_]], join=(c > 0))
            if g < 4:
                for hh in range(2):
                    S.dma("sp", TAzI[b_][:, hh, :], C.TAZ[g * 2 + hh, 1, :, :], writes=[bTAI[b_]], join=(hh > 0))
            if g == 0:
                ldF(0)()
            vsrc = C.V0[s * T:(s + 1) * T, g * 128:(g + 1) * 128].rearrange("(t p) c -> p t c", p=128)
            for c in range(0, NT, 8):
                S.dma("sp", VA[b_][:, c:c + 8, 0:64], vsrc[:, c:c + 8, 0:64], writes=[bV[b_]], join=(c > 0))
                S.dma("sp", VA[b_][:, c:c + 8, 192:256], vsrc[:, c:c + 8, 64:128], writes=[bV[b_]], join=True)
        return f

    def ldF(g):
        def f():
            for hh in range(2):
                S.dma("sp", TAzF[:, hh, :], C.TAZ[g * 2 + hh, 0, :, :], writes=[bTAF], join=(hh > 0))
        return f

    def mk_tile(s, g, gi, qb, qi, hp, ki, kb, nk, ti, fin):
        isna = g < 4
        b_ = gi % 2
        Qg, Kg, Vg = Qz[b_][hp], KT[b_], VA[b_]
        bQg, bKg, bVg = bQ[b_], bK[b_], bV[b_]
        p = ti % NST
        e_, be_ = (E[ti % NE], bE[ti % NE]) if isna else (Eb[ti % NE], bEb[ti % NE])
        em, bem = Em[ti % NEM], bEm[ti % NEM]
        pA = NST + (qi % 2) * 2
        pB = NST + 1 + (qi % 2) * 2
        R = qb * 8

        def a():
            S.op("pe", lambda e: e.matmul(C.ps[p][:], lhsT=Kg[:, kb * 128:(kb + 1) * 128], rhs=Qg[:, qb * 512:(qb + 1) * 512], start=True, stop=True),
                 reads=[bKg, bQg], writes=[C.bps[p]])
            S.op("act", lambda e: e.activation(out=e_[:], in_=C.ps[p][:], func=AF.Exp, scale=0.125), reads=[C.bps[p]], writes=[be_])
            if isna:
                rk0 = kb * 2
                s0 = 7 - rk0 + R
                fast = all((_na_valid(R + f, rk0 + ph) == (4 <= s0 - ph + f <= 11)) for f in range(8) for ph in range(2))
                if fast:
                    S.op("dve" if (ti % 3) != 2 else "pool", lambda e: e.tensor_tensor(out=em[:], in0=e_[:], in1=TAzI[b_][:, hp, (s0 + 8) * 64:(s0 + 16) * 64], op=ALU.mult), reads=[be_, bTAI[b_]], writes=[bem])
                else:
                    assert qb in (0, 7)
                    for ph in range(2):
                        rk = rk0 + ph
                        fs = [f for f in range(8) if _na_valid(R + f, rk)]
                        pp = slice(ph * 64, (ph + 1) * 64)
                        if fs:
                            f1, f2 = fs[0], fs[-1]
                            assert fs == list(range(f1, f2 + 1))
                            c1 = (s0 + 8 + f1) * 64
                            n = f2 - f1 + 1
                            S.op("dve", lambda e: e.tensor_tensor(out=em[pp, f1 * 64:(f2 + 1) * 64], in0=e_[pp, f1 * 64:(f2 + 1) * 64],
                                                                in1=TAzF[pp, hp, c1:c1 + n * 64], op=ALU.mult), reads=[be_, bTAF], writes=[bem])
                            if f1 > 0:
                                S.op("pool", lambda e: e.memset(em[pp, 0:f1 * 64], 0.0), writes=[bem])
                            if f2 < 7:
                                S.op("pool", lambda e: e.memset(em[pp, (f2 + 1) * 64:512], 0.0), writes=[bem])
                        else:
                            S.op("pool", lambda e: e.memset(em[pp, :], 0.0), writes=[bem])
            else:
                mi = (kb * 128 - qb * 512 + 1024) // 128
                eng = "dve" if (ti % 3) != 2 else "pool"
                S.op(eng, lambda e: e.tensor_tensor(out=em[:], in0=e_[:], in1=DM[:, mi, :], op=ALU.mult), reads=[be_, bDM], writes=[bem])

        def b():
            first, last = ki == 0, ki == nk - 1
            pX = pA if hp == 0 else pB
            S.op("pe", lambda e: e.matmul(C.ps[pX][:], lhsT=Vg[:, kb, hp * 128:(hp + 1) * 128], rhs=em[:], start=first, stop=last),
                 reads=[bVg, bem], writes=[C.bps[pX]])
            if fin:
                j = qi % 2

                def f1():
                    S.op("act", lambda e: e.copy(out=rc[j][64:128, :], in_=C.ps[pA][64:128, :]), reads=[C.bps[pA]], writes=[brc[j]])
                    S.op("act", lambda e: e.copy(out=rc[j][0:64, :], in_=C.ps[pB][0:64, :]), reads=[C.bps[pB]], writes=[brc[j]])
                    S.dma("sp", rs[j][0:64, :], rc[j][64:128, :], reads=[brc[j]], writes=[brs[j]])
                    S.dma("sp", rs[j][64:128, :], rc[j][0:64, :], reads=[brc[j]], writes=[brs[j]])

                def f2():
                    S.op("dve", lambda e: e.reciprocal(out=rs[j][:], in_=rs[j][:]), reads=[brs[j]], writes=[brs[j]])

                def f3():
                    S.op("dve", lambda e: e.tensor_tensor(out=oT[j][0:64, :], in0=C.ps[pA][0:64, :], in1=rs[j][0:64, :], op=ALU.mult), reads=[C.bps[pA], brs[j]], writes=[boT[j]])

                def f4():
                    S.op("dve", lambda e: e.tensor_tensor(out=oT[j][64:128, :], in0=C.ps[pB][64:128, :], in1=rs[j][64:128, :], op=ALU.mult), reads=[C.bps[pB], brs[j]], writes=[boT[j]])
                    S.dma("sp", C.OT0[s, g, :, qb * 512:(qb + 1) * 512], oT[j][:], reads=[boT[j]])
                pending.append([2, f1])
                pending.append([7, f2])
                pending.append([9, f3])
                pending.append([10, f4])
        return a, b

    ti = 0
    qi = 0
    gi = 0
    pairs = [(s_, g_) for s_ in range(C.nseq) for g_ in range(8)]
    for pi_, (s, g) in enumerate(pairs):
        hooks = {}
        if pi_ == 0:
            hooks[0] = [mk_load(s, g, gi)]
        if pi_ + 1 < len(pairs):
            ns_, ng_ = pairs[pi_ + 1]
            hooks.setdefault(4, []).append(mk_load(ns_, ng_, gi + 1))
            if 1 <= ng_ <= 3:
                hooks.setdefault(2, []).append(ldF(ng_))
        qorder = [0, 7, 1, 2, 3, 4, 5, 6] if g < 4 else list(range(NB))
        for qn, qb in enumerate(qorder):
            if g < 4:
                R = qb * 8
                lo = min(max(R - 4, 0), 56)
                hi = min(max(R + 7 - 4, 0), 56) + 8
                kbs = list(range(lo // 2, (hi + 1) // 2))
            else:
                kbs = list(range(max(0, qb * 4 - 8), min(NT, qb * 4 + 4 + 8)))
            nk = len(kbs)
            for ki, kb in enumerate(kbs):
                for hp in range(2):
                    a, b = mk_tile(s, g, gi, qb, qi, hp, ki, kb, nk, ti, fin=(ki == nk - 1 and hp == 1))
                    if ki == 0 and hp == 0 and qn in hooks:
                        hk = hooks[qn]
                        a = (lambda hk=hk, a0=a: ([h() for h in hk], a0()))
                    stA.append(a)
                    stB.append(b)
                    ti += 1
            qi += 1
        gi += 1
    n = len(stA)
    for i in range(n + LA + 12):
        if i < n:
            stA[i]()
        if LA <= i < n + LA:
            stB[i - LA]()
        for pe_ in list(pending):
            pe_[0] -= 1
            if pe_[0] <= 0:
                pe_[1]()
                pending.remove(pe_)
    assert not pending
    S.barrier()
    sc.close()


def p3_outproj0(C):
    nc, S = C.nc, C.S
    sc = Scope(nc)
    alloc_psum(C, sc, 6, 2)
    W, bW = load_w(C, sc, "wo0", C.w_out0, 8, D)
    grow, bg = load_grow(C, sc, 1)
    nb = NormBufs(sc)
    oT = [sc.sb("oTb", [128, 8, 512], BF16) for _ in range(2)]
    boT = bufs(2)
    xt = [sc.sb("xt", [128, D], F32) for _ in range(2)]
    bxt = bufs(2)
    x1 = [sc.sb("x1", [128, D], F32) for _ in range(2)]
    bx1 = bufs(2)
    hb = [sc.sb("hb", [128, D], BF16) for _ in range(2)]
    bhb = bufs(2)
    hT = [sc.sb("hT", [128, 8, 128], BF16) for _ in range(2)]
    bhT = bufs(2)
    def load_o(s, tb):
        o, bo = oT[tb % 2], boT[tb % 2]
        for k in range(0, 8, 2):
            S.dma("sp", o[:, k:k + 2, :], C.OT0[s, k:k + 2, :, tb * 512:(tb + 1) * 512].rearrange("k p t -> p k t"), writes=[bo], join=(k > 0))

    pi = 0
    for s in range(C.nseq):
        for tb in range(NB):
            o, bo = oT[tb % 2], boT[tb % 2]
            if tb == 0:
                load_o(s, tb)
            if tb + 1 < NB:
                load_o(s, tb + 1)
            for j in range(4):
                tt = tb * 4 + j
                r0 = s * T + tt * 128
                i2 = tt % 2
                S.dma("sp", xt[i2][:], C.x[r0:r0 + 128, :], writes=[bxt[i2]])
                for c in range(2):
                    p = pi % 3
                    pi += 1
                    for k in range(8):
                        S.op("pe", lambda e: e.matmul(C.ps[p][:], lhsT=o[:, k, j * 128:(j + 1) * 128], rhs=W[:, k, c * 512:(c + 1) * 512], start=(k == 0), stop=(k == 7)),
                             reads=[bW, bo], writes=[C.bps[p]])
                    S.op("dve", lambda e: e.tensor_tensor(out=x1[i2][:, c * 512:(c + 1) * 512], in0=C.ps[p][:], in1=xt[i2][:, c * 512:(c + 1) * 512], op=ALU.add),
                         reads=[C.bps[p], bxt[i2]], writes=[bx1[i2]])
                S.dma("sp", C.X1[r0:r0 + 128, :], x1[i2][:], reads=[bx1[i2]])
                rmsnorm(C, nb, x1[i2][:], bx1[i2], grow, bg, hb[i2][:], bhb[i2])
                transpose_to(C, hb[i2], bhb[i2], 8, lambda a, b: hT[i2][:, a:b, :], bhT[i2], "act")
                S.dma("sp", C.HTa[s, :, :, tt * 128:(tt + 1) * 128].rearrange("k p t -> p k t"), hT[i2][:], reads=[bhT[i2]])
    S.barrier()
    sc.close()


def p4_ffn(C, layer):
    nc, S = C.nc, C.S
    sc = Scope(nc)
    alloc_psum(C, sc, 7, 1)
    HTin = C.HTa if layer == 0 else C.HTb
    Xin = C.X1 if layer == 0 else C.X3
    Wu, bWu = load_w(C, sc, "wu", C.w_up[layer], 8, 2 * FF)
    Wd, bWd = load_w(C, sc, "wd", C.w_dn[layer], 22, D)
    cw = sc.sb("cw", [128, 44, 4], F32)
    bcw = Buf()
    S.dma("sp", cw[:], C.cwT[layer].rearrange("p (t c) -> p t c", c=4), writes=[bcw])
    grow, bg = load_grow(C, sc, 2 if layer == 0 else 4)
    nb = NormBufs(sc, 1)
    hT = sc.sb("h2T", [128, 8, 514], BF16)
    bh = Buf()
    gT = sc.sb("gT", [128, 22, 512], BF16)
    bgT = bufs(22)
    ah = sc.sb("ahalo", [128, 2, 44], F32)
    bah = Buf()
    yus = [sc.sb("yu", [128, 512], F32) for _ in range(2)]
    ygs = [sc.sb("yg", [128, 512], F32) for _ in range(2)]
    ggs = [sc.sb("gg", [128, 512], F32) for _ in range(2)]
    byus, bygs, bggs = bufs(2), bufs(2), bufs(2)
    xt = sc.sb("xt", [128, D], F32)
    bxt = Buf()
    x2 = sc.sb("x2", [128, D], F32)
    bx2 = Buf()
    if layer == 0:
        hbs = [sc.sb("hb", [128, D], BF16) for _ in range(2)]
        h3T = sc.sb("h3T", [128, 8, 128], BF16)
        bhbs, bh3 = bufs(2), Buf()
    else:
        yo = sc.sb("yo", [128, D], F32)
        byo = Buf()
    S.op("pool", lambda e: e.memset(hT[:], 0.0), writes=[bh])
    def load_h(s, tb):
        t0 = tb * 512
        lo = max(t0 - 1, 0)
        hi = min(t0 + 513, T)
        c0 = lo - (t0 - 1)
        if tb == 0:
            S.op("pool", lambda e: e.memset(hT[:, :, 0:1], 0.0), writes=[bh])
        if tb == NB - 1:
            S.op("pool", lambda e: e.memset(hT[:, :, 513:514], 0.0), writes=[bh])
        for k in range(0, 8, 2):
            S.dma("sp", hT[:, k:k + 2, c0:c0 + (hi - lo)], HTin[s, k:k + 2, :, lo:hi].rearrange("k p t -> p k t"), writes=[bh], join=(k > 0))

    pi = 0
    deferred = []
    prev_s2 = [None]
    for s in range(C.nseq):
        for tb in range(NB):
            if tb == 0:
                load_h(s, tb)
            hal = hT[:, :, 0:514:513]
            for f0 in range(0, 44, 11):
                p = 6
                for f in range(f0, f0 + 11):
                    for k in range(8):
                        S.op("pe", lambda e: e.matmul(C.ps[p][:, (f - f0) * 2:(f - f0) * 2 + 2], lhsT=Wu[:, k, f * 128:(f + 1) * 128], rhs=hal[:, k, :],
                                                      start=(k == 0), stop=(k == 7)), reads=[bWu, bh], writes=[C.bps[p]])
                pv = C.ps[p][:, 0:22].rearrange("p (f c) -> p c f", c=2)
                S.op("dve", lambda e: e.tensor_tensor(out=ah[:, 0, f0:f0 + 11], in0=pv[:, 0, :], in1=cw[:, f0:f0 + 11, 0], op=ALU.mult), reads=[C.bps[p], bcw], writes=[bah])
                S.op("dve", lambda e: e.tensor_tensor(out=ah[:, 1, f0:f0 + 11], in0=pv[:, 1, :], in1=cw[:, f0:f0 + 11, 2], op=ALU.mult), reads=[C.bps[p], bcw], writes=[bah])
            for f in range(22):
                yu, yg, gg = yus[f % 2], ygs[f % 2], ggs[f % 2]
                byu, byg, bgg = byus[f % 2], bygs[f % 2], bggs[f % 2]
                for (ft, yt, byt) in ((f, yu, byu), (22 + f, yg, byg)):
                    p = pi % 4
                    pi += 1
                    for k in range(8):
                        S.op("pe", lambda e: e.matmul(C.ps[p][:], lhsT=Wu[:, k, ft * 128:(ft + 1) * 128], rhs=hT[:, k, 1:513], start=(k == 0), stop=(k == 7)),
                             reads=[bWu, bh], writes=[C.bps[p]])
                    A = C.ps[p]
                    S.op("act", lambda e: e.activation(out=yt[:], in_=A[:], func=AF.Identity, scale=cw[:, ft, 1:2], bias=cw[:, ft, 3:4]), reads=[C.bps[p], bcw], writes=[byt])
                    S.op("dve", lambda e: e.scalar_tensor_tensor(out=yt[:, 1:512], in0=A[:, 0:511], scalar=cw[:, ft, 0:1], in1=yt[:, 1:512], op0=ALU.mult, op1=ALU.add),
                         reads=[C.bps[p], bcw, byt], writes=[byt])
                    S.op("dve", lambda e: e.scalar_tensor_tensor(out=yt[:, 0:511], in0=A[:, 1:512], scalar=cw[:, ft, 2:3], in1=yt[:, 0:511], op0=ALU.mult, op1=ALU.add),
                         reads=[C.bps[p], bcw, byt], writes=[byt])
                    S.op("pool", lambda e: e.tensor_tensor(out=yt[:, 0:1], in0=yt[:, 0:1], in1=ah[:, 0, ft:ft + 1], op=ALU.add), reads=[byt, bah], writes=[byt])
                    S.op("pool", lambda e: e.tensor_tensor(out=yt[:, 511:512], in0=yt[:, 511:512], in1=ah[:, 1, ft:ft + 1], op=ALU.add), reads=[byt, bah], writes=[byt])
                def s2(f=f, yu=yu, yg=yg, gg=gg, byu=byu, byg=byg, bgg=bgg):
                    S.op("act", lambda e: e.activation(out=gg[:], in_=yg[:], func=AF.Gelu_apprx_tanh), reads=[byg], writes=[bgg])
                    S.op("pool", lambda e: e.tensor_tensor(out=gT[:, f, :], in0=yu[:], in1=gg[:], op=ALU.mult), reads=[byu, bgg], writes=[bgT[f]])
                if prev_s2[0] is not None:
                    prev_s2[0]()
                prev_s2[0] = s2
            prev_s2[0]()
            prev_s2[0] = None
            if tb + 1 < NB:
                load_h(s, tb + 1)
            for j in range(4):
                tt = tb * 4 + j
                r0 = s * T + tt * 128
                S.dma("sp", xt[:], Xin[r0:r0 + 128, :], writes=[bxt])
                for c in range(2):
                    p = 4 + (pi % 2)
                    pi += 1
                    for f in range(22):
                        S.op("pe", lambda e: e.matmul(C.ps[p][:], lhsT=gT[:, f, j * 128:(j + 1) * 128], rhs=Wd[:, f, c * 512:(c + 1) * 512], start=(f == 0), stop=(f == 21)),
                             reads=[bWd, bgT[f]], writes=[C.bps[p]])
                    S.op("dve", lambda e: e.tensor_tensor(out=x2[:, c * 512:(c + 1) * 512], in0=C.ps[p][:], in1=xt[:, c * 512:(c + 1) * 512], op=ALU.add),
                         reads=[C.bps[p], bxt], writes=[bx2])
                if layer == 0:
                    S.dma("sp", C.X2[r0:r0 + 128, :], x2[:], reads=[bx2])
                    hb, bhb = hbs[tt % 2], bhbs[tt % 2]
                    rmsnorm(C, nb, x2[:], bx2, grow, bg, hb[:], bhb)

                    def fin(hb=hb, bhb=bhb, tt=tt, s=s):
                        transpose_to(C, hb, bhb, 8, lambda a, b: h3T[:, a:b, :], bh3, "act")
                        S.dma("sp", C.HTb[s, :, :, tt * 128:(tt + 1) * 128].rearrange("k p t -> p k t"), h3T[:], reads=[bh3])
                    deferred.append(fin)
                    if len(deferred) > 1:
                        deferred.pop(0)()
                else:
                    rmsnorm(C, nb, x2[:], bx2, grow, bg, yo[:], byo)
                    S.dma("sp", C.y[r0:r0 + 128, :], yo[:], reads=[byo])
            while deferred:
                deferred.pop(0)()
    S.barrier()
    sc.close()


def p5_inproj1(C):
    nc, S = C.nc, C.S
    sc = Scope(nc)
    alloc_psum(C, sc, 6, 2)
    W, bW = load_w(C, sc, "w1", C.w_in1, 8, 6144)
    hT = [sc.sb("h3T", [128, 8, 512], BF16) for _ in range(2)]
    bh = bufs(2)
    cs = [sc.sb("cs1", [128, 2, 512], F32) for _ in range(2)]
    bcs = bufs(2)
    tm = [sc.sb("tm", [128, 512], F32) for _ in range(4)]
    btm = bufs(4)
    ob = [sc.sb("ob", [128, 512], BF16) for _ in range(4)]
    bob = bufs(4)
    kr = [sc.sb("kr", [128, 2, 512], BF16) for _ in range(2)]
    bkr = bufs(2)
    ktm = [sc.sb("ktm", [128, 256], BF16) for _ in range(2)]
    bktm = bufs(2)
    def load_in(s, tb):
        h, bhc = hT[tb % 2], bh[tb % 2]
        csc, bcsc = cs[tb % 2], bcs[tb % 2]
        for k in range(0, 8, 2):
            S.dma("sp", h[:, k:k + 2, :], C.HTb[s, k:k + 2, :, tb * 512:(tb + 1) * 512].rearrange("k p t -> p k t"), writes=[bhc], join=(k > 0))
        S.dma("sp", csc[:, 0, :], C.cos1[:, tb * 512:(tb + 1) * 512], writes=[bcsc])
        S.dma("sp", csc[:, 1, :], C.sin1[:, tb * 512:(tb + 1) * 512], writes=[bcsc], join=True)

    oi = 0
    pi = 0
    ki = 0
    for s in range(C.nseq):
        for tb in range(NB):
            h, bhc = hT[tb % 2], bh[tb % 2]
            csc, bcsc = cs[tb % 2], bcs[tb % 2]
            if tb == 0:
                load_in(s, tb)
            if tb + 1 < NB:
                load_in(s, tb + 1)
            for qk in range(2):
                for hd in range(4):
                    pa, pb = pi % 4, (pi + 1) % 4
                    pi += 2
                    for (p, c0) in ((pa, qk * 1024 + hd * 256), (pb, qk * 1024 + hd * 256 + 128)):
                        for k in range(8):
                            S.op("pe", lambda e: e.matmul(C.ps[p][:], lhsT=W[:, k, c0:c0 + 128], rhs=h[:, k, :], start=(k == 0), stop=(k == 7)),
                                 reads=[bW, bhc], writes=[C.bps[p]])
                    outs = []
                    for half in range(2):
                        ta, bta = tm[(oi * 2) % 4], btm[(oi * 2) % 4]
                        tb_, btb = tm[(oi * 2 + 1) % 4], btm[(oi * 2 + 1) % 4]
                        o, bo = ob[oi % 4], bob[oi % 4]
                        oi += 1
                        S.op("dve", lambda e: e.tensor_tensor(out=ta[:], in0=C.ps[pa][:], in1=csc[:, half, :], op=ALU.mult), reads=[C.bps[pa], bcsc], writes=[bta])
                        S.op("dve", lambda e: e.tensor_tensor(out=tb_[:], in0=C.ps[pb][:], in1=csc[:, 1 - half, :], op=ALU.mult), reads=[C.bps[pb], bcsc], writes=[btb])
                        if qk == 0:
                            S.op("pool", lambda e: e.tensor_tensor(out=o[:], in0=ta[:], in1=tb_[:], op=(ALU.subtract if half == 0 else ALU.add)), reads=[bta, btb], writes=[bo])
                            S.dma("sp", C.QTR[s, hd * 2 + half, :, tb * 512:(tb + 1) * 512], o[:], reads=[bo])
                        else:
                            kk, bkk = kr[ki % 2], bkr[ki % 2]
                            S.op("pool", lambda e: e.tensor_tensor(out=kk[:, half, :], in0=ta[:], in1=tb_[:], op=(ALU.subtract if half == 0 else ALU.add)), reads=[bta, btb], writes=[bkk])
                            S.dma("sp", C.KTR[s, hd * 2 + half, :, tb * 512:(tb + 1) * 512], kk[:, half, :], reads=[bkk])
                    if qk == 1:
                        kk, bkk = kr[ki % 2], bkr[ki % 2]
                        ki += 1
                        for j in range(4):
                            r0 = s * T + (tb * 4 + j) * 128
                            kt_, bkt_ = ktm[j % 2], bktm[j % 2]
                            hh = C.pti % C.npt
                            C.pti += 1
                            for half in range(2):
                                S.op("pe", lambda e: e.transpose(out=C.pt[hh][:, half, :], in_=kk[:, half, j * 128:(j + 1) * 128], identity=C.idt[:]),
                                     reads=[bkk, C.bidt], writes=[C.bpt[hh]])
                            S.op("act", lambda e: e.copy(out=kt_[:].rearrange("p (a b) -> p a b", a=2), in_=C.pt[hh][:, 0:2, :]), reads=[C.bpt[hh]], writes=[bkt_])
                            S.dma("sp", C.KTM[r0:r0 + 128, hd * 256:(hd + 1) * 256], kt_[:], reads=[bkt_])
            for j in range(4):
                r0 = s * T + (tb * 4 + j) * 128
                for c in range(8):
                    p = pi % 4
                    pi += 1
                    c0 = 2048 + c * 512
                    for k in range(8):
                        S.op("pe", lambda e: e.matmul(C.ps[p][:], lhsT=h[:, k, j * 128:(j + 1) * 128], rhs=W[:, k, c0:c0 + 512], start=(k == 0), stop=(k == 7)),
                             reads=[bW, bhc], writes=[C.bps[p]])
                    o, bo = ob[oi % 4], bob[oi % 4]
                    oi += 1
                    if c < 4:
                        S.op("dve", lambda e: e.tensor_copy(out=o[:], in_=C.ps[p][:]), reads=[C.bps[p]], writes=[bo])
                        S.dma("sp", C.VR[r0:r0 + 128, c * 512:(c + 1) * 512], o[:], reads=[bo])
                    else:
                        S.op("act", lambda e: e.activation(out=o[:], in_=C.ps[p][:], func=AF.Silu), reads=[C.bps[p]], writes=[bo])
                        S.dma("sp", C.SG[r0:r0 + 128, (c - 4) * 512:(c - 3) * 512], o[:], reads=[bo])
    S.barrier()
    sc.close()


def p6_retention(C):
    nc, S = C.nc, C.S
    sc = Scope(nc)
    alloc_psum(C, sc, 0, 1)
    pin = sc.psum("pin", [128, 512], F32)
    bpin = Buf()
    pout = [sc.psum("pout", [128, 512], F32) for _ in range(2)]
    bpout = bufs(2)
    pst = [sc.psum("pst", [128, 2, 512], F32) for _ in range(2)]
    bpst = bufs(2)
    Wo, bWo = load_w(C, sc, "wo1", C.w_out1, 16, D)
    grow, bg = load_grow(C, sc, 3)
    nb = NormBufs(sc, 1)
    lg = sc.sb("lg", [128, 8], F32)
    DT = sc.sb("DT", [128, 8, 128], F32)
    KD = sc.sb("KD", [128, 8], F32)
    SD = sc.sb("SD", [128, 8], F32)
    QDT = sc.sb("QDT", [128, 2, 8, 128], F32)
    KDT = sc.sb("KDT", [128, 2, 1024], F32)
    sc2 = Scope(nc)
    rc = sc2.sb("retc", [128, 7, 128], F32)
    QD = sc2.sb("QD", [128, 8, 128], F32)
    onesf = sc2.sb("onesf", [128, 256], F32)
    brc = Buf()
    S.dma("sp", rc[:], C.retc[0:7].rearrange("c p q -> p c q"), writes=[brc])
    blg = Buf()
    S.dma("sp", lg[:], C.decay.partition_broadcast(128), writes=[blg])
    S.op("act", lambda e: e.activation(out=lg[:], in_=lg[:], func=AF.Exp), reads=[blg], writes=[blg])
    S.op("act", lambda e: e.activation(out=lg[:], in_=lg[:], func=AF.Ln, bias=1.0), reads=[blg], writes=[blg])
    S.op("dve", lambda e: e.tensor_scalar(out=lg[:], in0=lg[:], scalar1=-1.0, scalar2=None, op0=ALU.mult), reads=[blg], writes=[blg])
    bDT, bQD, bKD, bSD = Buf(), Buf(), Buf(), Buf()
    for d in range(2):
        for hd in range(4):
            i = d * 4 + hd
            S.op("act", lambda e: e.activation(out=DT[:, i, :], in_=rc[:, 2 * d, :], func=AF.Exp, scale=lg[:, i:i + 1]), reads=[brc, blg], writes=[bDT])
            S.op("dve", lambda e: e.tensor_tensor(out=DT[:, i, :], in0=DT[:, i, :], in1=rc[:, 2 * d + 1, :], op=ALU.mult), reads=[brc, bDT], writes=[bDT])
            S.op("act", lambda e: e.activation(out=QD[:, i, :], in_=rc[:, 4 + d, :], func=AF.Exp, scale=lg[:, i:i + 1]), reads=[brc, blg], writes=[bQD])
            S.op("dve", lambda e: e.tensor_scalar(out=QD[:, i, :], in0=QD[:, i, :], scalar1=1.0 / 16, scalar2=None, op0=ALU.mult), reads=[bQD], writes=[bQD])
            S.op("act", lambda e: e.activation(out=KD[:, i:i + 1], in_=rc[:, 6, d:d + 1], func=AF.Exp, scale=lg[:, i:i + 1]), reads=[brc, blg], writes=[bKD])
            S.op("act", lambda e: e.activation(out=SD[:, i:i + 1], in_=rc[:, 6, 2:3], func=AF.Exp, scale=lg[:, i:i + 1]), reads=[brc, blg], writes=[bSD])

    bQDT, bKDT, bof = Buf(), Buf(), Buf()
    S.op("pool", lambda e: e.memset(onesf[:], 1.0), writes=[bof])
    for d in range(2):
        for hd in range(4):
            i = d * 4 + hd
            for t in range(2):
                S.op("pool", lambda e: e.tensor_copy(out=QDT[:, d, hd * 2 + t, :], in_=QD[:, i, :]), reads=[bQD], writes=[bQDT])
            S.op("dve", lambda e: e.tensor_scalar(out=KDT[:, d, hd * 256:(hd + 1) * 256], in0=onesf[:], scalar1=KD[:, i:i + 1], scalar2=None, op0=ALU.mult),
                 reads=[bof, bKD], writes=[bKDT])

    S.barrier()
    sc2.close()

    S32 = sc.sb("S32", [128, 4, 2, 512], F32)
    Sbf = sc.sb("Sbf", [128, 4, 2, 512], BF16)
    bS32, bSbf = bufs(4), bufs(4)
    qT = [sc.sb("qT", [128, 8, 128], BF16) for _ in range(4)]
    kT = [sc.sb("kT", [128, 8, 128], BF16) for _ in range(4)]
    kM = [sc.sb("kM", [128, D], BF16) for _ in range(4)]
    vM = [sc.sb("vM", [128, 2048], BF16) for _ in range(4)]
    bq, bk, bkm, bv = bufs(4), bufs(4), bufs(4), bufs(4)
    iT = [sc.sb("iT", [128, 4, 128], BF16) for _ in range(4)]
    biT = bufs(4)
    qd = [sc.sb("qd", [128, 8, 128], BF16) for _ in range(4)]
    bqd = bufs(4)
    kd = [sc.sb("kd", [128, 1024], BF16) for _ in range(4)]
    bkd = bufs(4)
    rf = [sc.sb("rf", [128, 512], F32) for _ in range(4)]
    brf = bufs(4)
    rfl = [sc.sb("rfl", [128, 2048], F32) for _ in range(2)]
    brfl = bufs(2)
    sgl = [sc.sb("sgl", [128, 2048], BF16) for _ in range(2)]
    bsgl = bufs(2)
    rr = [sc.sb("rr", [128, 512], F32) for _ in range(2)]
    brr = bufs(2)
    st = [sc.sb("st", [128, 8], F32) for _ in range(2)]
    bst = bufs(2)
    rg = sc.sb("rg", [128, 2048], BF16)
    brg = bufs(4)
    rgT = sc.sb("rgT", [128, 16, 128], BF16)
    brgT = Buf()
    xt = [sc.sb("xt", [128, D], F32) for _ in range(2)]
    bxt = bufs(2)
    x3 = sc.sb("x3", [128, D], F32)
    bx3 = Buf()
    hb = sc.sb("hb", [128, D], BF16)
    bhb = Buf()
    h4T = sc.sb("h4T", [128, 8, 128], BF16)
    bh4 = Buf()
    cnt = dict(r=0, o=0)
    late = []

    def mk_chunk(s, d, n, c, ci):
        r0 = s * T + c * 128
        j = ci % 4
        q_, k_, km_, v_ = qT[j], kT[j], kM[j], vM[j]

        def a():
            S.dma("sp", q_[:], C.QTR[s, :, :, c * 128:(c + 1) * 128].rearrange("k p t -> p k t"), writes=[bq[j]])
            S.dma("sp", k_[:], C.KTR[s, :, :, c * 128:(c + 1) * 128].rearrange("k p t -> p k t"), writes=[bk[j]])
            S.dma("sp", km_[:], C.KTM[r0:r0 + 128, :], writes=[bkm[j]])
            S.dma("sp", v_[:], C.VR[r0:r0 + 128, :], writes=[bv[j]])

        def ac():
            for hd in range(4):
                for i in range(2):
                    S.op("pe", lambda e: e.matmul(pin[:, hd * 128:(hd + 1) * 128], lhsT=k_[:, hd * 2 + i, :], rhs=q_[:, hd * 2 + i, :], start=(i == 0), stop=(i == 1)),
                         reads=[bk[j], bq[j]], writes=[bpin])
            S.op("dve", lambda e: e.tensor_tensor(out=iT[j][:], in0=pin[:].rearrange("p (h q) -> p h q", h=4), in1=DT[:, d * 4:(d + 1) * 4, :], op=ALU.mult),
                 reads=[bpin, bDT], writes=[biT[j]])
            if n > 0:
                S.op("pool", lambda e: e.tensor_tensor(out=qd[j][:], in0=q_[:], in1=QDT[:, d, :, :], op=ALU.mult), reads=[bq[j], bQDT], writes=[bqd[j]])
            S.op("pool", lambda e: e.tensor_tensor(out=kd[j][:], in0=km_[:], in1=KDT[:, d, :], op=ALU.mult), reads=[bkm[j], bKDT], writes=[bkd[j]])

        j2 = ci % 2

        def a2():
            if d == 1:
                S.dma("sp", rfl[j2][:], C.RF[r0:r0 + 128, :], writes=[brfl[j2]])
                S.dma("sp", sgl[j2][:], C.SG[r0:r0 + 128, :], writes=[bsgl[j2]])
                S.dma("sp", xt[j2][:], C.X2[r0:r0 + 128, :], writes=[bxt[j2]])

        def b():
            for hd in range(4):
                if hd == 2 and late:
                    late.pop(0)()
                di = d * 4 + hd
                o_ = cnt["o"] % 2
                cnt["o"] += 1
                po, bpo = pout[o_], bpout[o_]
                ps_, bps_ = pst[o_], bpst[o_]
                vh = v_[:, hd * 512:(hd + 1) * 512]
                S.op("pe", lambda e: e.matmul(po[:], lhsT=iT[j][:, hd, :], rhs=vh, start=True, stop=(n == 0)), reads=[biT[j], bv[j]], writes=[bpo])
                if n > 0:
                    for i in range(2):
                        S.op("pe", lambda e: e.matmul(po[:], lhsT=qd[j][:, hd * 2 + i, :], rhs=Sbf[:, hd, i, :], start=False, stop=(i == 1)),
                             reads=[bqd[j], bSbf[hd]], writes=[bpo])
                rq = cnt["r"] % 4
                ri = cnt["r"] % 2
                cnt["r"] += 1
                if d == 0:
                    S.op("act", lambda e: e.copy(out=rf[rq][:], in_=po[:]), reads=[bpo], writes=[brf[rq]])
                    S.dma("sp", C.RF[r0:r0 + 128, hd * 512:(hd + 1) * 512], rf[rq][:], reads=[brf[rq]])
                else:
                    r_, br_ = rr[ri], brr[ri]
                    s_, bs_ = st[ri], bst[ri]
                    S.op("dve", lambda e: e.tensor_tensor(out=r_[:], in0=po[:], in1=rfl[j2][:, hd * 512:(hd + 1) * 512], op=ALU.add), reads=[bpo, brfl[j2]], writes=[br_])
                for i in range(2):
                    S.op("pe", lambda e: e.matmul(ps_[:, i, :], lhsT=kd[j][:, hd * 256 + i * 128:hd * 256 + (i + 1) * 128], rhs=vh, start=True, stop=True),
                         reads=[bkd[j], bv[j]], writes=[bps_])
                if n == 0:
                    S.op("dve", lambda e: e.tensor_copy(out=S32[:, hd, :, :], in_=ps_[:]), reads=[bps_], writes=[bS32[hd]])
                else:
                    S.op("dve", lambda e: e.scalar_tensor_tensor(out=S32[:, hd, :, :], in0=S32[:, hd, :, :], scalar=SD[:, di:di + 1], in1=ps_[:], op0=ALU.mult, op1=ALU.add),
                         reads=[bps_, bS32[hd], bSD], writes=[bS32[hd]])
                S.op("act", lambda e: e.copy(out=Sbf[:, hd, :, :], in_=S32[:, hd, :, :]), reads=[bS32[hd]], writes=[bSbf[hd]])
                if d == 1:
                    S.op("dve", lambda e: e.bn_stats(out=s_[:, 0:6], in_=r_[:]), reads=[br_], writes=[bs_])
                    S.op("dve", lambda e: e.bn_aggr(out=s_[:, 6:8], in_=s_[:, 0:6]), reads=[bs_], writes=[bs_])
                    S.op("dve", lambda e: e.tensor_scalar(out=s_[:, 7:8], in0=s_[:, 7:8], scalar1=EPS, scalar2=None, op0=ALU.add), reads=[bs_], writes=[bs_])
                    S.op("act", lambda e: e.activation(out=s_[:, 7:8], in_=s_[:, 7:8], func=AF.Sqrt), reads=[bs_], writes=[bs_])
                    S.op("dve", lambda e: e.reciprocal(out=s_[:, 7:8], in_=s_[:, 7:8]), reads=[bs_], writes=[bs_])
                    S.op("dve", lambda e: e.tensor_scalar(out=r_[:], in0=r_[:], scalar1=s_[:, 6:7], scalar2=s_[:, 7:8], op0=ALU.subtract, op1=ALU.mult), reads=[br_, bs_], writes=[br_])
                    S.op("pool", lambda e: e.tensor_tensor(out=rg[:, hd * 512:(hd + 1) * 512], in0=r_[:], in1=sgl[j2][:, hd * 512:(hd + 1) * 512], op=ALU.mult),
                         reads=[br_, bsgl[j2]], writes=[brg[hd]])
            if d == 1:
                for hd in range(4):
                    transpose_to(C, rg[:, hd * 512:(hd + 1) * 512], brg[hd], 4, lambda a_, b_: rgT[:, hd * 4 + a_:hd * 4 + b_, :], brgT, "act")
                for c2 in range(2):
                    o_ = cnt["o"] % 2
                    cnt["o"] += 1
                    po, bpo = pout[o_], bpout[o_]
                    for k in range(16):
                        S.op("pe", lambda e: e.matmul(po[:], lhsT=rgT[:, k, :], rhs=Wo[:, k, c2 * 512:(c2 + 1) * 512], start=(k == 0), stop=(k == 15)),
                             reads=[bWo, brgT], writes=[bpo])
                    S.op("dve", lambda e: e.tensor_tensor(out=x3[:, c2 * 512:(c2 + 1) * 512], in0=po[:], in1=xt[j2][:, c2 * 512:(c2 + 1) * 512], op=ALU.add),
                         reads=[bpo, bxt[j2]], writes=[bx3])
                S.dma("sp", C.X3[r0:r0 + 128, :], x3[:], reads=[bx3])
                rmsnorm(C, nb, x3[:], bx3, grow, bg, hb[:], bhb)

                def e2():
                    transpose_to(C, hb, bhb, 8, lambda a_, b_: h4T[:, a_:b_, :], bh4, "act")
                    S.dma("sp", C.HTb[s, :, :, c * 128:(c + 1) * 128].rearrange("k p t -> p k t"), h4T[:], reads=[bh4])
                late.append(e2)
        return a, ac, a2, b

    ci = 0
    for s in range(C.nseq):
        for d in range(2):
            chunks = list(range(NT)) if d == 0 else list(range(NT - 1, -1, -1))
            st_ = []
            for n, c in enumerate(chunks):
                st_.append(mk_chunk(s, d, n, c, ci))
                ci += 1
            for k in range(NT + 4):
                if k >= 4:
                    st_[k - 4][3]()
                if k < NT:
                    st_[k][0]()
                if 2 <= k < NT + 2:
                    st_[k - 2][1]()
                if 3 <= k < NT + 3:
                    st_[k - 3][2]()
            while late:
                late.pop(0)()
            S.barrier()
    S.barrier()
    sc.close()


def _prep_shared(inp):
    f = np.float32
    w_in = np.asarray(inp["even_w_in"], f)[0]
    def swap(cols):
        c = cols.reshape(D, 8, 2, 32)
        return c[:, :, ::-1, :].reshape(D, 512)
    dq = w_in[:, 1536:2048]
    dk = w_in[:, 2048:2560]
    w_in0 = np.ascontiguousarray(np.concatenate([w_in, swap(dq), swap(dk)], axis=1))
    cw = np.asarray(inp["ffn_conv_w"], f)
    cb = np.asarray(inp["ffn_conv_b"], f)
    cwT = []
    for l in range(2):
        a = np.concatenate([cw[l], cb[l][None]], 0)
        a = a.reshape(4, 44, 128).transpose(2, 1, 0)
        cwT.append(np.ascontiguousarray(a.reshape(128, 44 * 4)))
    norms = np.ascontiguousarray(np.concatenate([
        np.asarray(inp["attn_norm"], f)[0:1], np.asarray(inp["ffn_norm"], f)[0:1],
        np.asarray(inp["attn_norm"], f)[1:2], np.asarray(inp["ffn_norm"], f)[1:2],
        np.asarray(inp["final_norm"], f)[None]], 0))
    rpb = np.asarray(inp["na_rpb"], f)[0]
    rpbT = np.ascontiguousarray(rpb[:, ::-1, :].transpose(2, 0, 1).reshape(31, 8 * 15))
    decay = np.ascontiguousarray(np.concatenate([np.asarray(inp["ret_decay_fwd"], f)[0], np.asarray(inp["ret_decay_bwd"], f)[0]]))
    sh = dict(
        w_in0=w_in0, w_out0=np.ascontiguousarray(np.asarray(inp["even_w_out"], f)[0]),
        w_up0=np.ascontiguousarray(np.asarray(inp["ffn_w_up"], f)[0]), w_up1=np.ascontiguousarray(np.asarray(inp["ffn_w_up"], f)[1]),
        w_dn0=np.ascontiguousarray(np.asarray(inp["ffn_w_down"], f)[0]), w_dn1=np.ascontiguousarray(np.asarray(inp["ffn_w_down"], f)[1]),
        cwT0=cwT[0], cwT1=cwT[1],
        w_in1=np.ascontiguousarray(np.asarray(inp["ret_w_in"], f)[0]), w_out1=np.ascontiguousarray(np.asarray(inp["ret_w_out"], f)[0]),
        norms=norms, rpbT=rpbT, decay=decay,
    )
    sh.update(_consts())
    return sh


def _assign():
    seqs = [("p", i) for i in range(BATCH)] + [("s", i) for i in range(DEC_BATCH)]
    slots = [[] for _ in range(NCORES)]
    for i, sq in enumerate(seqs):
        slots[i % NCORES].append(sq)
    return slots


def kernel(**inputs):
    nseq = 3
    sh = _prep_shared(inputs)
    xp = np.asarray(inputs["x_prompt"], np.float32)
    xs = np.asarray(inputs["x_sample"], np.float32)
    slots = _assign()
    in_maps = []
    for c in range(NCORES):
        xc = np.zeros((nseq * T, D), np.float32)
        for j, (kind, i) in enumerate(slots[c]):
            xc[j * T:(j + 1) * T] = xp[i] if kind == "p" else xs[i]
        for j in range(len(slots[c]), nseq):
            xc[j * T:(j + 1) * T] = xc[0:T]
        m = dict(sh)
        m["x"] = xc
        in_maps.append(m)
    nc, _ = build(nseq)
    res = run_bass_kernel_spmd(nc, in_maps, core_ids=list(range(NCORES)))
    yp = np.zeros((BATCH, T, D), np.float32)
    ys = np.zeros((DEC_BATCH, T, D), np.float32)
    for c in range(NCORES):
        yc = np.asarray(res.results[c]["y"]).reshape(nseq, T, D)
        for j, (kind, i) in enumerate(slots[c]):
            if kind == "p":
                yp[i] = yc[j]
            else:
                ys[i] = yc[j]
    return (yp, ys)
```

```python
import math
from contextlib import ExitStack

import numpy as np
import ml_dtypes
import concourse.bass as bass
import concourse.mybir as mybir
from concourse.bass_utils import run_bass_kernel_spmd

F32 = mybir.dt.float32
BF16 = mybir.dt.bfloat16
AF = mybir.ActivationFunctionType
ALU = mybir.AluOpType

T = 4096
D = 1024
NT = 32
NB = 8
FF = 2816
NCORES = 8
EPS = 1e-6
BATCH, DEC_BATCH = 16, 4


class Buf:
    __slots__ = ("w", "r", "ws")

    def __init__(self):
        self.w = None
        self.r = {}
        self.ws = []


def bufs(n):
    return [Buf() for _ in range(n)]


class Sched:
    LIMIT = 30000

    def __init__(self, nc, n_dma_sems=40):
        self.nc = nc
        self.eng = dict(pe=nc.tensor, act=nc.scalar, dve=nc.vector, pool=nc.gpsimd, sp=nc.sync)
        self.csem, self.ccnt, self.nsem = {}, {}, 0
        for e in self.eng:
            self._newsem(e)
        self.known = {e: {} for e in self.eng}
        self.dsems, self.dcnt, self.dnext = {}, {}, {}
        for e, n in (("sp", n_dma_sems), ("pool", 12), ("act", 8)):
            self.dsems[e] = [nc.alloc_semaphore(f"dq_{e}{i}") for i in range(n)]
            self.dcnt[e] = [0] * n
            self.dnext[e] = 0
        self.ninstr = 0
        self.out_tks = []
        import os
        self.cap = int(os.environ.get('KCAP', '1000000000'))
        self.nreal = 0

    def _newsem(self, e):
        self.nsem += 1
        self.csem[e] = self.nc.alloc_semaphore(f"c_{e}_{self.nsem}")
        self.ccnt[e] = 0

    def _wait(self, e, tk):
        if tk is None:
            return
        sem, val = tk
        k = self.known[e]
        if k.get(sem.num, 0) >= val:
            return
        if e == "pe" and sem is self.csem["pe"]:
            return
        self.eng[e].wait_ge(sem, val)
        self.ninstr += 1
        k[sem.num] = val

    def _deps(self, e, reads, writes, join=False):
        for b in reads:
            self._wait(e, b.w)
            for tk in b.ws:
                self._wait(e, tk)
        for b in writes:
            if not join:
                self._wait(e, b.w)
                for tk in b.ws:
                    self._wait(e, tk)
            for tk in list(b.r.values()):
                self._wait(e, tk)

    def _mark(self, tk, reads, writes, join=False):
        sem, val = tk
        for b in reads:
            b.r[sem.num] = tk
        for b in writes:
            if join:
                b.ws.append(tk)
            else:
                b.w = tk
                b.ws = []
            b.r = {}

    def op(self, e, fn, reads=(), writes=()):
        self.nreal += 1
        if self.nreal > self.cap:
            return None
        self._deps(e, reads, writes)
        ins = fn(self.eng[e])
        if self.ccnt[e] >= self.LIMIT:
            self._newsem(e)
        self.ccnt[e] += 1
        sem = self.csem[e]
        ins.then_inc(sem, 1)
        self.ninstr += 1
        tk = (sem, self.ccnt[e])
        self._mark(tk, reads, writes)
        return tk

    def dma(self, e, out, in_, reads=(), writes=(), join=False, **kw):
        self.nreal += 1
        if self.nreal > self.cap:
            return None
        self._deps(e, reads, writes, join)
        i = self.dnext[e]
        self.dnext[e] = (i + 1) % len(self.dsems[e])
        sem = self.dsems[e][i]
        if self.dcnt[e][i] > 0:
            self._wait(e, (sem, self.dcnt[e][i]))
        self.dcnt[e][i] += 16
        self.eng[e].dma_start(out=out, in_=in_, **kw).then_inc(sem, 16)
        self.ninstr += 1
        tk = (sem, self.dcnt[e][i])
        self._mark(tk, reads, writes, join)
        return tk

    def barrier(self):
        for e in self.eng:
            for e2 in self.eng:
                if e2 != e and self.ccnt[e2] > 0:
                    self._wait(e, (self.csem[e2], self.ccnt[e2]))
            for q in self.dsems:
                for i, sem in enumerate(self.dsems[q]):
                    if self.dcnt[q][i] > 0:
                        self._wait(e, (sem, self.dcnt[q][i]))


class Scope:
    cnt = [0]

    def __init__(self, nc):
        self.nc = nc
        self.es = ExitStack()

    def sb(self, name, shape, dt):
        Scope.cnt[0] += 1
        return self.es.enter_context(self.nc.sbuf_tensor(f"{name}_{Scope.cnt[0]}", list(shape), dt))

    def psum(self, name, shape, dt):
        Scope.cnt[0] += 1
        return self.es.enter_context(self.nc.psum_tensor(f"{name}_{Scope.cnt[0]}", list(shape), dt))

    def close(self):
        self.es.close()


def alloc_psum(C, sc, nps, npt):
    C.pt = [sc.psum("pt", [128, 8, 128], BF16) for _ in range(npt)]
    C.bpt = bufs(npt)
    C.npt = npt
    C.pti = 0
    C.ps = [sc.psum("ps", [128, 512], F32) for _ in range(nps)]
    C.bps = bufs(nps)


def _rope_tables(dh, reps):
    half = dh // 2
    inv = (1.0 / (np.float32(10000.0) ** (np.arange(0, dh, 2, dtype=np.float32) / np.float32(dh)))).astype(np.float32)
    ang = np.arange(T, dtype=np.float32)[None, :] * inv[:, None]
    c = np.cos(ang).astype(np.float32)
    s = np.sin(ang).astype(np.float32)
    cos = np.concatenate([c, c], 0)
    sin = np.concatenate([-s, s], 0)
    return np.tile(cos, (reps, 1)).copy(), np.tile(sin, (reps, 1)).copy()


def _dil_masks():
    m = np.zeros((20, 128, 512), np.float32)
    k = np.arange(128)[:, None]
    q = np.arange(512)[None, :]
    for i in range(20):
        d = (i * 128 - 1024) + k - q
        ad = np.abs(d)
        m[i] = (ad <= 64).astype(np.float32) + ((d % 4 == 0) & (ad <= 256)) + ((d % 16 == 0) & (ad <= 1024))
    return m.astype(ml_dtypes.bfloat16)


def _na_onehot():
    L = np.zeros((31, 64, 128), np.float32)
    for cq in range(64):
        cs = min(max(cq - 8, 0), 48)
        for ck in range(cs, cs + 16):
            b = ck - cq + 15
            L[b, cq, ck] = 1.0
            L[b, cq, ck + 64] = 1.0
    return L.reshape(31, 64 * 128)


def _ret_consts():
    k = np.arange(128, dtype=np.float32)[:, None]
    q = np.arange(128, dtype=np.float32)[None, :]
    c = np.zeros((8, 128, 128), np.float32)
    c[0] = np.maximum(q - k, 0)
    c[1] = (q >= k) / 16.0
    c[2] = np.maximum(k - q, 0)
    c[3] = (k > q) / 16.0
    c[4] = np.broadcast_to(q + 1.0, (128, 128))
    c[5] = np.broadcast_to(128.0 - q, (128, 128))
    c[6, :, 0] = 127.0 - k[:, 0]
    c[6, :, 1] = k[:, 0]
    c[6, :, 2] = 128.0
    return c


_CONSTS = {}


def _consts():
    if not _CONSTS:
        cos0, sin0 = _rope_tables(64, 2)
        cos1, sin1 = _rope_tables(256, 1)
        _CONSTS.update(
            ident=np.eye(128, dtype=np.float32).astype(ml_dtypes.bfloat16),
            cos0=cos0, sin0=sin0,
            cos1=np.ascontiguousarray(cos1[:128]), sin1=np.ascontiguousarray(sin1[128:]),
            dmask=_dil_masks(), naL=_na_onehot(), retc=_ret_consts(),
        )
    return _CONSTS


class Ctx:
    pass


def build(nseq, upto=99, debug=False):
    nc = bass.Bass("TRN2", target_bir_lowering=False)
    S = Sched(nc)
    C = Ctx()
    C.nc, C.S, C.nseq = nc, S, nseq
    NTOK = nseq * T

    def din(name, shape, dt=F32):
        return nc.dram_tensor(name, list(shape), dt, kind="ExternalInput").ap()

    def dscr(name, shape, dt, out=False):
        return nc.dram_tensor(name, list(shape), dt, kind="ExternalOutput" if (out or (debug and name in debug)) else "Internal").ap()

    C.x = din("x", [NTOK, D])
    C.w_in0 = din("w_in0", [D, 4096])
    C.w_out0 = din("w_out0", [D, D])
    C.w_up = [din(f"w_up{l}", [D, 2 * FF]) for l in range(2)]
    C.w_dn = [din(f"w_dn{l}", [FF, D]) for l in range(2)]
    C.cwT = [din(f"cwT{l}", [128, 44 * 4]) for l in range(2)]
    C.w_in1 = din("w_in1", [D, 6144])
    C.w_out1 = din("w_out1", [2048, D])
    C.norms = din("norms", [5, D])
    C.rpbT = din("rpbT", [31, 8 * 15])
    C.decay = din("decay", [8])
    C.ident = din("ident", [128, 128], BF16)
    C.cos0 = din("cos0", [128, T])
    C.sin0 = din("sin0", [128, T])
    C.cos1 = din("cos1", [128, T])
    C.sin1 = din("sin1", [128, T])
    C.dmask = din("dmask", [20, 128, 512], BF16)
    C.naL = din("naL", [31, 64 * 128])
    C.retc = din("retc", [8, 128, 128])

    C.y = dscr("y", [NTOK, D], F32, out=True)
    C.QK0 = dscr("QK0", [nseq, 16, 128, T], BF16)
    C.V0 = dscr("V0", [NTOK, D], BF16)
    C.OT0 = dscr("OT0", [nseq, 8, 128, T], BF16)
    C.X1 = dscr("X1", [NTOK, D], F32)
    C.X2 = dscr("X2", [NTOK, D], F32)
    C.X3 = dscr("X3", [NTOK, D], F32)
    C.HTa = dscr("HTa", [nseq, 8, 128, T], BF16)
    C.HTb = dscr("HTb", [nseq, 8, 128, T], BF16)
    C.QTR = dscr("QTR", [nseq, 8, 128, T], BF16)
    C.KTR = dscr("KTR", [nseq, 8, 128, T], BF16)
    C.KTM = dscr("KTM", [NTOK, D], BF16)
    C.VR = dscr("VR", [NTOK, 2048], BF16)
    C.SG = dscr("SG", [NTOK, 2048], BF16)
    C.RF = dscr("RF", [NTOK, 2048], F32)
    C.TAZ = dscr("TAZ", [8, 2, 128, 32 * 64], F32)

    C.idt = nc.alloc_sbuf_tensor("idt", [128, 128], BF16)
    C.bidt = Buf()
    S.dma("sp", C.idt[:], C.ident[:, :], writes=[C.bidt])

    phases = [p1_inproj0, p2_attn, p3_outproj0,
              lambda c: p4_ffn(c, 0), p5_inproj1, p6_retention, lambda c: p4_ffn(c, 1)]
    for i, ph in enumerate(phases):
        if i >= upto:
            break
        ph(C)
        S.barrier()
    S.barrier()
    return nc, S


def load_w(C, sc, name, w_dram, kc, f, eng="pool"):
    t = sc.sb(name, [128, kc, f], BF16)
    b = Buf()
    src = w_dram.rearrange("(k p) f -> p k f", p=128)
    step = max(1, 2048 // f) if f < 2048 else 1
    for k in range(0, kc, step):
        k2 = min(kc, k + step)
        C.S.dma(eng, t[:, k:k2, :], src[:, k:k2, :], writes=[b], join=(k > 0))
    return t, b


def load_grow(C, sc, idx):
    g = sc.sb("grow", [128, D], F32)
    b = Buf()
    C.S.dma("sp", g[:], C.norms[idx, :].partition_broadcast(128), writes=[b])
    return g, b


class NormBufs:
    def __init__(self, sc, n=2):
        self.n = n
        self.junk = [sc.sb("nj", [128, D], BF16) for _ in range(n)]
        self.ss = [sc.sb("nss", [128, 1], F32) for _ in range(n)]
        self.b = [bufs(2) for _ in range(n)]
        self.i = 0


def rmsnorm(C, nb, xt, bx, grow, bg, out, bout):
    S = C.S
    i = nb.i % nb.n
    nb.i += 1
    junk, ss, (bj, bs) = nb.junk[i], nb.ss[i], nb.b[i]
    S.op("act", lambda e: e.activation(out=junk[:], in_=xt, func=AF.Square, accum_out=ss[:]), reads=[bx], writes=[bj, bs])
    S.op("dve", lambda e: e.tensor_scalar(out=ss[:], in0=ss[:], scalar1=1.0 / D, scalar2=EPS, op0=ALU.mult, op1=ALU.add), reads=[bs], writes=[bs])
    S.op("act", lambda e: e.activation(out=ss[:], in_=ss[:], func=AF.Sqrt), reads=[bs], writes=[bs])
    S.op("dve", lambda e: e.reciprocal(out=ss[:], in_=ss[:]), reads=[bs], writes=[bs])
    S.op("dve", lambda e: e.scalar_tensor_tensor(out=out, in0=xt, scalar=ss[:, 0:1], in1=grow[:], op0=ALU.mult, op1=ALU.mult),
         reads=[bx, bs, bg], writes=[bout])


def transpose_to(C, src, bsrc, nk, dst_fn, bdst, evac_eng):
    S = C.S
    for g in range(0, nk, 8):
        h = C.pti % C.npt
        C.pti += 1
        n = min(8, nk - g)
        for k in range(g, g + n):
            S.op("pe", lambda e: e.transpose(out=C.pt[h][:, k - g, :], in_=src[:, k * 128:(k + 1) * 128], identity=C.idt[:]),
                 reads=[bsrc, C.bidt], writes=[C.bpt[h]])
        if evac_eng == "act":
            S.op("act", lambda e: e.copy(out=dst_fn(g, g + n), in_=C.pt[h][:, 0:n, :]), reads=[C.bpt[h]], writes=[bdst])
        else:
            S.op(evac_eng, lambda e: e.tensor_copy(out=dst_fn(g, g + n), in_=C.pt[h][:, 0:n, :]), reads=[C.bpt[h]], writes=[bdst])


def p1_inproj0(C):
    nc, S = C.nc, C.S
    sc = Scope(nc)
    alloc_psum(C, sc, 6, 2)
    W, bW = load_w(C, sc, "w0", C.w_in0, 8, 4096)
    grow, bg = load_grow(C, sc, 0)
    nb = NormBufs(sc)
    xt = [sc.sb("xt", [128, D], F32) for _ in range(2)]
    bxt = bufs(2)
    hb = [sc.sb("hb", [128, D], BF16) for _ in range(2)]
    bhb = bufs(2)
    hT = [sc.sb("hT", [128, 8, 512], BF16) for _ in range(2)]
    bhT = bufs(2)
    cs = [sc.sb("cs", [128, 2, 512], F32) for _ in range(2)]
    bcs = bufs(2)
    t1 = [sc.sb("t1", [128, 512], F32) for _ in range(2)]
    t2 = [sc.sb("t2", [128, 512], F32) for _ in range(2)]
    bt1, bt2 = bufs(2), bufs(2)
    ob = [sc.sb("ob", [128, 512], BF16) for _ in range(4)]
    bob = bufs(4)
    oi = 0
    pi = 0
    for s in range(C.nseq):
        for tb in range(NB):
            hTc, bhTc = hT[tb % 2], bhT[tb % 2]
            csc, bcsc = cs[tb % 2], bcs[tb % 2]
            S.dma("sp", csc[:, 0, :], C.cos0[:, tb * 512:(tb + 1) * 512], writes=[bcsc])
            S.dma("sp", csc[:, 1, :], C.sin0[:, tb * 512:(tb + 1) * 512], writes=[bcsc], join=True)
            for j in range(4):
                tt = tb * 4 + j
                r0 = s * T + tt * 128
                S.dma("sp", xt[tt % 2][:], C.x[r0:r0 + 128, :], writes=[bxt[tt % 2]])
                rmsnorm(C, nb, xt[tt % 2][:], bxt[tt % 2], grow, bg, hb[tt % 2][:], bhb[tt % 2])
                transpose_to(C, hb[tt % 2], bhb[tt % 2], 8, lambda a, b: hTc[:, a:b, j * 128:(j + 1) * 128], bhTc, "act")
            for ft in list(range(8)) + list(range(12, 20)):
                p = pi % 4
                pi += 1
                for k in range(8):
                    S.op("pe", lambda e: e.matmul(C.ps[p][:], lhsT=W[:, k, ft * 128:(ft + 1) * 128], rhs=hTc[:, k, :], start=(k == 0), stop=(k == 7)),
                         reads=[bW, bhTc], writes=[C.bps[p]])
                o, bo = ob[oi % 4], bob[oi % 4]
                oi += 1
                if ft < 8:
                    S.op("act", lambda e: e.copy(out=o[:], in_=C.ps[p][:]), reads=[C.bps[p]], writes=[bo])
                    dst = ft
                else:
                    p2 = pi % 4
                    pi += 1
                    fs = ft + 12
                    for k in range(8):
                        S.op("pe", lambda e: e.matmul(C.ps[p2][:], lhsT=W[:, k, fs * 128:(fs + 1) * 128], rhs=hTc[:, k, :], start=(k == 0), stop=(k == 7)),
                             reads=[bW, bhTc], writes=[C.bps[p2]])
                    a, ba = t1[oi % 2], bt1[oi % 2]
                    b, bb = t2[oi % 2], bt2[oi % 2]
                    S.op("dve", lambda e: e.tensor_tensor(out=a[:], in0=C.ps[p][:], in1=csc[:, 0, :], op=ALU.mult), reads=[C.bps[p], bcsc], writes=[ba])
                    S.op("dve", lambda e: e.tensor_tensor(out=b[:], in0=C.ps[p2][:], in1=csc[:, 1, :], op=ALU.mult), reads=[C.bps[p2], bcsc], writes=[bb])
                    S.op("pool", lambda e: e.tensor_tensor(out=o[:], in0=a[:], in1=b[:], op=ALU.add), reads=[ba, bb], writes=[bo])
                    dst = ft - 4
                S.dma("sp", C.QK0[s, dst, :, tb * 512:(tb + 1) * 512], o[:], reads=[bo])
            for j in range(4):
                r0 = s * T + (tb * 4 + j) * 128
                for ci, c0 in enumerate((1024, 2560)):
                    p = pi % 4
                    pi += 1
                    for k in range(8):
                        S.op("pe", lambda e: e.matmul(C.ps[p][:], lhsT=hTc[:, k, j * 128:(j + 1) * 128], rhs=W[:, k, c0:c0 + 512], start=(k == 0), stop=(k == 7)),
                             reads=[bW, bhTc], writes=[C.bps[p]])
                    o, bo = ob[oi % 4], bob[oi % 4]
                    oi += 1
                    if ci == 0:
                        S.op("act", lambda e: e.copy(out=o[:], in_=C.ps[p][:]), reads=[C.bps[p]], writes=[bo])
                    else:
                        S.op("dve", lambda e: e.tensor_copy(out=o[:], in_=C.ps[p][:]), reads=[C.bps[p]], writes=[bo])
                    S.dma("sp", C.V0[r0:r0 + 128, ci * 512:(ci + 1) * 512], o[:], reads=[bo])
    S.barrier()
    sc.close()


def _na_valid(rq, rk):
    rs = min(max(rq - 4, 0), 56)
    return rs <= rk < rs + 8


def p2_attn(C):
    nc, S = C.nc, C.S
    sc = Scope(nc)
    alloc_psum(C, sc, 8, 0)
    sc2 = Scope(nc)
    L = sc2.sb("naL", [31, 64 * 128], F32)
    PT = sc2.sb("naPT", [31, 8 * 15], F32)
    Z = [sc2.sb("naZ", [128, 2, 32 * 64], F32) for _ in range(2)]
    bL, bPT, bZ = Buf(), Buf(), bufs(2)
    for c in range(0, 64 * 128, 2048):
        S.dma("sp", L[:, c:c + 2048], C.naL[:, c:c + 2048], writes=[bL], join=(c > 0))
    S.dma("sp", PT[:], C.rpbT[:, :], writes=[bPT])
    S.op("act", lambda e: e.activation(out=PT[:], in_=PT[:], func=AF.Exp), reads=[bPT], writes=[bPT])
    for z in range(2):
        S.op("pool", lambda e: e.memset(Z[z][:], 0.0), writes=[bZ[z]])
    for h in range(8):
        z = h % 2
        for half in range(2):
            p = (h * 2 + half) % 4
            pv = C.ps[p][:].rearrange("p (s c) -> p s c", c=64)
            ns = 8 if half == 0 else 7
            for cq in range(64):
                S.op("pe", lambda e: e.matmul(pv[:, 0:ns, cq], lhsT=L[:, cq * 128:(cq + 1) * 128], rhs=PT[:, h * 15 + half * 8:h * 15 + half * 8 + ns],
                                              start=True, stop=True), reads=[bL, bPT], writes=[C.bps[p]])
            a0 = (8 + half * 8) * 64
            S.op("dve", lambda e: e.tensor_copy(out=Z[z][0:64, 0, a0:a0 + ns * 64], in_=C.ps[p][0:64, 0:ns * 64]), reads=[C.bps[p]], writes=[bZ[z]])
            S.op("dve", lambda e: e.tensor_copy(out=Z[z][64:128, 0, a0 + 64:a0 + 64 + ns * 64], in_=C.ps[p][64:128, 0:ns * 64]), reads=[C.bps[p]], writes=[bZ[z]])
            i0, i1 = (4, 8) if half == 0 else (0, 4)
            S.op("dve", lambda e: e.tensor_copy(out=Z[z][0:64, 1, a0 + i0 * 64:a0 + i1 * 64], in_=C.ps[p][0:64, i0 * 64:i1 * 64]), reads=[C.bps[p]], writes=[bZ[z]])
            S.op("dve", lambda e: e.tensor_copy(out=Z[z][64:128, 1, a0 + 64 + i0 * 64:a0 + 64 + i1 * 64], in_=C.ps[p][64:128, i0 * 64:i1 * 64]), reads=[C.bps[p]], writes=[bZ[z]])
        S.dma("sp", C.TAZ[h, :, :, :].rearrange("v p c -> p v c"), Z[z][:], reads=[bZ[z]])
    S.barrier()
    sc2.close()
    TAzI = [sc.sb("TAzI", [128, 2, 32 * 64], F32) for _ in range(2)]
    bTAI = bufs(2)
    TAzF = sc.sb("TAzF", [128, 2, 32 * 64], F32)
    bTAF = Buf()

    DM = sc.sb("dmask", [128, 20, 512], BF16)
    bDM = Buf()
    for i in range(0, 20, 4):
        S.dma("sp", DM[:, i:i + 4, :], C.dmask[i:i + 4].rearrange("i p q -> p i q"), writes=[bDM], join=(i > 0))
    LA = 8
    NST = 4
    Qz = [[sc.sb("Qz", [128, T], BF16) for _ in range(2)] for _ in range(2)]
    KT = [sc.sb("KT", [128, T], BF16) for _ in range(2)]
    VA = [sc.sb("VA", [128, NT, 256], BF16) for _ in range(2)]
    bQ, bK, bV = bufs(2), bufs(2), bufs(2)
    for b_ in range(2):
        S.op("pool", lambda e: e.memset(Qz[b_][0][64:128, :], 0.0), writes=[bQ[b_]])
        S.op("pool", lambda e: e.memset(Qz[b_][1][0:64, :], 0.0), writes=[bQ[b_]])
        S.op("pool", lambda e: e.memset(VA[b_][:, :, 64:192], 1.0), writes=[bV[b_]])
    NE, NEM = 8, LA + 4
    E = [sc.sb("E", [128, 512], F32) for _ in range(NE)]
    bE = bufs(NE)
    Eb = [sc.sb("Eb", [128, 512], BF16) for _ in range(NE)]
    bEb = bufs(NE)
    Em = [sc.sb("Em", [128, 512], BF16) for _ in range(NEM)]
    bEm = bufs(NEM)
    rc = [sc.sb("rc", [128, 512], F32) for _ in range(2)]
    rs = [sc.sb("rs", [128, 512], F32) for _ in range(2)]
    brc, brs = bufs(2), bufs(2)
    oT = [sc.sb("oT", [128, 512], BF16) for _ in range(2)]
    boT = bufs(2)

    stA, stB = [], []
    pending = []

    def mk_load(s, g, gi):
        def f():
            b_ = gi % 2
            qt, kt = (g, 4 + g) if g < 4 else (8 + (g - 4), 12 + (g - 4))
            for c in range(0, T, 2048):
                S.dma("sp", Qz[b_][0][0:64, c:c + 2048], C.QK0[s, qt, 0:64, c:c + 2048], writes=[bQ[b_]], join=(c > 0))
                S.dma("sp", Qz[b_][1][64:128, c:c + 2048], C.QK0[s, qt, 64:128, c:c + 2048], writes=[bQ[b_]], join=True)
                S.dma("sp", KT[b_][:, c:c + 2048], C.QK0[s, kt, :, c:c + 2048], writes=[bK[b_]], join=(c > 0))
            if g < 4:
                for hh in range(2):
                    S.dma("sp", TAzI[b_][:, hh, :], C.TAZ[g * 2 + hh, 1, :, :], writes=[bTAI[b_]], join=(hh > 0))
            if g == 0:
                ldF(0)()
            vsrc = C.V0[s * T:(s + 1) * T, g * 128:(g + 1) * 128].rearrange("(t p) c -> p t c", p=128)
            for c in range(0, NT, 8):
                S.dma("sp", VA[b_][:, c:c + 8, 0:64], vsrc[:, c:c + 8, 0:64], writes=[bV[b_]], join=(c > 0))
                S.dma("sp", VA[b_][:, c:c + 8, 192:256], vsrc[:, c:c + 8, 64:128], writes=[bV[b_]], join=True)
        return f

    def ldF(g):
        def f():
            for hh in range(2):
                S.dma("sp", TAzF[:, hh, :], C.TAZ[g * 2 + hh, 0, :, :], writes=[bTAF], join=(hh > 0))
        return f

    def mk_tile(s, g, gi, qb, qi, hp, ki, kb, nk, ti, fin):
        isna = g < 4
        b_ = gi % 2
        Qg, Kg, Vg = Qz[b_][hp], KT[b_], VA[b_]
        bQg, bKg, bVg = bQ[b_], bK[b_], bV[b_]
        p = ti % NST
        e_, be_ = (E[ti % NE], bE[ti % NE]) if isna else (Eb[ti % NE], bEb[ti % NE])
        em, bem = Em[ti % NEM], bEm[ti % NEM]
        pA = NST + (qi % 2) * 2
        pB = NST + 1 + (qi % 2) * 2
        R = qb * 8

        def a():
            S.op("pe", lambda e: e.matmul(C.ps[p][:], lhsT=Kg[:, kb * 128:(kb + 1) * 128], rhs=Qg[:, qb * 512:(qb + 1) * 512], start=True, stop=True),
                 reads=[bKg, bQg], writes=[C.bps[p]])
            S.op("act", lambda e: e.activation(out=e_[:], in_=C.ps[p][:], func=AF.Exp, scale=0.125), reads=[C.bps[p]], writes=[be_])
            if isna:
                rk0 = kb * 2
                s0 = 7 - rk0 + R
                fast = all((_na_valid(R + f, rk0 + ph) == (4 <= s0 - ph + f <= 11)) for f in range(8) for ph in range(2))
                if fast:
                    S.op("dve" if (ti % 3) != 2 else "pool", lambda e: e.tensor_tensor(out=em[:], in0=e_[:], in1=TAzI[b_][:, hp, (s0 + 8) * 64:(s0 + 16) * 64], op=ALU.mult), reads=[be_, bTAI[b_]], writes=[bem])
                else:
                    assert qb in (0, 7)
                    for ph in range(2):
                        rk = rk0 + ph
                        fs = [f for f in range(8) if _na_valid(R + f, rk)]
                        pp = slice(ph * 64, (ph + 1) * 64)
                        if fs:
                            f1, f2 = fs[0], fs[-1]
                            assert fs == list(range(f1, f2 + 1))
                            c1 = (s0 + 8 + f1) * 64
                            n = f2 - f1 + 1
                            S.op("dve", lambda e: e.tensor_tensor(out=em[pp, f1 * 64:(f2 + 1) * 64], in0=e_[pp, f1 * 64:(f2 + 1) * 64],
                                                                in1=TAzF[pp, hp, c1:c1 + n * 64], op=ALU.mult), reads=[be_, bTAF], writes=[bem])
                            if f1 > 0:
                                S.op("pool", lambda e: e.memset(em[pp, 0:f1 * 64], 0.0), writes=[bem])
                            if f2 < 7:
                                S.op("pool", lambda e: e.memset(em[pp, (f2 + 1) * 64:512], 0.0), writes=[bem])
                        else:
                            S.op("pool", lambda e: e.memset(em[pp, :], 0.0), writes=[bem])
            else:
                mi = (kb * 128 - qb * 512 + 1024) // 128
                eng = "dve" if (ti % 3) != 2 else "pool"
                S.op(eng, lambda e: e.tensor_tensor(out=em[:], in0=e_[:], in1=DM[:, mi, :], op=ALU.mult), reads=[be_, bDM], writes=[bem])

        def b():
            first, last = ki == 0, ki == nk - 1
            pX = pA if hp == 0 else pB
            S.op("pe", lambda e: e.matmul(C.ps[pX][:], lhsT=Vg[:, kb, hp * 128:(hp + 1) * 128], rhs=em[:], start=first, stop=last),
                 reads=[bVg, bem], writes=[C.bps[pX]])
            if fin:
                j = qi % 2

                def f1():
                    S.op("act", lambda e: e.copy(out=rc[j][64:128, :], in_=C.ps[pA][64:128, :]), reads=[C.bps[pA]], writes=[brc[j]])
                    S.op("act", lambda e: e.copy(out=rc[j][0:64, :], in_=C.ps[pB][0:64, :]), reads=[C.bps[pB]], writes=[brc[j]])
                    S.dma("sp", rs[j][0:64, :], rc[j][64:128, :], reads=[brc[j]], writes=[brs[j]])
                    S.dma("sp", rs[j][64:128, :], rc[j][0:64, :], reads=[brc[j]], writes=[brs[j]])

                def f2():
                    S.op("dve", lambda e: e.reciprocal(out=rs[j][:], in_=rs[j][:]), reads=[brs[j]], writes=[brs[j]])

                def f3():
                    S.op("dve", lambda e: e.tensor_tensor(out=oT[j][0:64, :], in0=C.ps[pA][0:64, :], in1=rs[j][0:64, :], op=ALU.mult), reads=[C.bps[pA], brs[j]], writes=[boT[j]])

                def f4():
                    S.op("dve", lambda e: e.tensor_tensor(out=oT[j][64:128, :], in0=C.ps[pB][64:128, :], in1=rs[j][64:128, :], op=ALU.mult), reads=[C.bps[pB], brs[j]], writes=[boT[j]])
                    S.dma("sp", C.OT0[s, g, :, qb * 512:(qb + 1) * 512], oT[j][:], reads=[boT[j]])
                pending.append([2, f1])
                pending.append([7, f2])
                pending.append([9, f3])
                pending.append([10, f4])
        return a, b

    ti = 0
    qi = 0
    gi = 0
    pairs = [(s_, g_) for s_ in range(C.nseq) for g_ in range(8)]
    for pi_, (s, g) in enumerate(pairs):
        hooks = {}
        if pi_ == 0:
            hooks[0] = [mk_load(s, g, gi)]
        if pi_ + 1 < len(pairs):
            ns_, ng_ = pairs[pi_ + 1]
            hooks.setdefault(4, []).append(mk_load(ns_, ng_, gi + 1))
            if 1 <= ng_ <= 3:
                hooks.setdefault(2, []).append(ldF(ng_))
        qorder = [0, 7, 1, 2, 3, 4, 5, 6] if g < 4 else list(range(NB))
        for qn, qb in enumerate(qorder):
            if g < 4:
                R = qb * 8
                lo = min(max(R - 4, 0), 56)
                hi = min(max(R + 7 - 4, 0), 56) + 8
                kbs = list(range(lo // 2, (hi + 1) // 2))
            else:
                kbs = list(range(max(0, qb * 4 - 8), min(NT, qb * 4 + 4 + 8)))
            nk = len(kbs)
            for ki, kb in enumerate(kbs):
                for hp in range(2):
                    a, b = mk_tile(s, g, gi, qb, qi, hp, ki, kb, nk, ti, fin=(ki == nk - 1 and hp == 1))
                    if ki == 0 and hp == 0 and qn in hooks:
                        hk = hooks[qn]
                        a = (lambda hk=hk, a0=a: ([h() for h in hk], a0()))
                    stA.append(a)
                    stB.append(b)
                    ti += 1
            qi += 1
        gi += 1
    n = len(stA)
    for i in range(n + LA + 12):
        if i < n:
            stA[i]()
        if LA <= i < n + LA:
            stB[i - LA]()
        for pe_ in list(pending):
            pe_[0] -= 1
            if pe_[0] <= 0:
                pe_[1]()
                pending.remove(pe_)
    assert not pending
    S.barrier()
    sc.close()


def p3_outproj0(C):
    nc, S = C.nc, C.S
    sc = Scope(nc)
    alloc_psum(C, sc, 6, 2)
    W, bW = load_w(C, sc, "wo0", C.w_out0, 8, D)
    grow, bg = load_grow(C, sc, 1)
    nb = NormBufs(sc)
    oT = [sc.sb("oTb", [128, 8, 512], BF16) for _ in range(2)]
    boT = bufs(2)
    xt = [sc.sb("xt", [128, D], F32) for _ in range(2)]
    bxt = bufs(2)
    x1 = [sc.sb("x1", [128, D], F32) for _ in range(2)]
    bx1 = bufs(2)
    hb = [sc.sb("hb", [128, D], BF16) for _ in range(2)]
    bhb = bufs(2)
    hT = [sc.sb("hT", [128, 8, 128], BF16) for _ in range(2)]
    bhT = bufs(2)
    def load_o(s, tb):
        o, bo = oT[tb % 2], boT[tb % 2]
        for k in range(0, 8, 2):
            S.dma("sp", o[:, k:k + 2, :], C.OT0[s, k:k + 2, :, tb * 512:(tb + 1) * 512].rearrange("k p t -> p k t"), writes=[bo], join=(k > 0))

    pi = 0
    for s in range(C.nseq):
        for tb in range(NB):
            o, bo = oT[tb % 2], boT[tb % 2]
            if tb == 0:
                load_o(s, tb)
            if tb + 1 < NB:
                load_o(s, tb + 1)
            for j in range(4):
                tt = tb * 4 + j
                r0 = s * T + tt * 128
                i2 = tt % 2
                S.dma("sp", xt[i2][:], C.x[r0:r0 + 128, :], writes=[bxt[i2]])
                for c in range(2):
                    p = pi % 3
                    pi += 1
                    for k in range(8):
                        S.op("pe", lambda e: e.matmul(C.ps[p][:], lhsT=o[:, k, j * 128:(j + 1) * 128], rhs=W[:, k, c * 512:(c + 1) * 512], start=(k == 0), stop=(k == 7)),
                             reads=[bW, bo], writes=[C.bps[p]])
                    S.op("dve", lambda e: e.tensor_tensor(out=x1[i2][:, c * 512:(c + 1) * 512], in0=C.ps[p][:], in1=xt[i2][:, c * 512:(c + 1) * 512], op=ALU.add),
                         reads=[C.bps[p], bxt[i2]], writes=[bx1[i2]])
                S.dma("sp", C.X1[r0:r0 + 128, :], x1[i2][:], reads=[bx1[i2]])
                rmsnorm(C, nb, x1[i2][:], bx1[i2], grow, bg, hb[i2][:], bhb[i2])
                transpose_to(C, hb[i2], bhb[i2], 8, lambda a, b: hT[i2][:, a:b, :], bhT[i2], "act")
                S.dma("sp", C.HTa[s, :, :, tt * 128:(tt + 1) * 128].rearrange("k p t -> p k t"), hT[i2][:], reads=[bhT[i2]])
    S.barrier()
    sc.close()


def p4_ffn(C, layer):
    nc, S = C.nc, C.S
    sc = Scope(nc)
    alloc_psum(C, sc, 7, 1)
    HTin = C.HTa if layer == 0 else C.HTb
    Xin = C.X1 if layer == 0 else C.X3
    Wu, bWu = load_w(C, sc, "wu", C.w_up[layer], 8, 2 * FF)
    Wd, bWd = load_w(C, sc, "wd", C.w_dn[layer], 22, D)
    cw = sc.sb("cw", [128, 44, 4], F32)
    bcw = Buf()
    S.dma("sp", cw[:], C.cwT[layer].rearrange("p (t c) -> p t c", c=4), writes=[bcw])
    grow, bg = load_grow(C, sc, 2 if layer == 0 else 4)
    nb = NormBufs(sc, 1)
    hT = sc.sb("h2T", [128, 8, 514], BF16)
    bh = Buf()
    gT = sc.sb("gT", [128, 22, 512], BF16)
    bgT = bufs(22)
    ah = sc.sb("ahalo", [128, 2, 44], F32)
    bah = Buf()
    yus = [sc.sb("yu", [128, 512], F32) for _ in range(2)]
    ygs = [sc.sb("yg", [128, 512], F32) for _ in range(2)]
    ggs = [sc.sb("gg", [128, 512], F32) for _ in range(2)]
    byus, bygs, bggs = bufs(2), bufs(2), bufs(2)
    xt = sc.sb("xt", [128, D], F32)
    bxt = Buf()
    x2 = sc.sb("x2", [128, D], F32)
    bx2 = Buf()
    if layer == 0:
        hbs = [sc.sb("hb", [128, D], BF16) for _ in range(2)]
        h3T = sc.sb("h3T", [128, 8, 128], BF16)
        bhbs, bh3 = bufs(2), Buf()
    else:
        yo = sc.sb("yo", [128, D], F32)
        byo = Buf()
    S.op("pool", lambda e: e.memset(hT[:], 0.0), writes=[bh])
    def load_h(s, tb):
        t0 = tb * 512
        lo = max(t0 - 1, 0)
        hi = min(t0 + 513, T)
        c0 = lo - (t0 - 1)
        if tb == 0:
            S.op("pool", lambda e: e.memset(hT[:, :, 0:1], 0.0), writes=[bh])
        if tb == NB - 1:
            S.op("pool", lambda e: e.memset(hT[:, :, 513:514], 0.0), writes=[bh])
        for k in range(0, 8, 2):
            S.dma("sp", hT[:, k:k + 2, c0:c0 + (hi - lo)], HTin[s, k:k + 2, :, lo:hi].rearrange("k p t -> p k t"), writes=[bh], join=(k > 0))

    pi = 0
    deferred = []
    prev_s2 = [None]
    for s in range(C.nseq):
        for tb in range(NB):
            if tb == 0:
                load_h(s, tb)
            hal = hT[:, :, 0:514:513]
            for f0 in range(0, 44, 11):
                p = 6
                for f in range(f0, f0 + 11):
                    for k in range(8):
                        S.op("pe", lambda e: e.matmul(C.ps[p][:, (f - f0) * 2:(f - f0) * 2 + 2], lhsT=Wu[:, k, f * 128:(f + 1) * 128], rhs=hal[:, k, :],
                                                      start=(k == 0), stop=(k == 7)), reads=[bWu, bh], writes=[C.bps[p]])
                pv = C.ps[p][:, 0:22].rearrange("p (f c) -> p c f", c=2)
                S.op("dve", lambda e: e.tensor_tensor(out=ah[:, 0, f0:f0 + 11], in0=pv[:, 0, :], in1=cw[:, f0:f0 + 11, 0], op=ALU.mult), reads=[C.bps[p], bcw], writes=[bah])
                S.op("dve", lambda e: e.tensor_tensor(out=ah[:, 1, f0:f0 + 11], in0=pv[:, 1, :], in1=cw[:, f0:f0 + 11, 2], op=ALU.mult), reads=[C.bps[p], bcw], writes=[bah])
            for f in range(22):
                yu, yg, gg = yus[f % 2], ygs[f % 2], ggs[f % 2]
                byu, byg, bgg = byus[f % 2], bygs[f % 2], bggs[f % 2]
                for (ft, yt, byt) in ((f, yu, byu), (22 + f, yg, byg)):
                    p = pi % 4
                    pi += 1
                    for k in range(8):
                        S.op("pe", lambda e: e.matmul(C.ps[p][:], lhsT=Wu[:, k, ft * 128:(ft + 1) * 128], rhs=hT[:, k, 1:513], start=(k == 0), stop=(k == 7)),
                             reads=[bWu, bh], writes=[C.bps[p]])
                    A = C.ps[p]
                    S.op("act", lambda e: e.activation(out=yt[:], in_=A[:], func=AF.Identity, scale=cw[:, ft, 1:2], bias=cw[:, ft, 3:4]), reads=[C.bps[p], bcw], writes=[byt])
                    S.op("dve", lambda e: e.scalar_tensor_tensor(out=yt[:, 1:512], in0=A[:, 0:511], scalar=cw[:, ft, 0:1], in1=yt[:, 1:512], op0=ALU.mult, op1=ALU.add),
                         reads=[C.bps[p], bcw, byt], writes=[byt])
                    S.op("dve", lambda e: e.scalar_tensor_tensor(out=yt[:, 0:511], in0=A[:, 1:512], scalar=cw[:, ft, 2:3], in1=yt[:, 0:511], op0=ALU.mult, op1=ALU.add),
                         reads=[C.bps[p], bcw, byt], writes=[byt])
                    S.op("pool", lambda e: e.tensor_tensor(out=yt[:, 0:1], in0=yt[:, 0:1], in1=ah[:, 0, ft:ft + 1], op=ALU.add), reads=[byt, bah], writes=[byt])
                    S.op("pool", lambda e: e.tensor_tensor(out=yt[:, 511:512], in0=yt[:, 511:512], in1=ah[:, 1, ft:ft + 1], op=ALU.add), reads=[byt, bah], writes=[byt])
                def s2(f=f, yu=yu, yg=yg, gg=gg, byu=byu, byg=byg, bgg=bgg):
                    S.op("act", lambda e: e.activation(out=gg[:], in_=yg[:], func=AF.Gelu_apprx_tanh), reads=[byg], writes=[bgg])
                    S.op("pool", lambda e: e.tensor_tensor(out=gT[:, f, :], in0=yu[:], in1=gg[:], op=ALU.mult), reads=[byu, bgg], writes=[bgT[f]])
                if prev_s2[0] is not None:
                    prev_s2[0]()
                prev_s2[0] = s2
            prev_s2[0]()
            prev_s2[0] = None
            if tb + 1 < NB:
                load_h(s, tb + 1)
            for j in range(4):
                tt = tb * 4 + j
                r0 = s * T + tt * 128
                S.dma("sp", xt[:], Xin[r0:r0 + 128, :], writes=[bxt])
                for c in range(2):
                    p = 4 + (pi % 2)
                    pi += 1
                    for f in range(22):
                        S.op("pe", lambda e: e.matmul(C.ps[p][:], lhsT=gT[:, f, j * 128:(j + 1) * 128], rhs=Wd[:, f, c * 512:(c + 1) * 512], start=(f == 0), stop=(f == 21)),
                             reads=[bWd, bgT[f]], writes=[C.bps[p]])
                    S.op("dve", lambda e: e.tensor_tensor(out=x2[:, c * 512:(c + 1) * 512], in0=C.ps[p][:], in1=xt[:, c * 512:(c + 1) * 512], op=ALU.add),
                         reads=[C.bps[p], bxt], writes=[bx2])
                if layer == 0:
                    S.dma("sp", C.X2[r0:r0 + 128, :], x2[:], reads=[bx2])
                    hb, bhb = hbs[tt % 2], bhbs[tt % 2]
                    rmsnorm(C, nb, x2[:], bx2, grow, bg, hb[:], bhb)

                    def fin(hb=hb, bhb=bhb, tt=tt, s=s):
                        transpose_to(C, hb, bhb, 8, lambda a, b: h3T[:, a:b, :], bh3, "act")
                        S.dma("sp", C.HTb[s, :, :, tt * 128:(tt + 1) * 128].rearrange("k p t -> p k t"), h3T[:], reads=[bh3])
                    deferred.append(fin)
                    if len(deferred) > 1:
                        deferred.pop(0)()
                else:
                    rmsnorm(C, nb, x2[:], bx2, grow, bg, yo[:], byo)
                    S.dma("sp", C.y[r0:r0 + 128, :], yo[:], reads=[byo])
            while deferred:
                deferred.pop(0)()
    S.barrier()
    sc.close()


def p5_inproj1(C):
    nc, S = C.nc, C.S
    sc = Scope(nc)
    alloc_psum(C, sc, 6, 2)
    W, bW = load_w(C, sc, "w1", C.w_in1, 8, 6144)
    hT = [sc.sb("h3T", [128, 8, 512], BF16) for _ in range(2)]
    bh = bufs(2)
    cs = [sc.sb("cs1", [128, 2, 512], F32) for _ in range(2)]
    bcs = bufs(2)
    tm = [sc.sb("tm", [128, 512], F32) for _ in range(4)]
    btm = bufs(4)
    ob = [sc.sb("ob", [128, 512], BF16) for _ in range(4)]
    bob = bufs(4)
    kr = [sc.sb("kr", [128, 2, 512], BF16) for _ in range(2)]
    bkr = bufs(2)
    ktm = [sc.sb("ktm", [128, 256], BF16) for _ in range(2)]
    bktm = bufs(2)
    late5 = []

    def k_tm(kk, bkk, hd, s, tb):
        for j in range(4):
            r0 = s * T + (tb * 4 + j) * 128
            kt_, bkt_ = ktm[j % 2], bktm[j % 2]
            hh = C.pti % C.npt
            C.pti += 1
            for half in range(2):
                S.op("pe", lambda e: e.transpose(out=C.pt[hh][:, half, :], in_=kk[:, half, j * 128:(j + 1) * 128], identity=C.idt[:]),
                     reads=[bkk, C.bidt], writes=[C.bpt[hh]])
            S.op("act", lambda e: e.copy(out=kt_[:].rearrange("p (a b) -> p a b", a=2), in_=C.pt[hh][:, 0:2, :]), reads=[C.bpt[hh]], writes=[bkt_])
            S.dma("sp", C.KTM[r0:r0 + 128, hd * 256:(hd + 1) * 256], kt_[:], reads=[bkt_])

    def load_in(s, tb):
        h, bhc = hT[tb % 2], bh[tb % 2]
        csc, bcsc = cs[tb % 2], bcs[tb % 2]
        for k in range(0, 8, 2):
            S.dma("sp", h[:, k:k + 2, :], C.HTb[s, k:k + 2, :, tb * 512:(tb + 1) * 512].rearrange("k p t -> p k t"), writes=[bhc], join=(k > 0))
        S.dma("sp", csc[:, 0, :], C.cos1[:, tb * 512:(tb + 1) * 512], writes=[bcsc])
        S.dma("sp", csc[:, 1, :], C.sin1[:, tb * 512:(tb + 1) * 512], writes=[bcsc], join=True)

    oi = 0
    pi = 0
    ki = 0
    for s in range(C.nseq):
        for tb in range(NB):
            h, bhc = hT[tb % 2], bh[tb % 2]
            csc, bcsc = cs[tb % 2], bcs[tb % 2]
            if tb == 0:
                load_in(s, tb)
            if tb + 1 < NB:
                load_in(s, tb + 1)
            for qk in range(2):
                for hd in range(4):
                    pa, pb = pi % 4, (pi + 1) % 4
                    pi += 2
                    for (p, c0) in ((pa, qk * 1024 + hd * 256), (pb, qk * 1024 + hd * 256 + 128)):
                        for k in range(8):
                            S.op("pe", lambda e: e.matmul(C.ps[p][:], lhsT=W[:, k, c0:c0 + 128], rhs=h[:, k, :], start=(k == 0), stop=(k == 7)),
                                 reads=[bW, bhc], writes=[C.bps[p]])
                    outs = []
                    for half in range(2):
                        ta, bta = tm[(oi * 2) % 4], btm[(oi * 2) % 4]
                        tb_, btb = tm[(oi * 2 + 1) % 4], btm[(oi * 2 + 1) % 4]
                        o, bo = ob[oi % 4], bob[oi % 4]
                        oi += 1
                        S.op("dve", lambda e: e.tensor_tensor(out=ta[:], in0=C.ps[pa][:], in1=csc[:, half, :], op=ALU.mult), reads=[C.bps[pa], bcsc], writes=[bta])
                        S.op("dve", lambda e: e.tensor_tensor(out=tb_[:], in0=C.ps[pb][:], in1=csc[:, 1 - half, :], op=ALU.mult), reads=[C.bps[pb], bcsc], writes=[btb])
                        if qk == 0:
                            S.op("pool", lambda e: e.tensor_tensor(out=o[:], in0=ta[:], in1=tb_[:], op=(ALU.subtract if half == 0 else ALU.add)), reads=[bta, btb], writes=[bo])
                            S.dma("sp", C.QTR[s, hd * 2 + half, :, tb * 512:(tb + 1) * 512], o[:], reads=[bo])
                        else:
                            kk, bkk = kr[ki % 2], bkr[ki % 2]
                            S.op("pool", lambda e: e.tensor_tensor(out=kk[:, half, :], in0=ta[:], in1=tb_[:], op=(ALU.subtract if half == 0 else ALU.add)), reads=[bta, btb], writes=[bkk])
                            S.dma("sp", C.KTR[s, hd * 2 + half, :, tb * 512:(tb + 1) * 512], kk[:, half, :], reads=[bkk])
                    if late5:
                        late5.pop(0)()
                    if qk == 1:
                        kk, bkk = kr[ki % 2], bkr[ki % 2]
                        ki += 1
                        late5.append(lambda kk=kk, bkk=bkk, hd=hd, s=s, tb=tb: k_tm(kk, bkk, hd, s, tb))
                    if False:
                        for j in range(4):
                            r0 = s * T + (tb * 4 + j) * 128
                            kt_, bkt_ = ktm[j % 2], bktm[j % 2]
                            hh = C.pti % C.npt
                            C.pti += 1
                            for half in range(2):
                                S.op("pe", lambda e: e.transpose(out=C.pt[hh][:, half, :], in_=kk[:, half, j * 128:(j + 1) * 128], identity=C.idt[:]),
                                     reads=[bkk, C.bidt], writes=[C.bpt[hh]])
                            S.op("act", lambda e: e.copy(out=kt_[:].rearrange("p (a b) -> p a b", a=2), in_=C.pt[hh][:, 0:2, :]), reads=[C.bpt[hh]], writes=[bkt_])
                            S.dma("sp", C.KTM[r0:r0 + 128, hd * 256:(hd + 1) * 256], kt_[:], reads=[bkt_])
            while late5:
                late5.pop(0)()
            for j in range(4):
                r0 = s * T + (tb * 4 + j) * 128
                for c in range(8):
                    p = pi % 4
                    pi += 1
                    c0 = 2048 + c * 512
                    for k in range(8):
                        S.op("pe", lambda e: e.matmul(C.ps[p][:], lhsT=h[:, k, j * 128:(j + 1) * 128], rhs=W[:, k, c0:c0 + 512], start=(k == 0), stop=(k == 7)),
                             reads=[bW, bhc], writes=[C.bps[p]])
                    o, bo = ob[oi % 4], bob[oi % 4]
                    oi += 1
                    if c < 4:
                        S.op("dve", lambda e: e.tensor_copy(out=o[:], in_=C.ps[p][:]), reads=[C.bps[p]], writes=[bo])
                        S.dma("sp", C.VR[r0:r0 + 128, c * 512:(c + 1) * 512], o[:], reads=[bo])
                    else:
                        S.op("act", lambda e: e.activation(out=o[:], in_=C.ps[p][:], func=AF.Silu), reads=[C.bps[p]], writes=[bo])
                        S.dma("sp", C.SG[r0:r0 + 128, (c - 4) * 512:(c - 3) * 512], o[:], reads=[bo])
    S.barrier()
    sc.close()


def p6_retention(C):
    nc, S = C.nc, C.S
    sc = Scope(nc)
    alloc_psum(C, sc, 0, 1)
    pin = sc.psum("pin", [128, 512], F32)
    bpin = Buf()
    pout = [sc.psum("pout", [128, 512], F32) for _ in range(2)]
    bpout = bufs(2)
    pst = [sc.psum("pst", [128, 2, 512], F32) for _ in range(2)]
    bpst = bufs(2)
    Wo, bWo = load_w(C, sc, "wo1", C.w_out1, 16, D)
    grow, bg = load_grow(C, sc, 3)
    nb = NormBufs(sc, 1)
    lg = sc.sb("lg", [128, 8], F32)
    DT = sc.sb("DT", [128, 8, 128], F32)
    KD = sc.sb("KD", [128, 8], F32)
    SD = sc.sb("SD", [128, 8], F32)
    QDT = sc.sb("QDT", [128, 2, 8, 128], F32)
    KDT = sc.sb("KDT", [128, 2, 1024], F32)
    sc2 = Scope(nc)
    rc = sc2.sb("retc", [128, 7, 128], F32)
    QD = sc2.sb("QD", [128, 8, 128], F32)
    onesf = sc2.sb("onesf", [128, 256], F32)
    brc = Buf()
    S.dma("sp", rc[:], C.retc[0:7].rearrange("c p q -> p c q"), writes=[brc])
    blg = Buf()
    S.dma("sp", lg[:], C.decay.partition_broadcast(128), writes=[blg])
    S.op("act", lambda e: e.activation(out=lg[:], in_=lg[:], func=AF.Exp), reads=[blg], writes=[blg])
    S.op("act", lambda e: e.activation(out=lg[:], in_=lg[:], func=AF.Ln, bias=1.0), reads=[blg], writes=[blg])
    S.op("dve", lambda e: e.tensor_scalar(out=lg[:], in0=lg[:], scalar1=-1.0, scalar2=None, op0=ALU.mult), reads=[blg], writes=[blg])
    bDT, bQD, bKD, bSD = Buf(), Buf(), Buf(), Buf()
    for d in range(2):
        for hd in range(4):
            i = d * 4 + hd
            S.op("act", lambda e: e.activation(out=DT[:, i, :], in_=rc[:, 2 * d, :], func=AF.Exp, scale=lg[:, i:i + 1]), reads=[brc, blg], writes=[bDT])
            S.op("dve", lambda e: e.tensor_tensor(out=DT[:, i, :], in0=DT[:, i, :], in1=rc[:, 2 * d + 1, :], op=ALU.mult), reads=[brc, bDT], writes=[bDT])
            S.op("act", lambda e: e.activation(out=QD[:, i, :], in_=rc[:, 4 + d, :], func=AF.Exp, scale=lg[:, i:i + 1]), reads=[brc, blg], writes=[bQD])
            S.op("dve", lambda e: e.tensor_scalar(out=QD[:, i, :], in0=QD[:, i, :], scalar1=1.0 / 16, scalar2=None, op0=ALU.mult), reads=[bQD], writes=[bQD])
            S.op("act", lambda e: e.activation(out=KD[:, i:i + 1], in_=rc[:, 6, d:d + 1], func=AF.Exp, scale=lg[:, i:i + 1]), reads=[brc, blg], writes=[bKD])
            S.op("act", lambda e: e.activation(out=SD[:, i:i + 1], in_=rc[:, 6, 2:3], func=AF.Exp, scale=lg[:, i:i + 1]), reads=[brc, blg], writes=[bSD])

    bQDT, bKDT, bof = Buf(), Buf(), Buf()
    S.op("pool", lambda e: e.memset(onesf[:], 1.0), writes=[bof])
    for d in range(2):
        for hd in range(4):
            i = d * 4 + hd
            for t in range(2):
                S.op("pool", lambda e: e.tensor_copy(out=QDT[:, d, hd * 2 + t, :], in_=QD[:, i, :]), reads=[bQD], writes=[bQDT])
            S.op("dve", lambda e: e.tensor_scalar(out=KDT[:, d, hd * 256:(hd + 1) * 256], in0=onesf[:], scalar1=KD[:, i:i + 1], scalar2=None, op0=ALU.mult),
                 reads=[bof, bKD], writes=[bKDT])

    S.barrier()
    sc2.close()

    S32 = sc.sb("S32", [128, 4, 2, 512], F32)
    Sbf = sc.sb("Sbf", [128, 4, 2, 512], BF16)
    bS32, bSbf = bufs(4), bufs(4)
    qT = [sc.sb("qT", [128, 8, 128], BF16) for _ in range(4)]
    kT = [sc.sb("kT", [128, 8, 128], BF16) for _ in range(4)]
    kM = [sc.sb("kM", [128, D], BF16) for _ in range(4)]
    vM = [sc.sb("vM", [128, 2048], BF16) for _ in range(4)]
    bq, bk, bkm, bv = bufs(4), bufs(4), bufs(4), bufs(4)
    iT = [sc.sb("iT", [128, 4, 128], BF16) for _ in range(4)]
    biT = bufs(4)
    qd = [sc.sb("qd", [128, 8, 128], BF16) for _ in range(4)]
    bqd = bufs(4)
    kd = [sc.sb("kd", [128, 1024], BF16) for _ in range(4)]
    bkd = bufs(4)
    rf = [sc.sb("rf", [128, 512], F32) for _ in range(4)]
    brf = bufs(4)
    rfl = [sc.sb("rfl", [128, 2048], F32) for _ in range(2)]
    brfl = bufs(2)
    sgl = [sc.sb("sgl", [128, 2048], BF16) for _ in range(2)]
    bsgl = bufs(2)
    rr = [sc.sb("rr", [128, 512], F32) for _ in range(2)]
    brr = bufs(2)
    st = [sc.sb("st", [128, 8], F32) for _ in range(2)]
    bst = bufs(2)
    rg = sc.sb("rg", [128, 2048], BF16)
    brg = bufs(4)
    rgT = sc.sb("rgT", [128, 16, 128], BF16)
    brgT = Buf()
    xt = [sc.sb("xt", [128, D], F32) for _ in range(2)]
    bxt = bufs(2)
    x3 = sc.sb("x3", [128, D], F32)
    bx3 = Buf()
    hb = sc.sb("hb", [128, D], BF16)
    bhb = Buf()
    h4T = sc.sb("h4T", [128, 8, 128], BF16)
    bh4 = Buf()
    cnt = dict(r=0, o=0)
    late = []

    def mk_chunk(s, d, n, c, ci):
        r0 = s * T + c * 128
        j = ci % 4
        q_, k_, km_, v_ = qT[j], kT[j], kM[j], vM[j]

        def a():
            S.dma("sp", q_[:], C.QTR[s, :, :, c * 128:(c + 1) * 128].rearrange("k p t -> p k t"), writes=[bq[j]])
            S.dma("sp", k_[:], C.KTR[s, :, :, c * 128:(c + 1) * 128].rearrange("k p t -> p k t"), writes=[bk[j]])
            S.dma("sp", km_[:], C.KTM[r0:r0 + 128, :], writes=[bkm[j]])
            S.dma("sp", v_[:], C.VR[r0:r0 + 128, :], writes=[bv[j]])

        def ac():
            for hd in range(4):
                for i in range(2):
                    S.op("pe", lambda e: e.matmul(pin[:, hd * 128:(hd + 1) * 128], lhsT=k_[:, hd * 2 + i, :], rhs=q_[:, hd * 2 + i, :], start=(i == 0), stop=(i == 1)),
                         reads=[bk[j], bq[j]], writes=[bpin])
            S.op("dve", lambda e: e.tensor_tensor(out=iT[j][:], in0=pin[:].rearrange("p (h q) -> p h q", h=4), in1=DT[:, d * 4:(d + 1) * 4, :], op=ALU.mult),
                 reads=[bpin, bDT], writes=[biT[j]])
            if n > 0:
                S.op("pool", lambda e: e.tensor_tensor(out=qd[j][:], in0=q_[:], in1=QDT[:, d, :, :], op=ALU.mult), reads=[bq[j], bQDT], writes=[bqd[j]])
            S.op("pool", lambda e: e.tensor_tensor(out=kd[j][:], in0=km_[:], in1=KDT[:, d, :], op=ALU.mult), reads=[bkm[j], bKDT], writes=[bkd[j]])

        j2 = ci % 2

        def a2():
            if d == 1:
                S.dma("sp", rfl[j2][:], C.RF[r0:r0 + 128, :], writes=[brfl[j2]])
                S.dma("sp", sgl[j2][:], C.SG[r0:r0 + 128, :], writes=[bsgl[j2]])
                S.dma("sp", xt[j2][:], C.X2[r0:r0 + 128, :], writes=[bxt[j2]])

        def b():
            for hd in range(4):
                if hd == 2 and late:
                    late.pop(0)()
                di = d * 4 + hd
                o_ = cnt["o"] % 2
                cnt["o"] += 1
                po, bpo = pout[o_], bpout[o_]
                ps_, bps_ = pst[o_], bpst[o_]
                vh = v_[:, hd * 512:(hd + 1) * 512]
                S.op("pe", lambda e: e.matmul(po[:], lhsT=iT[j][:, hd, :], rhs=vh, start=True, stop=(n == 0)), reads=[biT[j], bv[j]], writes=[bpo])
                if n > 0:
                    for i in range(2):
                        S.op("pe", lambda e: e.matmul(po[:], lhsT=qd[j][:, hd * 2 + i, :], rhs=Sbf[:, hd, i, :], start=False, stop=(i == 1)),
                             reads=[bqd[j], bSbf[hd]], writes=[bpo])
                rq = cnt["r"] % 4
                ri = cnt["r"] % 2
                cnt["r"] += 1
                if d == 0:
                    S.op("act", lambda e: e.copy(out=rf[rq][:], in_=po[:]), reads=[bpo], writes=[brf[rq]])
                    S.dma("sp", C.RF[r0:r0 + 128, hd * 512:(hd + 1) * 512], rf[rq][:], reads=[brf[rq]])
                else:
                    r_, br_ = rr[ri], brr[ri]
                    s_, bs_ = st[ri], bst[ri]
                    S.op("dve", lambda e: e.tensor_tensor(out=r_[:], in0=po[:], in1=rfl[j2][:, hd * 512:(hd + 1) * 512], op=ALU.add), reads=[bpo, brfl[j2]], writes=[br_])
                for i in range(2):
                    S.op("pe", lambda e: e.matmul(ps_[:, i, :], lhsT=kd[j][:, hd * 256 + i * 128:hd * 256 + (i + 1) * 128], rhs=vh, start=True, stop=True),
                         reads=[bkd[j], bv[j]], writes=[bps_])
                if n == 0:
                    S.op("dve", lambda e: e.tensor_copy(out=S32[:, hd, :, :], in_=ps_[:]), reads=[bps_], writes=[bS32[hd]])
                else:
                    S.op("dve", lambda e: e.scalar_tensor_tensor(out=S32[:, hd, :, :], in0=S32[:, hd, :, :], scalar=SD[:, di:di + 1], in1=ps_[:], op0=ALU.mult, op1=ALU.add),
                         reads=[bps_, bS32[hd], bSD], writes=[bS32[hd]])
                S.op("act", lambda e: e.copy(out=Sbf[:, hd, :, :], in_=S32[:, hd, :, :]), reads=[bS32[hd]], writes=[bSbf[hd]])
                if d == 1:
                    S.op("dve", lambda e: e.bn_stats(out=s_[:, 0:6], in_=r_[:]), reads=[br_], writes=[bs_])
                    S.op("dve", lambda e: e.bn_aggr(out=s_[:, 6:8], in_=s_[:, 0:6]), reads=[bs_], writes=[bs_])
                    S.op("dve", lambda e: e.tensor_scalar(out=s_[:, 7:8], in0=s_[:, 7:8], scalar1=EPS, scalar2=None, op0=ALU.add), reads=[bs_], writes=[bs_])
                    S.op("act", lambda e: e.activation(out=s_[:, 7:8], in_=s_[:, 7:8], func=AF.Sqrt), reads=[bs_], writes=[bs_])
                    S.op("dve", lambda e: e.reciprocal(out=s_[:, 7:8], in_=s_[:, 7:8]), reads=[bs_], writes=[bs_])
                    S.op("dve", lambda e: e.tensor_scalar(out=r_[:], in0=r_[:], scalar1=s_[:, 6:7], scalar2=s_[:, 7:8], op0=ALU.subtract, op1=ALU.mult), reads=[br_, bs_], writes=[br_])
                    S.op("pool", lambda e: e.tensor_tensor(out=rg[:, hd * 512:(hd + 1) * 512], in0=r_[:], in1=sgl[j2][:, hd * 512:(hd + 1) * 512], op=ALU.mult),
                         reads=[br_, bsgl[j2]], writes=[brg[hd]])
            if d == 1:
                for hd in range(4):
                    transpose_to(C, rg[:, hd * 512:(hd + 1) * 512], brg[hd], 4, lambda a_, b_: rgT[:, hd * 4 + a_:hd * 4 + b_, :], brgT, "act")
                for c2 in range(2):
                    o_ = cnt["o"] % 2
                    cnt["o"] += 1
                    po, bpo = pout[o_], bpout[o_]
                    for k in range(16):
                        S.op("pe", lambda e: e.matmul(po[:], lhsT=rgT[:, k, :], rhs=Wo[:, k, c2 * 512:(c2 + 1) * 512], start=(k == 0), stop=(k == 15)),
                             reads=[bWo, brgT], writes=[bpo])
                    S.op("dve", lambda e: e.tensor_tensor(out=x3[:, c2 * 512:(c2 + 1) * 512], in0=po[:], in1=xt[j2][:, c2 * 512:(c2 + 1) * 512], op=ALU.add),
                         reads=[bpo, bxt[j2]], writes=[bx3])
                S.dma("sp", C.X3[r0:r0 + 128, :], x3[:], reads=[bx3])
                rmsnorm(C, nb, x3[:], bx3, grow, bg, hb[:], bhb)

                def e2():
                    transpose_to(C, hb, bhb, 8, lambda a_, b_: h4T[:, a_:b_, :], bh4, "act")
                    S.dma("sp", C.HTb[s, :, :, c * 128:(c + 1) * 128].rearrange("k p t -> p k t"), h4T[:], reads=[bh4])
                late.append(e2)
        return a, ac, a2, b

    ci = 0
    for s in range(C.nseq):
        for d in range(2):
            chunks = list(range(NT)) if d == 0 else list(range(NT - 1, -1, -1))
            st_ = []
            for n, c in enumerate(chunks):
                st_.append(mk_chunk(s, d, n, c, ci))
                ci += 1
            for k in range(NT + 4):
                if k >= 4:
                    st_[k - 4][3]()
                if k < NT:
                    st_[k][0]()
                if 2 <= k < NT + 2:
                    st_[k - 2][1]()
                if 3 <= k < NT + 3:
                    st_[k - 3][2]()
            while late:
                late.pop(0)()
            S.barrier()
    S.barrier()
    sc.close()


def _prep_shared(inp):
    f = np.float32
    w_in = np.asarray(inp["even_w_in"], f)[0]
    def swap(cols):
        c = cols.reshape(D, 8, 2, 32)
        return c[:, :, ::-1, :].reshape(D, 512)
    dq = w_in[:, 1536:2048]
    dk = w_in[:, 2048:2560]
    w_in0 = np.ascontiguousarray(np.concatenate([w_in, swap(dq), swap(dk)], axis=1))
    cw = np.asarray(inp["ffn_conv_w"], f)
    cb = np.asarray(inp["ffn_conv_b"], f)
    cwT = []
    for l in range(2):
        a = np.concatenate([cw[l], cb[l][None]], 0)
        a = a.reshape(4, 44, 128).transpose(2, 1, 0)
        cwT.append(np.ascontiguousarray(a.reshape(128, 44 * 4)))
    norms = np.ascontiguousarray(np.concatenate([
        np.asarray(inp["attn_norm"], f)[0:1], np.asarray(inp["ffn_norm"], f)[0:1],
        np.asarray(inp["attn_norm"], f)[1:2], np.asarray(inp["ffn_norm"], f)[1:2],
        np.asarray(inp["final_norm"], f)[None]], 0))
    rpb = np.asarray(inp["na_rpb"], f)[0]
    rpbT = np.ascontiguousarray(rpb[:, ::-1, :].transpose(2, 0, 1).reshape(31, 8 * 15))
    decay = np.ascontiguousarray(np.concatenate([np.asarray(inp["ret_decay_fwd"], f)[0], np.asarray(inp["ret_decay_bwd"], f)[0]]))
    sh = dict(
        w_in0=w_in0, w_out0=np.ascontiguousarray(np.asarray(inp["even_w_out"], f)[0]),
        w_up0=np.ascontiguousarray(np.asarray(inp["ffn_w_up"], f)[0]), w_up1=np.ascontiguousarray(np.asarray(inp["ffn_w_up"], f)[1]),
        w_dn0=np.ascontiguousarray(np.asarray(inp["ffn_w_down"], f)[0]), w_dn1=np.ascontiguousarray(np.asarray(inp["ffn_w_down"], f)[1]),
        cwT0=cwT[0], cwT1=cwT[1],
        w_in1=np.ascontiguousarray(np.asarray(inp["ret_w_in"], f)[0]), w_out1=np.ascontiguousarray(np.asarray(inp["ret_w_out"], f)[0]),
        norms=norms, rpbT=rpbT, decay=decay,
    )
    sh.update(_consts())
    return sh


def _assign():
    seqs = [("p", i) for i in range(BATCH)] + [("s", i) for i in range(DEC_BATCH)]
    slots = [[] for _ in range(NCORES)]
    for i, sq in enumerate(seqs):
        slots[i % NCORES].append(sq)
    return slots


def kernel(**inputs):
    nseq = 3
    sh = _prep_shared(inputs)
    xp = np.asarray(inputs["x_prompt"], np.float32)
    xs = np.asarray(inputs["x_sample"], np.float32)
    slots = _assign()
    in_maps = []
    for c in range(NCORES):
        xc = np.zeros((nseq * T, D), np.float32)
        for j, (kind, i) in enumerate(slots[c]):
            xc[j * T:(j + 1) * T] = xp[i] if kind == "p" else xs[i]
        for j in range(len(slots[c]), nseq):
            xc[j * T:(j + 1) * T] = xc[0:T]
        m = dict(sh)
        m["x"] = xc
        in_maps.append(m)
    nc, _ = build(nseq)
    res = run_bass_kernel_spmd(nc, in_maps, core_ids=list(range(NCORES)))
    yp = np.zeros((BATCH, T, D), np.float32)
    ys = np.zeros((DEC_BATCH, T, D), np.float32)
    for c in range(NCORES):
        yc = np.asarray(res.results[c]["y"]).reshape(nseq, T, D)
        for j, (kind, i) in enumerate(slots[c]):
            if kind == "p":
                yp[i] = yc[j]
            else:
                ys[i] = yc[j]
    return (yp, ys)
```
